# Optimizing a Trainium2 kernel written in Bass

```python
import math
import jax
import jax.numpy as jnp
from jax import lax
import numpy as np

D_MODEL = 2048
BATCH = 2
SEQ = 4096
DEPTH = 2
DEC_BATCH = 8
DEC_SEQ = 4
PAST_LEN = 16384
PAGE_SIZE = 128

MOBA_HEADS = 8
HEAD_DIM = 128
MOBA_BLOCK = 256
MOBA_TOPK = 3
MOBA_Q_BLOCK = 32
GLA_HEADS = 4
GLA_DK = 128
GLA_DV = 256
GLA_GATE_RANK = 16
GLA_TAU = 16.0
GLA_CHUNK = 32

RMS_EPS = 1e-6
NEG_INF = -1e30
MOBA_WIDTH = MOBA_HEADS * HEAD_DIM
GLA_K_WIDTH = GLA_HEADS * GLA_DK
GLA_V_WIDTH = GLA_HEADS * GLA_DV
IN_WIDTHS = (MOBA_WIDTH, MOBA_WIDTH, MOBA_WIDTH, MOBA_WIDTH,
             GLA_K_WIDTH, GLA_K_WIDTH, GLA_V_WIDTH, GLA_V_WIDTH, GLA_GATE_RANK,
             2 * D_MODEL)
N_IN = sum(IN_WIDTHS)

kernel_name = 'hybrid_moba_gla_gated_merge_step'


def rmsnorm(x, g):
    xf = x.astype(jnp.float32)
    r = lax.rsqrt(jnp.mean(xf * xf, axis=-1, keepdims=True) + RMS_EPS)
    return (xf * r * g.astype(jnp.float32)).astype(x.dtype)


def moba_blocks(k, v):
    B, H, L, Dh = k.shape
    nb = L // MOBA_BLOCK
    kb = k[:, :, : nb * MOBA_BLOCK].reshape(B, H, nb, MOBA_BLOCK, Dh)
    vb = v[:, :, : nb * MOBA_BLOCK].reshape(B, H, nb, MOBA_BLOCK, Dh)
    if nb < MOBA_TOPK:
        pad = ((0, 0), (0, 0), (0, MOBA_TOPK - nb), (0, 0), (0, 0))
        kb = jnp.pad(kb, pad)
        vb = jnp.pad(vb, pad)
    means = jnp.mean(kb.astype(jnp.float32), axis=3)
    return kb, vb, means


def moba_query_block(q, q_pos, k, v, kb, vb, means):
    B, H, Q, Dh = q.shape
    L = k.shape[2]
    nb = means.shape[2]
    n_past = q_pos // MOBA_BLOCK
    s = jnp.einsum('bhqd,bhnd->bhqn', q.astype(jnp.float32), means)
    blk_ok = jnp.arange(nb)[None, :] < n_past[:, None]
    s = jnp.where(blk_ok, s, -jnp.inf)
    _, sel = lax.top_k(s, MOBA_TOPK)
    sel_ok = sel < n_past[:, None]
    bi = jnp.arange(B)[:, None, None, None]
    hi = jnp.arange(H)[None, :, None, None]
    ks = kb[bi, hi, sel]
    vs = vb[bi, hi, sel]
    own_idx = (q_pos // MOBA_BLOCK * MOBA_BLOCK)[:, None] + jnp.arange(MOBA_BLOCK)[None, :]
    own_ok = own_idx <= q_pos[:, None]
    own_idx = jnp.minimum(own_idx, L - 1)
    ko = k[:, :, own_idx]
    vo = v[:, :, own_idx]
    scale = HEAD_DIM ** -0.5
    l_sel = jnp.einsum('bhqd,bhqnkd->bhqnk', q, ks).astype(jnp.float32) * scale
    l_sel = jnp.where(sel_ok[..., None], l_sel, NEG_INF).reshape(B, H, Q, MOBA_TOPK * MOBA_BLOCK)
    l_own = jnp.einsum('bhqd,bhqkd->bhqk', q, ko).astype(jnp.float32) * scale
    l_own = jnp.where(own_ok, l_own, NEG_INF)
    p = jax.nn.softmax(jnp.concatenate([l_sel, l_own], axis=-1), axis=-1).astype(v.dtype)
    p_sel = p[..., : MOBA_TOPK * MOBA_BLOCK].reshape(B, H, Q, MOBA_TOPK, MOBA_BLOCK)
    p_own = p[..., MOBA_TOPK * MOBA_BLOCK:]
    return (jnp.einsum('bhqnk,bhqnkd->bhqd', p_sel, vs)
            + jnp.einsum('bhqk,bhqkd->bhqd', p_own, vo))


def moba(q, k, v, q_pos):
    B, H, T, Dh = q.shape
    kb, vb, means = moba_blocks(k, v)

    def attend(qp):
        return moba_query_block(qp[0], qp[1], k, v, kb, vb, means)

    if T > MOBA_Q_BLOCK and T % MOBA_Q_BLOCK == 0:
        nq = T // MOBA_Q_BLOCK
        qs = q.reshape(B, H, nq, MOBA_Q_BLOCK, Dh).transpose(2, 0, 1, 3, 4)
        ps = q_pos.reshape(nq, MOBA_Q_BLOCK)
        out = lax.map(attend, (qs, ps))
        return out.transpose(1, 2, 0, 3, 4).reshape(B, H, T, Dh)
    return attend((q, q_pos))


def gla(q, k, v, log_a, s0):
    B, H, T, dk = q.shape
    dv = v.shape[-1]
    c = math.gcd(T, GLA_CHUNK)
    nc = T // c

    def chunks(a):
        return a.astype(jnp.float32).reshape(B, H, nc, c, a.shape[-1]).transpose(2, 0, 1, 3, 4)

    causal = jnp.tril(jnp.ones((c, c), dtype=bool))

    def step(s, inp):
        qc, kc, vc, gc = inp
        b = jnp.cumsum(gc, axis=2)
        b_last = b[:, :, -1:, :]
        qi = qc * jnp.exp(b)
        ki = kc * jnp.exp(-b)
        a = jnp.where(causal, jnp.einsum('bhcd,bhsd->bhcs', qi, ki), 0.0)
        o = jnp.einsum('bhcs,bhsv->bhcv', a, vc) + jnp.einsum('bhcd,bhdv->bhcv', qi, s)
        kd = kc * jnp.exp(b_last - b)
        s = jnp.exp(b_last[:, :, 0, :])[..., None] * s + jnp.einsum('bhsd,bhsv->bhdv', kd, vc)
        return s, o

    s_fin, o = lax.scan(step, s0.astype(jnp.float32),
                        (chunks(q), chunks(k), chunks(v), chunks(log_a)))
    return o.transpose(1, 2, 0, 3, 4).reshape(B, H, T, dv), s_fin


def decoder_layer(x, past_k, past_v, s0, pos0, norm_g, w_in, w_gate2, b_gate, gla_norm_g,
                  w_branch_a, w_branch_b, b_merge, w_out):
    B, T, _ = x.shape
    xn = rmsnorm(x, norm_g)
    proj = xn @ w_in
    splits = [int(s) for s in np.cumsum(IN_WIDTHS)[:-1]]
    qa, ka, va, za, qb, kb, vb, zb, a_lr, g_merge = jnp.split(proj, splits, axis=-1)

    def heads(a, h):
        return a.reshape(B, T, h, -1).transpose(0, 2, 1, 3)

    k_rows = ka.reshape(B, T, MOBA_HEADS, HEAD_DIM)
    v_rows = va.reshape(B, T, MOBA_HEADS, HEAD_DIM)
    k_full = k_rows.transpose(0, 2, 1, 3)
    v_full = v_rows.transpose(0, 2, 1, 3)
    if past_k is not None:
        k_full = jnp.concatenate([past_k, k_full], axis=2)
        v_full = jnp.concatenate([past_v, v_full], axis=2)
    q_pos = pos0 + jnp.arange(T, dtype=jnp.int32)
    o_a = moba(heads(qa, MOBA_HEADS), k_full, v_full, q_pos)
    o_a = o_a.transpose(0, 2, 1, 3).reshape(B, T, MOBA_WIDTH) * jax.nn.silu(za)
    y_a = o_a @ w_branch_a

    log_a = jax.nn.log_sigmoid((a_lr @ w_gate2 + b_gate).astype(jnp.float32)) / GLA_TAU
    o_b, s_new = gla(heads(qb, GLA_HEADS) * GLA_DK ** -0.5, heads(kb, GLA_HEADS),
                     heads(vb, GLA_HEADS), heads(log_a, GLA_HEADS), s0)
    o_b = rmsnorm(o_b, gla_norm_g[None, :, None, :]).astype(x.dtype)
    o_b = o_b.transpose(0, 2, 1, 3).reshape(B, T, GLA_V_WIDTH) * jax.nn.silu(zb)
    y_b = o_b @ w_branch_b

    g_a, g_b = jnp.split(jax.nn.sigmoid(g_merge + b_merge), 2, axis=-1)
    x = x + (g_a * y_a + g_b * y_b) @ w_out
    return x, k_rows, v_rows, s_new.astype(s0.dtype)


def setup_inputs(seed: int = 0) -> dict:
    key = jax.random.key(seed)
    ks = jax.random.split(key, 16)
    n_pages = PAST_LEN // PAGE_SIZE
    n_used = DEC_BATCH * n_pages
    n_pool = (5 * n_used + 3) // 4

    def nrm(k, shape, scale):
        return jax.random.normal(k, shape, jnp.float32) * scale

    x_prompt = nrm(ks[0], (BATCH, SEQ, D_MODEL), 1.0)
    x_sample = nrm(ks[1], (DEC_BATCH, DEC_SEQ, D_MODEL), 1.0)
    cache_k = nrm(ks[2], (DEPTH, n_pool, PAGE_SIZE, MOBA_HEADS, HEAD_DIM), 1.0)
    cache_v = nrm(ks[3], (DEPTH, n_pool, PAGE_SIZE, MOBA_HEADS, HEAD_DIM), 1.0)
    state_gla = nrm(ks[4], (DEPTH, DEC_BATCH, GLA_HEADS, GLA_DK, GLA_DV), 0.5)
    page_table = jax.random.permutation(ks[5], n_pool)[:n_used].reshape(DEC_BATCH, n_pages).astype(jnp.int32)
    norm_g = 1.0 + nrm(ks[6], (DEPTH, D_MODEL), 0.01)
    w_in = nrm(ks[7], (DEPTH, D_MODEL, N_IN), D_MODEL ** -0.5)
    w_gate2 = nrm(ks[8], (DEPTH, GLA_GATE_RANK, GLA_K_WIDTH), GLA_GATE_RANK ** -0.5)
    b_gate = nrm(ks[9], (DEPTH, GLA_K_WIDTH), 0.01)
    gla_norm_g = 1.0 + nrm(ks[10], (DEPTH, GLA_HEADS, GLA_DV), 0.01)
    w_branch_a = nrm(ks[11], (DEPTH, MOBA_WIDTH, D_MODEL), MOBA_WIDTH ** -0.5)
    w_branch_b = nrm(ks[12], (DEPTH, GLA_V_WIDTH, D_MODEL), GLA_V_WIDTH ** -0.5)
    b_merge = nrm(ks[13], (DEPTH, 2 * D_MODEL), 0.01)
    w_out = nrm(ks[14], (DEPTH, D_MODEL, D_MODEL), D_MODEL ** -0.5)
    final_norm_g = 1.0 + nrm(ks[15], (D_MODEL,), 0.01)
    return {'x_prompt': x_prompt, 'x_sample': x_sample, 'cache_k': cache_k, 'cache_v': cache_v,
            'state_gla': state_gla, 'page_table': page_table, 'norm_g': norm_g, 'w_in': w_in,
            'w_gate2': w_gate2, 'b_gate': b_gate, 'gla_norm_g': gla_norm_g,
            'w_branch_a': w_branch_a, 'w_branch_b': w_branch_b, 'b_merge': b_merge,
            'w_out': w_out, 'final_norm_g': final_norm_g}


def reference(x_prompt, x_sample, cache_k, cache_v, state_gla, page_table, norm_g, w_in,
              w_gate2, b_gate, gla_norm_g, w_branch_a, w_branch_b, b_merge, w_out, final_norm_g):
    dec_b, n_pages = page_table.shape
    past_len = n_pages * cache_k.shape[2]
    bp = x_prompt.shape[0]
    xp, xs = x_prompt, x_sample
    kp_l, vp_l, sp_l, ks_l, vs_l, ss_l = [], [], [], [], [], []
    for l in range(DEPTH):
        w = (norm_g[l], w_in[l], w_gate2[l], b_gate[l], gla_norm_g[l],
             w_branch_a[l], w_branch_b[l], b_merge[l], w_out[l])
        s0p = jnp.zeros((bp, GLA_HEADS, GLA_DK, GLA_DV), state_gla.dtype)
        xp, kp, vp, sp = decoder_layer(xp, None, None, s0p, 0, *w)
        past_k = cache_k[l][page_table].reshape(dec_b, past_len, MOBA_HEADS, HEAD_DIM).transpose(0, 2, 1, 3)
        past_v = cache_v[l][page_table].reshape(dec_b, past_len, MOBA_HEADS, HEAD_DIM).transpose(0, 2, 1, 3)
        xs, ksm, vsm, ssm = decoder_layer(xs, past_k, past_v, state_gla[l], past_len, *w)
        kp_l.append(kp)
        vp_l.append(vp)
        sp_l.append(sp)
        ks_l.append(ksm)
        vs_l.append(vsm)
        ss_l.append(ssm)
    y_prompt = rmsnorm(xp, final_norm_g)
    y_sample = rmsnorm(xs, final_norm_g)
    return (y_prompt, y_sample, jnp.stack(kp_l), jnp.stack(vp_l), jnp.stack(sp_l),
            jnp.stack(ks_l), jnp.stack(vs_l), jnp.stack(ss_l))
```

```python
from contextlib import ExitStack
import numpy as np
import ml_dtypes
import concourse.bass as bass
import concourse.mybir as mybir
from concourse.bass_utils import run_bass_kernel_spmd

F32 = mybir.dt.float32
BF16 = mybir.dt.bfloat16
I32 = mybir.dt.int32
U32 = mybir.dt.uint32
AF = mybir.ActivationFunctionType
ALU = mybir.AluOpType
AX = mybir.AxisListType

SAME_ENGINE_SYNC = True

D = 2048
KC = 16
NIN = 11280
NPOOL = 1280
NEG = -1.0e30
EPS = 1e-6
SCALE = 128 ** -0.5


class Buf:
    __slots__ = ("name", "last_w", "readers", "lt", "st")

    def __init__(self, name):
        self.name = name
        self.last_w = None
        self.readers = []
        self.lt = None
        self.st = None


class Track:
    __slots__ = ("sem", "count", "inc", "last_op")

    def __init__(self, sem, inc=16):
        self.sem = sem
        self.count = 0
        self.inc = inc
        self.last_op = None


class Op:
    __slots__ = ("eng", "fn", "deps", "track", "ordinal", "tick", "signal", "idx", "waits")


class Graph:
    ENGS = ("pe", "act", "dve", "pool", "sp")

    def __init__(self, nc, stack):
        self.nc = nc
        self.stack = stack
        self.ops = []
        self.esem = {e: stack.enter_context(nc.semaphore("es_" + e)) for e in self.ENGS}
        self.nsem = 5
        self.last = {e: None for e in self.ENGS}
        self.tracks = []
        self.rkeys = []

    def track(self, inc=16):
        self.nsem += 1
        t = Track(self.stack.enter_context(self.nc.semaphore("trk%d" % self.nsem)), inc)
        self.tracks.append(t)
        return t

    def add(self, eng, fn, reads=(), writes=(), dma=None, extra=()):
        op = Op()
        op.eng = eng
        op.fn = fn
        op.track = dma
        op.idx = len(self.ops)
        op.signal = False
        op.tick = None
        op.ordinal = None
        deps = set(extra)
        for b in reads:
            if b.last_w is not None:
                deps.add(b.last_w)
        for b in writes:
            if b.last_w is not None:
                deps.add(b.last_w)
            for r in b.readers:
                deps.add(r)
        op.deps = deps
        rkey = eng if dma is None else ("t", id(dma))
        for b in reads:
            b.readers = [r for r in b.readers if self.rkeys[r] != rkey]
            b.readers.append(op.idx)
        self.rkeys.append(rkey)
        for b in writes:
            b.last_w = op.idx
            b.readers = []
        if dma is not None:
            dma.count += 1
            op.ordinal = dma.count
            dma.last_op = op.idx
        self.ops.append(op)
        if fn is not None:
            self.last[eng] = op.idx
        return op

    def emit(self):
        ops = self.ops

        def skip(d, op):
            return d.eng == op.eng and (d.eng == "pe" or not SAME_ENGINE_SYNC)

        for op in ops:
            for j in op.deps:
                d = ops[j]
                if d.track is None and not skip(d, op):
                    d.signal = True
        cnt = {e: 0 for e in self.ENGS}
        for op in ops:
            if op.track is None and op.signal:
                cnt[op.eng] += 1
                op.tick = cnt[op.eng]
        seen = {e: {} for e in self.ENGS}
        for op in ops:
            w = {}
            for j in op.deps:
                d = ops[j]
                if d.track is not None:
                    key, val = d.track.sem, d.track.inc * d.ordinal
                else:
                    if skip(d, op):
                        continue
                    key, val = self.esem[d.eng], d.tick
                if w.get(key, 0) < val:
                    w[key] = val
            sd = seen[op.eng]
            op.waits = []
            for key, val in w.items():
                if sd.get(key, 0) < val:
                    sd[key] = val
                    op.waits.append((key, val))
        by_eng = {e: [op for op in ops if op.eng == e] for e in self.ENGS}
        esem = self.esem

        def run(engname, eng):
            for op in by_eng[engname]:
                for (sem, val) in op.waits:
                    eng.wait_ge(sem, val)
                if op.fn is None:
                    continue
                ins = op.fn(eng)
                if op.track is not None:
                    if op.track.inc == 16:
                        ins.then_inc(op.track.sem, 16)
                    else:
                        ins.then_inc(op.track.sem)
                elif op.signal:
                    ins.then_inc(esem[engname], 1)

        with self.nc.Block() as block:
            @block.tensor
            def _(e):
                run("pe", e)

            @block.scalar
            def _(e):
                run("act", e)

            @block.vector
            def _(e):
                run("dve", e)

            @block.gpsimd
            def _(e):
                run("pool", e)

            @block.sync
            def _(e):
                run("sp", e)


def _consts():
    c = np.zeros((128, 1024), np.float32)
    r = np.arange(128)
    c[:, 0:128] = np.eye(128)
    c[:, 128:256] = (r[:, None] <= r[None, :]).astype(np.float32)
    c[:, 256:384] = c[:, 128:256] * (-1.0 / 16.0)
    c[:, 384:512] = 1.0
    c[:, 512] = r
    for g_ in range(16):
        c[:, 528 + g_ * 16: 528 + (g_ + 1) * 16] = np.where(np.arange(16) < g_, 0.0, NEG)
    for h in range(8):
        for q in range(4):
            c[0:4, 784 + h * 4 + q] = (np.arange(4) <= q).astype(np.float32)
    em = np.zeros((64, 64 * 128), np.float32)
    for m in range(64):
        em[m, m * 128:(m + 1) * 128] = 1.0
    bd = np.zeros((4, 4 * 512), np.float32)
    for q in range(4):
        bd[q, q * 512:(q + 1) * 512] = 1.0
    return c, em, bd


def _owner(m):
    if m < 4:
        return m, 0
    if m < 8:
        return 7 - m, 1
    if m < 12:
        return m - 8, 2
    return 15 - m, 3


class _Stop(Exception):
    pass


def build(npool=NPOOL, stop=None):
    nc = bass.Bass("TRN2", target_bir_lowering=False)

    def din(name, shape, dt=F32):
        return nc.dram_tensor(name, list(shape), dt, kind="ExternalInput").ap()

    def dout(name, shape, dt=F32):
        return nc.dram_tensor(name, list(shape), dt, kind="ExternalOutput").ap()

    def dscr(name, shape, dt=F32):
        return nc.dram_tensor(name, list(shape), dt).ap()

    xp = din("xp", [1024, D])
    xs = din("xs", [4, D])
    ck = din("ck", [2 * npool * 128, 1024])
    cv = din("cv", [2 * npool * 128, 1024])
    sg = din("sg", [2, 4, 128, 256])
    ptin = din("pt", [1, 128], I32)
    meta = din("meta", [1, 64], I32)
    norm_g = din("norm_g", [2, D])
    w_in = din("w_in", [2, D, NIN])
    w_gate2 = din("w_gate2", [2, 16, 512])
    b_gate = din("b_gate", [2, 512])
    gla_g = din("gla_g", [2, 1024])
    w_a = din("w_a", [2, 1024, D])
    w_b = din("w_b", [2, 1024, D])
    b_merge = din("b_merge", [2, 4096])
    w_out = din("w_out", [2, D, D])
    fng = din("fng", [1, D])
    cst = din("cst", [128, 1024])
    emc = din("emc", [64, 64 * 128])
    bdc = din("bdc", [4, 2048])
    tab = din("tab", [128, 160])

    yp = dout("yp", [1024, D])
    ys = dout("ys", [4, D])
    kp = dout("kp", [2, 1024, 1024])
    vp = dout("vp", [2, 1024, 1024])
    gp = dout("gp", [2, 4, 128, 256])
    ks = dout("ks", [2, 4, 1024])
    vs = dout("vs", [2, 4, 1024])
    gs = dout("gs", [2, 4, 128, 256])

    xscr = dscr("xscr", [1028, D])
    gscr = dscr("gscr", [516, 4096], BF16)
    oscr = dscr("oscr", [4, 1024])
    EKN = 8 * 128 * 512
    EVN = 512 * 1024
    E2S = 128 * 4 * 256
    ESN = 2 * E2S
    EMN = 2 * 512 + 2 * 1024
    LR = [(l, r) for l in range(2) for r in range(2)]
    ekin = {lr: dscr("ekin%d%d" % lr, [EKN], BF16) for lr in LR}
    ekout = {lr: dscr("ekout%d%d" % lr, [4 * EKN], BF16) for lr in LR}
    evin = {lr: dscr("evin%d%d" % lr, [EVN], BF16) for lr in LR}
    evout = {lr: dscr("evout%d%d" % lr, [4 * EVN], BF16) for lr in LR}
    esin = {lr: dscr("esin%d%d" % lr, [ESN]) for lr in LR}
    esout = {lr: dscr("esout%d%d" % lr, [4 * ESN]) for lr in LR}
    emin = {lr: dscr("emin%d%d" % lr, [EMN]) for lr in LR}
    emout = {lr: dscr("emout%d%d" % lr, [4 * EMN]) for lr in LR}

    with ExitStack() as st:
        g = Graph(nc, st)

        TOTAL = (nc.sbuf_bytes_remaining // 64) * 64 - 64
        big = st.enter_context(nc.sbuf_tensor("big", [128, TOTAL // 2], BF16))

        class Reg:
            def __init__(self, base, size):
                self.base, self.size, self.cur = base, size, base

            def reset(self):
                self.cur = self.base

        PSZ, RSZ = 47104, 81920
        RP = Reg(0, PSZ)
        RR = Reg(PSZ, RSZ)
        RX = Reg(PSZ + RSZ, TOTAL - PSZ - RSZ)
        DTS = {F32: 4, BF16: 2, I32: 4, U32: 4}

        def sb(name, shape, dt, reg=None):
            reg = reg or RP
            n = 1
            for d_ in shape[1:]:
                n *= d_
            nb = n * DTS[dt]
            nb_al = (nb + 31) // 32 * 32
            off = reg.cur
            reg.cur += nb_al
            assert reg.cur <= reg.base + reg.size, (name, reg.cur, reg.base + reg.size)
            ap = big[0:shape[0], off // 2:(off + nb) // 2]
            if dt != BF16:
                ap = ap.bitcast(dt)
            if len(shape) == 3:
                ap = ap.rearrange("p (a b) -> p a b", a=shape[1])
            return ap


        bufs = {}

        def B(*key):
            if key not in bufs:
                bufs[key] = Buf(str(key))
            return bufs[key]

        def load(eng, out, in_, dst, reads=()):
            if dst.lt is None:
                dst.lt = {}
            if eng not in dst.lt:
                dst.lt[eng] = g.track()
            g.add(eng, lambda e: e.dma_start(out=out, in_=in_, allow_slow_non_contiguous=True), reads=list(reads), writes=[dst], dma=dst.lt[eng])

        def store(eng, out, in_, src, dst):
            if src.st is None:
                src.st = {}
            if eng not in src.st:
                src.st[eng] = g.track()
            g.add(eng, lambda e: e.dma_start(out=out, in_=in_, allow_slow_non_contiguous=True), reads=[src], writes=[dst], dma=src.st[eng])

        def mm(out, lhsT, rhs, start, stop, reads, writes):
            g.add("pe", lambda e: e.matmul(out, lhsT, rhs, start=start, stop=stop), reads=reads, writes=writes)

        def tr(out, in_, ident_ap, reads, writes):
            g.add("pe", lambda e: e.transpose(out, in_, ident_ap), reads=reads, writes=writes)

        def act(out, in_, func, reads, writes, bias=None, scale=None, accum=None):
            kw = {}
            if bias is not None:
                kw["bias"] = bias
            if scale is not None:
                kw["scale"] = scale
            if accum is not None:
                kw["accum_out"] = accum
            g.add("act", lambda e: e.activation(out=out, in_=in_, func=func, **kw), reads=reads, writes=writes)

        def dve(fn, reads, writes, eng="dve"):
            g.add(eng, fn, reads=reads, writes=writes)

        def cp(out, in_, reads, writes, eng="dve"):
            g.add(eng, lambda e: e.tensor_copy(out=out, in_=in_), reads=reads, writes=writes)

        def barrier():
            extra = set(g.last[e_] for e_ in ("pe", "act", "dve") if g.last[e_] is not None)
            for t_ in g.tracks:
                if t_.inc == 16 and t_.last_op is not None:
                    extra.add(t_.last_op)
            for e_ in Graph.ENGS:
                g.add(e_, None, extra=extra)

        psum = [st.enter_context(nc.psum_tensor("ps%d" % i, [128, 512], F32)) for i in range(8)]
        psb = [B("ps", i) for i in range(8)]
        rot = [0]

        def ps_next():
            i = rot[0]
            rot[0] = (rot[0] + 1) % 5
            return psum[i], psb[i]

        cf = sb("cf", [128, 1024], F32)
        cb = sb("cb", [128, 528], BF16)
        emb = sb("emb", [16, 16 * 128], BF16)
        bdb = sb("bdb", [4, 2048], BF16)
        tabs = sb("tabs", [128, 160], F32)
        ident_b = cb[:, 0:128]
        tri_b = cb[:, 128:256]
        ones_b = cb[:, 384:512]
        ident_f = cf[:, 0:128]
        trineg_f = cf[:, 256:384]
        iota_f = cf[:, 512:513]
        BC = B("const")
        load("sp", cf, cst[:, :], BC)
        load("pool", cb, cst[:, 0:528], BC)
        load("pool", emb, emc[0:16, 0:2048], BC)
        load("pool", bdb, bdc[:, :], BC)
        load("sp", tabs, tab[:, :], BC)
        gcol = sb("gcol", [128, 2, KC], F32)
        load("sp", gcol, norm_g.rearrange("l (k p) -> p l k", p=128), BC)
        gnrep = sb("gnrep", [128, 1024], F32)
        wg2 = sb("wg2", [17, 2, 512], F32)
        load("sp", wg2[0:16, :, :], w_gate2.rearrange("l r n -> r l n"), BC)
        load("sp", wg2[16:17, :, :], b_gate.rearrange("(o l) n -> o l n", o=1), BC)
        pidx = sb("pidx", [128, 2, 128], I32)
        S_run = sb("S_run", [128, 4, 256], F32)
        S_st = [sb("S_st%d" % i, [128, 4, 256], F32) for i in range(2)]
        S_bf = sb("S_bf", [128, 4, 256], BF16)
        S_s = sb("S_s", [128, 4, 256], F32)
        meansf = sb("meansf", [128, 16, 8], F32)
        meansb = sb("meansb", [128, 16, 8], BF16)
        mown = sb("mown", [128, 2, 8], F32)
        edec = sb("edec", [128, 5, 4], F32)
        ssq = sb("ssq", [128, 1], F32)
        rstd = sb("rstd", [128, 1], F32)
        epsc = sb("epsc", [128, 1], F32)
        dve(lambda e: e.memset(epsc, EPS), [], [B("ssq")])
        rec = sb("rec", [128, 1], F32)
        atot = sb("atot", [128, 4], F32)
        ssq4 = sb("ssq4", [128, 4], F32)
        smv = sb("smv", [128, 16], F32)
        top8 = sb("top8", [128, 8], F32)
        selb = sb("selb", [128, 16], F32)
        selbb = sb("selbb", [128, 16], BF16)
        selT = sb("selT", [16, 128], BF16)

        NT = 516
        xnT = sb("xnT", [128, KC, NT], BF16, RR)
        qT = sb("qT", [128, 8, NT], BF16, RR)
        obT = sb("obT", [128, 8, NT], BF16, RR)
        qi = sb("qi", [128, 4, NT], BF16, RR)
        ki = sb("ki", [128, 4, NT], BF16, RR)
        kd = sb("kd", [128, 5, 512], BF16, RR)
        vbr = sb("vbr", [128, 5, 1024], BF16, RR)
        sza = sb("sza", [128, 5, 1024], BF16, RR)
        szb = sb("szb", [128, 5, 1024], BF16, RR)
        alrT = sb("alrT", [17, NT], F32, RR)
        ksT = sb("ksT", [128, 8, 4], BF16, RR)
        dve(lambda e: e.memset(alrT, 1.0), [], [B("alrT", ti_) for ti_ in range(5)])
        vsr = sb("vsr", [4, 1024], BF16, RR)

        RX.reset()
        wbuf = [sb("wbuf%d" % i, [128, KC * 512], BF16, RX) for i in range(2)]
        wB = [B("wbuf", i) for i in range(2)]
        wrot = [0]
        qbT = sb("qbT", [128, 4, NT], BF16, RX)
        kbT = sb("kbT", [128, 4, NT], BF16, RX)
        xt = [sb("xt0", [128, D], F32, RX)] * 2
        xnb = sb("xnb", [128, D], BF16, RX)
        junk = xnb
        ef32 = [sb("ef32_%d" % i, [128, 512], F32, RX) for i in range(2)]
        eb16 = [sb("eb16_%d" % i, [128, 512], BF16, RX) for i in range(2)]
        ktst = [sb("ktst%d" % i, [128, 4, 128], BF16, RX) for i in range(2)]
        bms = [sb("bms%d" % i, [1, 512], BF16, RX) for i in range(2)]
        gtile = sb("gtile", [128, 512], F32, RX)
        bTt = sb("bTt", [128, 4, 128], F32, RX)
        eb_ = sb("eb_", [128, 4, 128], F32, RX)
        enb = sb("enb", [128, 4, 128], F32, RX)
        ekd = sb("ekd", [128, 4, 128], F32, RX)
        kdT = sb("kdT", [128, 4, 128], BF16, RX)
        Sl = sb("Sl", [128, 4, 256], F32, RX)
        ecnt = [0]
        cnt2 = [0]
        pti = sb("pti", [128, 128], I32, RX)
        ptf = sb("ptf", [128, 2, 128], F32, RX)
        RX.reset()
        kpgb = [sb("kpg%d" % i, [128, 2, 1024], BF16, RX) for i in range(2)]
        kTs = [sb("kTs%d" % i, [128, 16, 128], BF16, RX) for i in range(2)]
        PallT = sb("PallT", [128, 128, 32], BF16, RX)
        means_s = sb("means_s", [128, 512], BF16, RX)
        sms = sb("sms", [4, 512], F32, RX)
        top8s = sb("top8s", [4, 64], F32, RX)
        sel01b = sb("sel01b", [4, 512], BF16, RX)
        rhsbd = sb("rhsbd", [4, 4, 512], BF16, RX)
        maskrep = sb("maskrep", [128, 4, 512], BF16, RX)
        pown = sb("pown", [4, 32], F32, RX)
        pownb = sb("pownb", [4, 32], BF16, RX)
        osb = sb("osb", [32, 1024], F32, RX)
        ost = sb("ost", [4, 1024], F32, RX)
        oasb = sb("oasb", [4, 1024], BF16, RX)
        RX.reset()
        slb = [sb("slb%d" % i, [128, 4, 256], F32, RX) for i in range(2)]
        atg = [sb("atg%d" % i, [128, 4], F32, RX) for i in range(2)]
        att = [sb("att%d" % i, [128, 128], BF16, RX) for i in range(2)]
        otmp = sb("otmp", [128, 256], F32, RX)
        obst = sb("obst", [128, 1024], BF16, RX)
        junk2 = sb("junk2", [128, 256], BF16, RX)
        kbuf = [sb("kbuf%d" % i, [128, 17 * 256], BF16, RX) for i in range(2)]
        vbuf = [sb("vbuf%d" % i, [128, 34, 128], BF16, RX) for i in range(2)]
        ptt = [sb("ptt%d" % i, [128, 512], BF16, RX) for i in range(2)]
        oat = sb("oat", [128, 128], BF16, RX)
        RX.reset()
        wbufD = [sb("wbufD%d" % i, [128, KC * 512], BF16, RX) for i in range(2)]
        gab = [sb("gab%d" % i, [128, 2, 512], BF16, RX) for i in range(2)]
        mst = sb("mst", [128, 512], BF16, RX)
        t1 = sb("t1", [128, 512], F32, RX)
        t2 = sb("t2", [128, 512], F32, RX)
        xres = [sb("xres%d" % i, [128, 512], F32, RX) for i in range(2)]
        RX.reset()
        xtF = sb("xtF", [128, D], F32, RX)
        fngrep = sb("fngrep", [128, D], F32, RX)
        junkF = sb("junkF", [128, D], BF16, RX)

        BPI = B("pidx")
        load("sp", pti, ptin.broadcast_to([128, 128]), BPI)
        cp(ptf[:, 0, :], pti, [BPI], [BPI])
        dve(lambda e: e.tensor_scalar(out=ptf[:, 0, :], in0=ptf[:, 0, :], scalar1=128.0, scalar2=iota_f,
                                      op0=ALU.mult, op1=ALU.add), [BC, BPI], [BPI])
        dve(lambda e: e.tensor_scalar(out=ptf[:, 1, :], in0=ptf[:, 0, :], scalar1=float(npool * 128), scalar2=None,
                                      op0=ALU.add), [BC, BPI], [BPI])
        cp(pidx, ptf, [BPI], [BPI])

        ONE = B("one")

        def rms_rstd(dst_rstd, src_ap, n, tsz, reads, tmpjunk, jb):
            act(tmpjunk, src_ap, AF.Square, reads, [jb, B("ssq")], accum=ssq[:tsz, :])
            act(dst_rstd, ssq[:tsz, :], AF.Sqrt, [B("ssq")], [B("rstd")], scale=1.0 / n, bias=epsc[:tsz, :])
            dve(lambda e: e.reciprocal(out=dst_rstd, in_=dst_rstd), [B("rstd")], [B("rstd")])

        try:
            if stop == "INIT":
                raise _Stop()
            for l in range(2):
                dve(lambda e: e.memset(S_run[:], 0.0), [], [B("S_run")])
                dve(lambda e: e.memset(meansf, 0.0), [], [B("meansf")])
                load("sp", gnrep, gla_g[l:l + 1, :].broadcast_to([128, 1024]), B("gnrep"))
                load("sp", S_s[:], sg[l].rearrange("h k v -> k h v"), B("S_s"))
                for rd in range(2):
                    tiles = [(rd * 512 + i * 128, i * 128, 128, False) for i in range(4)]
                    if rd == 0:
                        tiles.append((0, 512, 4, True))
                    nt = len(tiles)
                    xsrc = (lambda r0, n: xp[r0:r0 + n, :]) if l == 0 else (lambda r0, n: xscr[r0:r0 + n, :])
                    xssrc = xs[:, :] if l == 0 else xscr[1024:1028, :]

                    barrier()
                    for ti, (r0, c0, tsz, smp) in enumerate(tiles):
                        xb_ = xt[ti % 2]
                        XB = B("xt", 0)
                        load("sp", xb_[:tsz, :], xssrc if smp else xsrc(r0, tsz), XB,
                             reads=([B("xscr", "s") if smp else B("xscr", r0)] if l == 1 else []))
                        rms_rstd(rstd[:tsz, :], xb_[:tsz, :], D, tsz, [XB], junk[:tsz, :], B("xnb"))
                        act(xnb[:tsz, :], xb_[:tsz, :], AF.Copy, [XB, B("rstd")], [B("xnb")], scale=rstd[:tsz, :])
                        for half in range(2):
                            pt_, PB = ps_next()
                            pv = pt_[:].bitcast(BF16)
                            for kk in range(8):
                                k = half * 8 + kk
                                tr(pv[:, kk * 128: kk * 128 + tsz], xnb[:tsz, k * 128:(k + 1) * 128],
                                   ident_b[:tsz, :tsz], [B("xnb"), BC], [PB])
                            for kk in range(8):
                                k = half * 8 + kk
                                if half == 0:
                                    act(xnT[:, k, c0:c0 + tsz], pv[:, kk * 128: kk * 128 + tsz], AF.Copy,
                                        [PB, BC], [B("xnT", ti)], scale=gcol[:, l, k:k + 1])
                                else:
                                    dve(lambda e, k=k, kk=kk, pv=pv, c0=c0, tsz=tsz, l=l: e.tensor_scalar(
                                        out=xnT[:, k, c0:c0 + tsz], in0=pv[:, kk * 128: kk * 128 + tsz],
                                        scalar1=gcol[:, l, k:k + 1], scalar2=None, op0=ALU.mult),
                                        [PB, BC], [B("xnT", ti)])

                    if stop == "P1":
                        raise _Stop()
                    chunks = []
                    for i in range(2):
                        chunks.append((i * 512, 512, "q", i))
                    for i in range(2):
                        chunks.append((1024 + i * 512, 512, "k", i))
                    for i in range(2):
                        chunks.append((2048 + i * 512, 512, "v", i))
                    for i in range(2):
                        chunks.append((3072 + i * 512, 512, "za", i))
                    chunks.append((4096, 512, "qb", 0))
                    chunks.append((4608, 512, "kb", 0))
                    for i in range(2):
                        chunks.append((5120 + i * 512, 512, "vb", i))
                    for i in range(2):
                        chunks.append((6144 + i * 512, 512, "zb", i))
                    chunks.append((7168, 16, "alr", 0))
                    for i in range(8):
                        chunks.append((7184 + i * 512, 512, "g", i))
                    wv = w_in[l].rearrange("(k p) n -> p k n", p=128)
                    e1kT = ekin[(l, rd)].rearrange("(h d t) -> h d t", h=8, d=128)
                    e1V = evin[(l, rd)].rearrange("(t n) -> t n", n=1024)
                    EKB = B("ekin", l, rd)
                    EVB = B("evin", l, rd)
                    for (col0, w, kind, sub) in chunks:
                        if _DEV.get("kinds") and kind not in _DEV["kinds"]:
                            continue
                        ws = wrot[0] % 2
                        wrot[0] += 1
                        wt = wbuf[ws][:, 0:KC * w].rearrange("p (k n) -> p k n", k=KC)
                        load("pool", wt, wv[:, :, col0:col0 + w], wB[ws])
                        if kind == "g":
                            load("pool", bms[sub % 2], b_merge[l:l + 1, sub * 512:(sub + 1) * 512], B("bms", sub % 2))
                        for ti, (r0, c0, tsz, smp) in enumerate(tiles):
                            pt_, PB = ps_next()
                            po = pt_[:tsz, 0:w]
                            for k in range(KC):
                                mm(po, xnT[:, k, c0:c0 + tsz], wt[:, k, :], k == 0, (k == KC - 1) and kind != "g",
                                   [B("xnT", ti), wB[ws]], [PB])
                            if kind == "g":
                                mm(po, ones_b[0:1, :tsz], bms[sub % 2][0:1, :], False, True,
                                   [BC, B("bms", sub % 2)], [PB])
                            es = ecnt[0] % 2
                            ecnt[0] += 1
                            EF, EB = B("ef32", es), B("eb16", es)
                            f32t, b16t = ef32[es], eb16[es]
                            if kind in ("q", "qb", "kb"):
                                act(b16t[:tsz, :], po, AF.Copy, [PB], [EB])
                                p2, PB2 = ps_next()
                                pv = p2[:].bitcast(BF16)
                                for hh in range(4):
                                    tr(pv[:, hh * 128: hh * 128 + tsz], b16t[:tsz, hh * 128:(hh + 1) * 128],
                                       ident_b[:tsz, :tsz], [EB, BC], [PB2])
                                pv4 = pv[:, 0:512].rearrange("p (h t) -> p h t", h=4)[:, :, 0:tsz]
                                if kind == "q":
                                    dstap = qT[:, sub * 4:(sub + 1) * 4, c0:c0 + tsz]
                                    dB = [B("qT", ti, sub * 4 + hh) for hh in range(4)]
                                elif kind == "qb":
                                    dstap = qbT[:, :, c0:c0 + tsz]
                                    dB = [B("qbT", ti)]
                                else:
                                    dstap = kbT[:, :, c0:c0 + tsz]
                                    dB = [B("kbT", ti)]
                                cp(dstap, pv4, [PB2], dB)
                            elif kind == "k":
                                cp(f32t[:tsz, :], po, [PB], [EF])
                                act(b16t[:tsz, :], f32t[:tsz, :], AF.Copy, [EF], [EB])
                                if smp:
                                    store("sp", ks[l][:, sub * 512:(sub + 1) * 512], f32t[:tsz, :], EF, B("ks", l))
                                else:
                                    store("sp", kp[l][r0:r0 + tsz, sub * 512:(sub + 1) * 512], f32t[:tsz, :], EF,
                                          B("kp", l))
                                p2, PB2 = ps_next()
                                pv = p2[:].bitcast(BF16)
                                for hh in range(4):
                                    tr(pv[:, hh * 128: hh * 128 + tsz], b16t[:tsz, hh * 128:(hh + 1) * 128],
                                       ident_b[:tsz, :tsz], [EB, BC], [PB2])
                                pv4 = pv[:, 0:512].rearrange("p (h t) -> p h t", h=4)[:, :, 0:tsz]
                                if smp:
                                    cp(ksT[:, sub * 4:(sub + 1) * 4, :], pv4, [PB2], [B("ksT")])
                                else:
                                    kt_ = ktst[es]
                                    KT = B("ktst", es)
                                    cp(kt_[:], pv4, [PB2], [KT])
                                    store("sp", e1kT[sub * 4:(sub + 1) * 4, :, c0:c0 + 128].rearrange("h d t -> d h t"),
                                          kt_[:], KT, EKB)
                                    bs = ti // 2
                                    if ti % 2 == 0:
                                        dve(lambda e, kt_=kt_, sub=sub, bs=bs: e.tensor_reduce(
                                            out=mown[:, bs, sub * 4:(sub + 1) * 4], in_=kt_[:], axis=AX.X, op=ALU.add),
                                            [KT], [B("mown")])
                                    else:
                                        dve(lambda e, kt_=kt_: e.tensor_reduce(
                                            out=ssq4[:], in_=kt_[:], axis=AX.X, op=ALU.add), [KT], [B("ssq4")])
                                        dve(lambda e, sub=sub, bs=bs: e.tensor_tensor(
                                            out=mown[:, bs, sub * 4:(sub + 1) * 4], in0=mown[:, bs, sub * 4:(sub + 1) * 4],
                                            in1=ssq4[:], op=ALU.add), [B("ssq4"), B("mown")], [B("mown")])
                            elif kind == "v":
                                cp(f32t[:tsz, :], po, [PB], [EF])
                                if smp:
                                    act(vsr[:, sub * 512:(sub + 1) * 512], f32t[:tsz, :], AF.Copy, [EF], [B("vsr")])
                                    if not _DEV.get("no_vs"):
                                        store("sp", vs[l][:, sub * 512:(sub + 1) * 512], f32t[:tsz, :], EF, B("vs", l))
                                else:
                                    act(b16t[:tsz, :], f32t[:tsz, :], AF.Copy, [EF], [EB])
                                    if not _DEV.get("no_vp"):
                                        store("sp", vp[l][r0:r0 + tsz, sub * 512:(sub + 1) * 512], f32t[:tsz, :], EF,
                                              B("vp", l))
                                    if not _DEV.get("no_e1"):
                                        store("sp", e1V[c0:c0 + tsz, sub * 512:(sub + 1) * 512], b16t[:tsz, :], EB, EVB)
                            elif kind == "za":
                                act(sza[:tsz, ti, sub * 512:(sub + 1) * 512], po, AF.Silu, [PB], [B("sza", ti)])
                            elif kind == "zb":
                                act(szb[:tsz, ti, sub * 512:(sub + 1) * 512], po, AF.Silu, [PB], [B("szb", ti)])
                            elif kind == "vb":
                                act(vbr[:tsz, ti, sub * 512:(sub + 1) * 512], po, AF.Copy, [PB], [B("vbr", ti)])
                            elif kind == "alr":
                                cp(f32t[:tsz, 0:16], po, [PB], [EF])
                                p2, PB2 = ps_next()
                                tr(p2[0:16, 0:tsz], f32t[:tsz, 0:16], ident_f[:tsz, :tsz], [EF, BC], [PB2])
                                cp(alrT[0:16, c0:c0 + tsz], p2[0:16, 0:tsz], [PB2], [B("alrT", ti)])
                            elif kind == "g":
                                act(b16t[:tsz, :], po, AF.Sigmoid, [PB], [EB])
                                store("sp", gscr[c0:c0 + tsz, sub * 512:(sub + 1) * 512], b16t[:tsz, :], EB, B("gscr"))

                    if stop == "P2":
                        raise _Stop()
                    e2s = esin[(l, rd)]
                    e2m = emin[(l, rd)]
                    ESB = B("esin", l, rd)
                    EMB = B("emin", l, rd)
                    for ti, (r0, c0, tsz, smp) in enumerate(tiles):
                        if ti == 0 and rd == 0 and l == 0:
                            pass
                        pu, PU = ps_next()
                        mm(pu[:tsz, :], alrT[0:17, c0:c0 + tsz], wg2[0:17, l, :], True, True, [B("alrT", ti), BC], [PU])
                        gt = gtile
                        act(gt[:tsz, :], pu[:tsz, :], AF.Exp, [PU], [B("gt")], scale=-1.0)
                        act(gt[:tsz, :], gt[:tsz, :], AF.Ln, [B("gt")], [B("gt")], bias=1.0)
                        pb_, PBb = ps_next()
                        pb4 = pb_[:, :].rearrange("p (h t) -> p h t", h=4)
                        for h in range(4):
                            mm(pb4[:, h, 0:tsz], gt[:tsz, h * 128:(h + 1) * 128], trineg_f[:tsz, :tsz], True, True,
                               [B("gt"), BC], [PBb])
                        cp(bTt[:, :, 0:tsz], pb4[:, :, 0:tsz], [PBb], [B("bTt")])
                        act(eb_[:, :, 0:tsz], bTt[:, :, 0:tsz], AF.Exp, [B("bTt")], [B("eb")])
                        act(enb[:, :, 0:tsz], bTt[:, :, 0:tsz], AF.Exp, [B("bTt")], [B("enb")], scale=-1.0)
                        for h in range(4):
                            act(ekd[:, h, 0:tsz], bTt[:, h, 0:tsz], AF.Exp, [B("bTt")], [B("ekd")], scale=-1.0,
                                bias=bTt[:, h, tsz - 1:tsz])
                        act(edec[:, ti, :], bTt[:, :, tsz - 1], AF.Exp, [B("bTt")], [B("edec", ti)])
                        dve(lambda e, c0=c0, tsz=tsz: e.scalar_tensor_tensor(
                            out=qi[:, :, c0:c0 + tsz], in0=qbT[:, :, c0:c0 + tsz], scalar=SCALE, in1=eb_[:, :, 0:tsz],
                            op0=ALU.mult, op1=ALU.mult), [B("qbT", ti), B("eb")], [B("qi", ti)])
                        dve(lambda e, c0=c0, tsz=tsz: e.tensor_tensor(
                            out=ki[:, :, c0:c0 + tsz], in0=kbT[:, :, c0:c0 + tsz], in1=enb[:, :, 0:tsz], op=ALU.mult),
                            [B("kbT", ti), B("enb")], [B("ki", ti)])
                        dve(lambda e, c0=c0, tsz=tsz: e.tensor_tensor(
                            out=kdT[:, :, 0:tsz], in0=kbT[:, :, c0:c0 + tsz], in1=ekd[:, :, 0:tsz], op=ALU.mult),
                            [B("kbT", ti), B("ekd")], [B("kdT")])
                        p2, PB2 = ps_next()
                        pv = p2[:].bitcast(BF16)
                        for h in range(4):
                            tr(pv[:tsz, h * 128:(h + 1) * 128], kdT[:, h, 0:tsz], ident_b[:, :], [B("kdT"), BC], [PB2])
                        cp(kd[:tsz, ti, :], pv[:tsz, 0:512], [PB2], [B("kd", ti)])
                        if smp:
                            continue
                        first = (ti % 2 == 0)
                        for half in range(2):
                            ps_, PS_ = ps_next()
                            for hh in range(2):
                                h = half * 2 + hh
                                mm(ps_[:, hh * 256:(hh + 1) * 256], kd[:tsz, ti, h * 128:(h + 1) * 128],
                                   vbr[:tsz, ti, h * 256:(h + 1) * 256], True, True, [B("kd", ti), B("vbr", ti)], [PS_])
                            for hh in range(2):
                                h = half * 2 + hh
                                if first:
                                    cp(Sl[:, h, :], ps_[:, hh * 256:(hh + 1) * 256], [PS_], [B("Sl")])
                                else:
                                    dve(lambda e, h=h, hh=hh, ps_=ps_, ti=ti: e.scalar_tensor_tensor(
                                        out=Sl[:, h, :], in0=Sl[:, h, :], scalar=edec[:, ti, h:h + 1],
                                        in1=ps_[:, hh * 256:(hh + 1) * 256], op0=ALU.mult, op1=ALU.add),
                                        [PS_, B("Sl"), B("edec", ti)], [B("Sl")])
                        if not first:
                            bs = ti // 2
                            store("sp", e2s[bs * E2S:(bs + 1) * E2S].rearrange("(p h v) -> p h v", p=128, h=4),
                                  Sl[:], B("Sl"), ESB)
                            dve(lambda e, ti=ti: e.tensor_tensor(out=atot[:], in0=edec[:, ti - 1, :], in1=edec[:, ti, :],
                                                                 op=ALU.mult), [B("edec", ti - 1), B("edec", ti)], [B("atot")])
                            store("sp", e2m[bs * 512:(bs + 1) * 512].rearrange("(p h) -> p h", p=128),
                                  atot[:], B("atot"), EMB)
                    store("sp", e2m[1024:EMN].rearrange("(p s h) -> p s h", p=128, s=2), mown[:], B("mown"), EMB)

                    if stop == "P3":
                        raise _Stop()
                    for (src, dst, SB_, DB_) in ((emin[(l, rd)], emout[(l, rd)], EMB, B("emout", l, rd)),
                                                (esin[(l, rd)], esout[(l, rd)], ESB, B("esout", l, rd)),
                                                (ekin[(l, rd)], ekout[(l, rd)], EKB, B("ekout", l, rd)),
                                                (evin[(l, rd)], evout[(l, rd)], EVB, B("evout", l, rd))):
                        trk = g.track(inc=1)
                        g.add("pool", lambda e, src=src, dst=dst: e.collective_compute(
                            "AllGather", ALU.bypass, replica_groups=[[0, 1, 2, 3], [4, 5, 6, 7]], ins=[src.opt()], outs=[dst.opt()]),
                            reads=[SB_], writes=[DB_], dma=trk)

                    eko = [ekout[(l, r_)].rearrange("(r n) -> r n", r=4) for r_ in range(2)]
                    evo = [evout[(l, r_)].rearrange("(r n) -> r n", r=4) for r_ in range(2)]
                    eso = esout[(l, rd)].rearrange("(r n) -> r n", r=4)
                    emo = emout[(l, rd)].rearrange("(r n) -> r n", r=4)
                    EKO = [B("ekout", l, r_) for r_ in range(2)]
                    EVO = [B("evout", l, r_) for r_ in range(2)]
                    ESO = B("esout", l, rd)
                    EMO = B("emout", l, rd)

                    if stop == "EX":
                        raise _Stop()
                    barrier()
                    if rd == 0:
                        QS = [B("qT", 4, h) for h in range(8)]
                        MP, MPB = psum[7], psb[7]
                        for n in range(64):
                            kpg = kpgb[n % 2]
                            KPG = B("kpg", n % 2)
                            for i in range(2):
                                pg = 2 * n + i
                                if KPG.lt is None:
                                    KPG.lt = {"pool": g.track()}
                                g.add("pool", lambda e, kpg=kpg, i=i, pg=pg, l=l: e.indirect_dma_start(
                                    out=kpg[:, i, :], out_offset=None, in_=ck[:, :],
                                    in_offset=bass.IndirectOffsetOnAxis(ap=pidx[:, l, pg:pg + 1], axis=0)),
                                    reads=[BPI], writes=[KPG], dma=KPG.lt["pool"])
                            for h in range(8):
                                for i in range(2):
                                    mm(MP[:, h * 64 + n: h * 64 + n + 1], kpg[:, i, h * 128:(h + 1) * 128], ones_b[:, 0:1],
                                       i == 0, i == 1, [KPG, BC], [MPB])
                            kts = kTs[n % 2]
                            KTS = B("kTs", n % 2)
                            for i in range(2):
                                p2, PB2 = ps_next()
                                pv = p2[:].bitcast(BF16)
                                for h in range(8):
                                    tr(pv[:, h * 128:(h + 1) * 128], kpg[:, i, h * 128:(h + 1) * 128], ident_b[:, :],
                                       [KPG, BC], [PB2])
                                cp(kts[:, i * 8:(i + 1) * 8, :], pv[:, 0:1024].rearrange("p (h t) -> p h t", h=8), [PB2], [KTS])
                            p3, PB3 = ps_next()
                            for i in range(2):
                                for h in range(8):
                                    mm(p3[:, i * 32 + h * 4: i * 32 + h * 4 + 4], kts[:, i * 8 + h, :], qT[:, h, 512:516],
                                       True, True, [KTS, QS[h]], [PB3])
                            act(PallT[:, 2 * n:2 * n + 2, :], p3[:, 0:64].rearrange("p (i c) -> p i c", i=2), AF.Exp,
                                [PB3], [B("PallT")], scale=SCALE)
                        cp(means_s[:], MP[:, :], [MPB], [B("means_s")])
                        p4, PB4 = ps_next()
                        for h in range(8):
                            mm(p4[0:4, h * 64:(h + 1) * 64], qT[:, h, 512:516], means_s[:, h * 64:(h + 1) * 64], True, True,
                               [QS[h], B("means_s")], [PB4])
                        cp(sms[:], p4[0:4, :], [PB4], [B("sms")])
                        for h in range(8):
                            dve(lambda e, h=h: e.max(out=top8s[0:4, h * 8:(h + 1) * 8], in_=sms[0:4, h * 64:(h + 1) * 64]),
                                [B("sms")], [B("top8s")])
                        for h in range(8):
                            dve(lambda e, h=h: e.tensor_scalar(out=sel01b[0:4, h * 64:(h + 1) * 64],
                                                               in0=sms[0:4, h * 64:(h + 1) * 64],
                                                               scalar1=top8s[0:4, h * 8 + 2:h * 8 + 3], scalar2=None,
                                                               op0=ALU.is_ge), [B("sms"), B("top8s")], [B("sel01b")])
                        for q_ in range(4):
                            dve(lambda e, q_=q_: e.tensor_tensor(out=rhsbd[0:4, q_, :], in0=bdb[0:4, q_ * 512:(q_ + 1) * 512],
                                                                 in1=sel01b[0:4, :], op=ALU.mult), [B("sel01b"), BC],
                                [B("rhsbd")])
                        for q_ in range(4):
                            p5, PB5 = ps_next()
                            mm(p5[:, :], ones_b[0:4, :], rhsbd[0:4, q_, :], True, True, [B("rhsbd"), BC], [PB5])
                            cp(maskrep[:, q_, :], p5[:, :], [PB5], [B("maskrep")])
                        P5v = PallT[:, :, :].rearrange("p (n i) (h q) -> p n i h q", i=2, h=8)
                        Mv = maskrep[:, :, :].rearrange("p q (h n) -> p n h q", h=8)
                        for i in range(2):
                            for h in range(8):
                                dve(lambda e, i=i, h=h: e.tensor_tensor(out=P5v[:, :, i, h, :], in0=P5v[:, :, i, h, :],
                                                                         in1=Mv[:, :, h, :], op=ALU.mult),
                                    [B("PallT"), B("maskrep")], [B("PallT")])
                        OA, OAB = psum[5], psb[5]
                        OB_, OBB = psum[6], psb[6]
                        DN, DNB = psum[7], psb[7]
                        for n in range(64):
                            vpg = kpgb[n % 2]
                            KPG = B("kpg", n % 2)
                            for i in range(2):
                                pg = 2 * n + i
                                g.add("pool", lambda e, vpg=vpg, i=i, pg=pg, l=l: e.indirect_dma_start(
                                    out=vpg[:, i, :], out_offset=None, in_=cv[:, :],
                                    in_offset=bass.IndirectOffsetOnAxis(ap=pidx[:, l, pg:pg + 1], axis=0)),
                                    reads=[BPI], writes=[KPG], dma=KPG.lt["pool"])
                            for i in range(2):
                                pg = 2 * n + i
                                mm(OA[0:32, :], PallT[:, pg, :], vpg[:, i, 0:512], pg == 0, False, [B("PallT"), KPG], [OAB])
                                mm(OB_[0:32, :], PallT[:, pg, :], vpg[:, i, 512:1024], pg == 0, False, [B("PallT"), KPG], [OBB])
                                mm(DN[0:32, 0:1], PallT[:, pg, :], ones_b[:, 0:1], pg == 0, False, [B("PallT"), BC], [DNB])
                        p6, PB6 = ps_next()
                        for h in range(8):
                            mm(p6[0:4, h * 4:(h + 1) * 4], ksT[:, h, :], qT[:, h, 512:516], True, True, [B("ksT"), QS[h]], [PB6])
                        act(pown[:], p6[0:4, 0:32], AF.Exp, [PB6], [B("pown")], scale=SCALE)
                        dve(lambda e: e.tensor_tensor(out=pownb[:], in0=pown[:], in1=cf[0:4, 784:816], op=ALU.mult),
                            [B("pown"), BC], [B("pownb")])
                        mm(OA[0:32, :], pownb[0:4, :], vsr[0:4, 0:512], False, True, [B("pownb"), B("vsr")], [OAB])
                        mm(OB_[0:32, :], pownb[0:4, :], vsr[0:4, 512:1024], False, True, [B("pownb"), B("vsr")], [OBB])
                        mm(DN[0:32, 0:1], pownb[0:4, :], ones_b[0:4, 0:1], False, True, [B("pownb"), BC], [DNB])
                        dve(lambda e: e.reciprocal(out=rec[0:32, :], in_=DN[0:32, 0:1]), [DNB], [B("rec")])
                        dve(lambda e: e.tensor_scalar(out=osb[:, 0:512], in0=OA[0:32, :], scalar1=rec[0:32, :], scalar2=None,
                                                      op0=ALU.mult), [OAB, B("rec")], [B("osb")])
                        dve(lambda e: e.tensor_scalar(out=osb[:, 512:1024], in0=OB_[0:32, :], scalar1=rec[0:32, :],
                                                      scalar2=None, op0=ALU.mult), [OBB, B("rec")], [B("osb")])
                        for h in range(8):
                            store("sp", oscr[:, h * 128:(h + 1) * 128], osb[4 * h:4 * h + 4, h * 128:(h + 1) * 128],
                                  B("osb"), B("oscr"))
                        load("sp", ost[:], oscr[:, :], B("ost"), reads=[B("oscr")])
                        dve(lambda e: e.tensor_tensor(out=oasb[:], in0=ost[:], in1=sza[0:4, 4, :], op=ALU.mult),
                            [B("ost"), B("sza", 4)], [B("oasb")])
                        p7, PB7 = ps_next()
                        pv = p7[:].bitcast(BF16)
                        for h in range(8):
                            tr(pv[:, h * 128:h * 128 + 4], oasb[0:4, h * 128:(h + 1) * 128], ident_b[0:4, 0:4],
                               [B("oasb"), BC], [PB7])
                        cp(qT[:, :, 512:516], pv[:, 0:1024].rearrange("p (h t) -> p h t", h=8)[:, :, 0:4], [PB7], QS)

                    if stop == "P4d":
                        raise _Stop()
                    if rd == 0:
                        barrier()
                    for gq in range(8):
                        m = rd * 8 + gq
                        jm, slot = _owner(m)
                        bs = slot % 2
                        for sl in range(2):
                            col = 128 + (rd * 2 + sl) * 8 + gq
                            if gq == 0:
                                dve(lambda e, sl=sl, col=col: e.tensor_scalar(out=S_st[sl][:], in0=S_run[:],
                                                                              scalar1=tabs[:, col:col + 1], scalar2=None,
                                                                              op0=ALU.mult), [B("S_run"), BC], [B("S_st", sl)])
                            else:
                                dve(lambda e, sl=sl, col=col: e.scalar_tensor_tensor(
                                    out=S_st[sl][:], in0=S_run[:], scalar=tabs[:, col:col + 1], in1=S_st[sl][:],
                                    op0=ALU.mult, op1=ALU.add), [B("S_run"), BC, B("S_st", sl)], [B("S_st", sl)])
                        sl_ = slb[gq % 2]
                        SLB = B("slb", gq % 2)
                        load("sp", sl_[:], eso[jm, bs * E2S:(bs + 1) * E2S].rearrange("(p h v) -> p h v", p=128, h=4), SLB,
                             reads=[ESO])
                        ag_ = atg[gq % 2]
                        AGB = B("atg", gq % 2)
                        load("sp", ag_[:], emo[jm, bs * 512:(bs + 1) * 512].rearrange("(p h) -> p h", p=128),
                             AGB, reads=[EMO])
                        for h in range(4):
                            dve(lambda e, h=h, sl_=sl_, ag_=ag_: e.scalar_tensor_tensor(
                                out=S_run[:, h, :], in0=S_run[:, h, :], scalar=ag_[:, h:h + 1], in1=sl_[:, h, :],
                                op0=ALU.mult, op1=ALU.add), [B("S_run"), SLB, AGB], [B("S_run")])
                        load("sp", meansf[:, m, :],
                             emo[jm, 1024:EMN].rearrange("(p s h) -> p s h", p=128, s=2)[:, bs, :], B("meansf"),
                             reads=[EMO])
                    if rd == 1:
                        store("sp", gp[l].rearrange("h k v -> k h v"), S_run[:], B("S_run"), B("gp"))
                    cp(meansb[:], meansf[:], [B("meansf")], [B("meansb")])

                    if stop == "P4a":
                        raise _Stop()
                    def gla_tile(ti, c0, tsz, Sf, SB_, update):
                        for h in range(4):
                            pa, PA = ps_next()
                            mm(pa[:tsz, 0:tsz], ki[:, h, c0:c0 + tsz], qi[:, h, c0:c0 + tsz], True, True,
                               [B("ki", ti), B("qi", ti)], [PA])
                            at_ = att[h % 2]
                            ATB = B("att", h % 2)
                            dve(lambda e, pa=pa, at_=at_: e.tensor_tensor(out=at_[:tsz, :tsz], in0=pa[:tsz, 0:tsz],
                                                                           in1=tri_b[:tsz, :tsz], op=ALU.mult),
                                [PA, BC], [ATB])
                            po_, PO = ps_next()
                            mm(po_[:tsz, 0:256], at_[:tsz, :tsz], vbr[:tsz, ti, h * 256:(h + 1) * 256], True, False,
                               [ATB, B("vbr", ti)], [PO])
                            mm(po_[:tsz, 0:256], qi[:, h, c0:c0 + tsz], S_bf[:, h, :], False, True,
                               [B("qi", ti), B("S_bf")], [PO])
                            rms_rstd(rstd[:tsz, :], po_[:tsz, 0:256], 256, tsz, [PO], junk2[:tsz, 0:256], B("junk2"))
                            dve(lambda e, po_=po_, h=h: e.scalar_tensor_tensor(
                                out=otmp[:tsz, :], in0=po_[:tsz, 0:256], scalar=rstd[:tsz, :],
                                in1=gnrep[:tsz, h * 256:(h + 1) * 256], op0=ALU.mult, op1=ALU.mult),
                                [PO, B("rstd"), B("gnrep")], [B("otmp")])
                            dve(lambda e, h=h: e.tensor_tensor(out=obst[:tsz, h * 256:(h + 1) * 256], in0=otmp[:tsz, :],
                                                               in1=szb[:tsz, ti, h * 256:(h + 1) * 256], op=ALU.mult),
                                [B("otmp"), B("szb", ti)], [B("obst")])
                            if update:
                                pn, PN = ps_next()
                                mm(pn[:, 0:256], kd[:tsz, ti, h * 128:(h + 1) * 128], vbr[:tsz, ti, h * 256:(h + 1) * 256],
                                   True, True, [B("kd", ti), B("vbr", ti)], [PN])
                                dve(lambda e, pn=pn, h=h: e.scalar_tensor_tensor(
                                    out=Sf[:, h, :], in0=Sf[:, h, :], scalar=edec[:, ti, h:h + 1], in1=pn[:, 0:256],
                                    op0=ALU.mult, op1=ALU.add), [PN, SB_, B("edec", ti)], [SB_])
                        p2, PB2 = ps_next()
                        pv = p2[:].bitcast(BF16)
                        for jj in range(8):
                            tr(pv[:, jj * 128:jj * 128 + tsz], obst[:tsz, jj * 128:(jj + 1) * 128], ident_b[:tsz, :tsz],
                               [B("obst"), BC], [PB2])
                        cp(obT[:, :, c0:c0 + tsz], pv[:, 0:1024].rearrange("p (h t) -> p h t", h=8)[:, :, 0:tsz], [PB2],
                           [B("obT", ti)])

                    for sl in range(2):
                        for tt in range(2):
                            ti = 2 * sl + tt
                            cp(S_bf[:], S_st[sl][:], [B("S_st", sl)], [B("S_bf")])
                            gla_tile(ti, ti * 128, 128, S_st[sl], B("S_st", sl), tt == 0)
                    if rd == 0:
                        cp(S_bf[:], S_s[:], [B("S_s")], [B("S_bf")])
                        gla_tile(4, 512, 4, S_s, B("S_s"), True)
                        store("sp", gs[l].rearrange("h k v -> k h v"), S_s[:], B("S_s"), B("gs"))

                    if stop == "P4b":
                        raise _Stop()
                    nblk = 7 if rd == 0 else 15
                    cands = ([0, 1, 2], list(range(7))) if rd == 0 else (list(range(11)), list(range(15)))
                    e1kT = ekin[(l, rd)].rearrange("(h d t) -> h d t", h=8, d=128)
                    e1V = evin[(l, rd)].rearrange("(t n) -> t n", n=1024)
                    OP, OPB = psum[6], psb[6]
                    DN, DNB = psum[7], psb[7]
                    for h in range(8):
                        kb_ = kbuf[h % 2]
                        vb_ = vbuf[h % 2]
                        KB, VB = B("kbuf", h % 2), B("vbuf", h % 2)
                        for m in range(nblk):
                            jm, slot = _owner(m)
                            rnd, bs = slot // 2, slot % 2
                            srck = eko[rnd][jm, :].rearrange("(h d t) -> h d t", h=8, d=128)
                            srcv = evo[rnd][jm, :].rearrange("(t n) -> t n", n=1024)
                            load("sp", kb_[:, m * 256:(m + 1) * 256], srck[h, :, bs * 256:(bs + 1) * 256], KB, reads=[EKO[rnd]])
                            load("sp", vb_[:, 2 * m:2 * m + 2, :],
                                 srcv[bs * 256:(bs + 1) * 256, h * 128:(h + 1) * 128].rearrange("(t p) d -> p t d", p=128),
                                 VB, reads=[EVO[rnd]])
                        load("sp", kb_[:, nblk * 256:nblk * 256 + 512], e1kT[h, :, :], KB, reads=[EKB])
                        load("sp", vb_[:, 2 * nblk:2 * nblk + 4, :],
                             e1V[:, h * 128:(h + 1) * 128].rearrange("(t p) d -> p t d", p=128), VB, reads=[EVB])
                        for sl in range(2):
                            s4 = rd * 2 + sl
                            for qt in range(2):
                                ti = 2 * sl + qt
                                c0 = ti * 128
                                QB = B("qT", ti, h)
                                qap = qT[:, h, c0:c0 + 128]
                                p1, P1B = ps_next()
                                mm(p1[:, 0:16], qap, meansb[:, :, h], True, True, [QB, B("meansb")], [P1B])
                                dve(lambda e, p1=p1, s4=s4: e.tensor_tensor(out=smv[:], in0=p1[:, 0:16],
                                                                            in1=tabs[:, s4 * 16:(s4 + 1) * 16], op=ALU.add),
                                    [P1B, BC], [B("smv")])
                                dve(lambda e: e.max(out=top8[:], in_=smv[:]), [B("smv")], [B("top8")])
                                dve(lambda e: e.tensor_scalar(out=selb[:], in0=smv[:], scalar1=top8[:, 2:3], scalar2=None,
                                                              op0=ALU.is_ge), [B("smv"), B("top8")], [B("selb")])
                                dve(lambda e, s4=s4: e.tensor_tensor(out=selb[:], in0=selb[:],
                                                                     in1=tabs[:, 64 + s4 * 16:64 + (s4 + 1) * 16], op=ALU.mult),
                                    [B("selb"), BC], [B("selb")])
                                dve(lambda e: e.tensor_scalar(out=selbb[:], in0=selb[:], scalar1=1.0, scalar2=1.0e30,
                                                              op0=ALU.subtract, op1=ALU.mult), [B("selb")], [B("selbb")])
                                p2, PB2 = ps_next()
                                pv = p2[:].bitcast(BF16)
                                tr(pv[0:16, 0:128], selbb[:, :], ident_b[:, :], [B("selbb"), BC], [PB2])
                                cp(selT[:], pv[0:16, 0:128], [PB2], [B("selT")])
                                kts_ = []
                                for m in cands[sl]:
                                    for kt in range(2):
                                        kts_.append((kb_[:, m * 256 + kt * 128:m * 256 + (kt + 1) * 128], m, 2 * m + kt, False))
                                for kt in range(qt + 1):
                                    o_ = (nblk + sl) * 256 + kt * 128
                                    kts_.append((kb_[:, o_:o_ + 128], None, 2 * (nblk + sl) + kt, kt == qt))
                                nk = len(kts_)
                                for g0 in range(0, nk, 4):
                                    grp = kts_[g0:g0 + 4]
                                    p3, PB3 = ps_next()
                                    for jx, (kap, em_, vi, tri_) in enumerate(grp):
                                        mm(p3[:, jx * 128:(jx + 1) * 128], kap, qap, True, em_ is None, [KB, QB], [PB3])
                                        if em_ is not None:
                                            mm(p3[:, jx * 128:(jx + 1) * 128], emb[0:16, em_ * 128:(em_ + 1) * 128], selT[:, :],
                                               False, True, [B("selT"), BC], [PB3])
                                    pt2 = ptt[cnt2[0] % 2]
                                    PTB = B("ptt", cnt2[0] % 2)
                                    cnt2[0] += 1
                                    act(pt2[:, 0:len(grp) * 128], p3[:, 0:len(grp) * 128], AF.Exp, [PB3], [PTB], scale=SCALE)
                                    for jx, (kap, em_, vi, tri_) in enumerate(grp):
                                        if tri_:
                                            dve(lambda e, pt2=pt2, jx=jx: e.tensor_tensor(
                                                out=pt2[:, jx * 128:(jx + 1) * 128], in0=pt2[:, jx * 128:(jx + 1) * 128],
                                                in1=tri_b[:, :], op=ALU.mult), [PTB, BC], [PTB])
                                    for jx, (kap, em_, vi, tri_) in enumerate(grp):
                                        first = (g0 + jx == 0)
                                        last = (g0 + jx == nk - 1)
                                        mm(OP[:, 0:128], pt2[:, jx * 128:(jx + 1) * 128], vb_[:, vi, :], first, last, [PTB, VB], [OPB])
                                        mm(DN[:, 0:1], pt2[:, jx * 128:(jx + 1) * 128], ones_b[:, 0:1], first, last, [PTB, BC], [DNB])
                                dve(lambda e: e.reciprocal(out=rec[:], in_=DN[:, 0:1]), [DNB], [B("rec")])
                                dve(lambda e, ti=ti, h=h: e.scalar_tensor_tensor(
                                    out=oat[:], in0=OP[:, 0:128], scalar=rec[:, 0:1], in1=sza[:, ti, h * 128:(h + 1) * 128],
                                    op0=ALU.mult, op1=ALU.mult), [OPB, B("rec"), B("sza", ti)], [B("oat")])
                                p4, PB4 = ps_next()
                                pv4 = p4[:].bitcast(BF16)
                                tr(pv4[:, 0:128], oat[:, :], ident_b[:, :], [B("oat"), BC], [PB4])
                                cp(qT[:, h, c0:c0 + 128], pv4[:, 0:128], [PB4], [QB])

                    if stop == "P4c":
                        raise _Stop()
                    barrier()
                    wav = w_a[l].rearrange("(k p) n -> p k n", p=128)
                    wbv = w_b[l].rearrange("(k p) n -> p k n", p=128)
                    wov = w_out[l].rearrange("(k p) n -> p k n", p=128)
                    for cc in range(4):
                        wsa = wrot[0] % 2
                        wrot[0] += 1
                        wta = wbufD[wsa][:, 0:8 * 512].rearrange("p (k n) -> p k n", k=8)
                        load("pool", wta, wav[:, :, cc * 512:(cc + 1) * 512], wB[wsa])
                        wsb = wrot[0] % 2
                        wrot[0] += 1
                        wtb = wbufD[wsb][:, 0:8 * 512].rearrange("p (k n) -> p k n", k=8)
                        load("pool", wtb, wbv[:, :, cc * 512:(cc + 1) * 512], wB[wsb])
                        for ti, (r0, c0, tsz, smp) in enumerate(tiles):
                            pa, PA = ps_next()
                            for h in range(8):
                                mm(pa[:tsz, :], qT[:, h, c0:c0 + tsz], wta[:, h, :], h == 0, h == 7, [B("qT", ti, h), wB[wsa]], [PA])
                            pb2, PBB = ps_next()
                            for h in range(8):
                                mm(pb2[:tsz, :], obT[:, h, c0:c0 + tsz], wtb[:, h, :], h == 0, h == 7, [B("obT", ti), wB[wsb]], [PBB])
                            gs_ = cnt2[0] % 2
                            cnt2[0] += 1
                            gb_ = gab[gs_]
                            GB = B("gab", gs_)
                            load("sp", gb_[:tsz, :, :],
                                 gscr[c0:c0 + tsz, :].rearrange("t (a c n) -> t a c n", a=2, c=4)[:, :, cc, :], GB,
                                 reads=[B("gscr")])
                            dve(lambda e, pa=pa, gb_=gb_, tsz=tsz: e.tensor_tensor(out=t1[:tsz, :], in0=pa[:tsz, :],
                                                                                  in1=gb_[:tsz, 0, :], op=ALU.mult),
                                [PA, GB], [B("t1")])
                            dve(lambda e, pb2=pb2, gb_=gb_, tsz=tsz: e.tensor_tensor(out=t2[:tsz, :], in0=pb2[:tsz, :],
                                                                                   in1=gb_[:tsz, 1, :], op=ALU.mult),
                                [PBB, GB], [B("t2")])
                            dve(lambda e, tsz=tsz: e.tensor_tensor(out=mst[:tsz, :], in0=t1[:tsz, :], in1=t2[:tsz, :],
                                                                   op=ALU.add), [B("t1"), B("t2")], [B("mst")])
                            p2, PB2 = ps_next()
                            pv = p2[:].bitcast(BF16)
                            for i in range(4):
                                tr(pv[:, i * 128:i * 128 + tsz], mst[:tsz, i * 128:(i + 1) * 128], ident_b[:tsz, :tsz],
                                   [B("mst"), BC], [PB2])
                            cp(xnT[:, cc * 4:(cc + 1) * 4, c0:c0 + tsz],
                               pv[:, 0:512].rearrange("p (h t) -> p h t", h=4)[:, :, 0:tsz], [PB2], [B("xnT", ti)])
                    for cc in range(4):
                        ws = wrot[0] % 2
                        wrot[0] += 1
                        wto = wbufD[ws][:, 0:KC * 512].rearrange("p (k n) -> p k n", k=KC)
                        load("pool", wto, wov[:, :, cc * 512:(cc + 1) * 512], wB[ws])
                        for ti, (r0, c0, tsz, smp) in enumerate(tiles):
                            po_, PO = ps_next()
                            for k in range(KC):
                                mm(po_[:tsz, :], xnT[:, k, c0:c0 + tsz], wto[:, k, :], k == 0, k == KC - 1,
                                   [B("xnT", ti), wB[ws]], [PO])
                            xs_ = cnt2[0] % 2
                            cnt2[0] += 1
                            xr = xres[xs_]
                            XR = B("xres", xs_)
                            XS = B("xscr", "s") if smp else B("xscr", r0)
                            if smp:
                                src = (xs if l == 0 else xscr[1024:1028, :])[:, cc * 512:(cc + 1) * 512]
                                dst = xscr[1024:1028, cc * 512:(cc + 1) * 512]
                            else:
                                src = (xp if l == 0 else xscr)[r0:r0 + tsz, cc * 512:(cc + 1) * 512]
                                dst = xscr[r0:r0 + tsz, cc * 512:(cc + 1) * 512]
                            load("sp", xr[:tsz, :], src, XR, reads=[XS])
                            dve(lambda e, po_=po_, xr=xr, tsz=tsz: e.tensor_tensor(out=xr[:tsz, :], in0=po_[:tsz, :],
                                                                                 in1=xr[:tsz, :], op=ALU.add), [PO, XR], [XR])
                            store("sp", dst, xr[:tsz, :], XR, XS)

            barrier()
            load("sp", fngrep, fng.broadcast_to([128, D]), B("fngrep"))
            fin = [(r0, 128, False) for r0 in range(0, 1024, 128)] + [(1024, 4, True)]
            for fi, (r0, tsz, smp) in enumerate(fin):
                xb_ = xtF
                XB = B("xtF")
                XS = B("xscr", "s") if smp else B("xscr", r0)
                load("sp", xb_[:tsz, :], xscr[r0:r0 + tsz, :], XB, reads=[XS])
                rms_rstd(rstd[:tsz, :], xb_[:tsz, :], D, tsz, [XB], junkF[:tsz, :], B("junkF"))
                dve(lambda e, xb_=xb_, tsz=tsz: e.scalar_tensor_tensor(out=xb_[:tsz, :], in0=xb_[:tsz, :], scalar=rstd[:tsz, :],
                                                                      in1=fngrep[:tsz, :], op0=ALU.mult, op1=ALU.mult),
                    [XB, B("rstd"), B("fngrep")], [XB])
                if smp:
                    store("sp", ys[:, :], xb_[:tsz, :], XB, B("ys"))
                else:
                    store("sp", yp[r0:r0 + tsz, :], xb_[:tsz, :], XB, B("yp"))
        except _Stop:
            pass
        fextra = set(g.last[e_] for e_ in ("pe", "act", "dve") if g.last[e_] is not None)
        for t_ in g.tracks:
            if t_.last_op is not None:
                fextra.add(t_.last_op)
        for e_ in Graph.ENGS:
            g.add(e_, None, extra=fextra)
        g.add("sp", None, reads=[B("yp"), B("ys"), B("kp", 0), B("kp", 1), B("vp", 0), B("vp", 1), B("gp"),
                                 B("ks", 0), B("ks", 1), B("vs", 0), B("vs", 1), B("gs")])
        g.emit()
    return nc


_NC_CACHE = {}
_DEV = {"npool": NPOOL, "stop": None}


def _blocks(j):
    return [j, 7 - j, 8 + j, 15 - j]


def _tab(j):
    t = np.zeros((128, 160), np.float32)
    blks = _blocks(j)
    for s_ in range(4):
        g_ = blks[s_]
        t[:, s_ * 16:(s_ + 1) * 16] = np.where(np.arange(16) < g_, 0.0, NEG)
        t[:, 64 + s_ * 16:64 + (s_ + 1) * 16] = (np.arange(16) < g_).astype(np.float32)
    for rd in range(2):
        for sl in range(2):
            g_ = blks[rd * 2 + sl]
            for gq in range(8):
                t[:, 128 + (rd * 2 + sl) * 8 + gq] = 1.0 if (rd * 8 + gq) == g_ else 0.0
    return t


def kernel(x_prompt, x_sample, cache_k, cache_v, state_gla, page_table, norm_g, w_in, w_gate2, b_gate,
           gla_norm_g, w_branch_a, w_branch_b, b_merge, w_out, final_norm_g):
    f = lambda a: np.ascontiguousarray(np.asarray(a, dtype=np.float32))
    x_prompt, x_sample = f(x_prompt), f(x_sample)
    ckf = f(cache_k).reshape(2 * _DEV["npool"] * 128, 1024)
    cvf = f(cache_v).reshape(2 * _DEV["npool"] * 128, 1024)
    state_gla = f(state_gla)
    page_table = np.ascontiguousarray(np.asarray(page_table, dtype=np.int32))
    cst, emc, bdc = _consts()
    shared = {
        "ck": ckf, "cv": cvf, "norm_g": f(norm_g), "w_in": f(w_in), "w_gate2": f(w_gate2), "b_gate": f(b_gate),
        "gla_g": f(gla_norm_g).reshape(2, 1024), "w_a": f(w_branch_a), "w_b": f(w_branch_b), "b_merge": f(b_merge),
        "w_out": f(w_out), "fng": f(final_norm_g).reshape(1, D), "cst": cst, "emc": emc, "bdc": bdc,
    }
    in_maps = []
    for c in range(8):
        b, j = c // 4, c % 4
        rows = np.concatenate([np.arange(m * 256, (m + 1) * 256) for m in _blocks(j)])
        d = dict(shared)
        d["xp"] = np.ascontiguousarray(x_prompt[b][rows])
        d["xs"] = np.ascontiguousarray(x_sample[c])
        d["sg"] = np.ascontiguousarray(state_gla[:, c])
        d["pt"] = np.ascontiguousarray(page_table[c].reshape(1, 128))
        d["meta"] = np.full((1, 64), c, np.int32)
        d["tab"] = _tab(j)
        in_maps.append(d)
    if "nc" not in _NC_CACHE:
        _NC_CACHE["nc"] = build(npool=_DEV["npool"], stop=_DEV["stop"])
    res = run_bass_kernel_spmd(_NC_CACHE["nc"], in_maps, core_ids=list(range(8)))
    R = res.results
    y_prompt = np.zeros((2, 4096, D), np.float32)
    k_prompt = np.zeros((2, 2, 4096, 8, 128), np.float32)
    v_prompt = np.zeros((2, 2, 4096, 8, 128), np.float32)
    gla_prompt = np.zeros((2, 2, 4, 128, 256), np.float32)
    y_sample = np.zeros((8, 4, D), np.float32)
    k_sample = np.zeros((2, 8, 4, 8, 128), np.float32)
    v_sample = np.zeros((2, 8, 4, 8, 128), np.float32)
    gla_sample = np.zeros((2, 8, 4, 128, 256), np.float32)
    for c in range(8):
        b, j = c // 4, c % 4
        rows = np.concatenate([np.arange(m * 256, (m + 1) * 256) for m in _blocks(j)])
        r = R[c]
        y_prompt[b, rows] = r["yp"]
        k_prompt[:, b, rows] = np.asarray(r["kp"]).reshape(2, 1024, 8, 128)
        v_prompt[:, b, rows] = np.asarray(r["vp"]).reshape(2, 1024, 8, 128)
        if j == 0:
            gla_prompt[:, b] = r["gp"]
        y_sample[c] = r["ys"]
        k_sample[:, c] = np.asarray(r["ks"]).reshape(2, 4, 8, 128)
        v_sample[:, c] = np.asarray(r["vs"]).reshape(2, 4, 8, 128)
        gla_sample[:, c] = r["gs"]
    return (y_prompt, y_sample, k_prompt, v_prompt, gla_prompt, k_sample, v_sample, gla_sample)
```

```python
from contextlib import ExitStack
import numpy as np
import ml_dtypes
import concourse.bass as bass
import concourse.mybir as mybir
from concourse.bass_utils import run_bass_kernel_spmd

F32 = mybir.dt.float32
BF16 = mybir.dt.bfloat16
I32 = mybir.dt.int32
U32 = mybir.dt.uint32
AF = mybir.ActivationFunctionType
ALU = mybir.AluOpType
AX = mybir.AxisListType

SAME_ENGINE_SYNC = True

D = 2048
KC = 16
NIN = 11280
NPOOL = 1280
NEG = -1.0e30
EPS = 1e-6
SCALE = 128 ** -0.5


class Buf:
    __slots__ = ("name", "last_w", "readers", "lt", "st")

    def __init__(self, name):
        self.name = name
        self.last_w = None
        self.readers = []
        self.lt = None
        self.st = None


class Track:
    __slots__ = ("sem", "count", "inc", "last_op")

    def __init__(self, sem, inc=16):
        self.sem = sem
        self.count = 0
        self.inc = inc
        self.last_op = None


class Op:
    __slots__ = ("eng", "fn", "deps", "track", "ordinal", "tick", "signal", "idx", "waits")


class Graph:
    ENGS = ("pe", "act", "dve", "pool", "sp")

    def __init__(self, nc, stack):
        self.nc = nc
        self.stack = stack
        self.ops = []
        self.esem = {e: stack.enter_context(nc.semaphore("es_" + e)) for e in self.ENGS}
        self.nsem = 5
        self.last = {e: None for e in self.ENGS}
        self.tracks = []
        self.rkeys = []

    def track(self, inc=16):
        self.nsem += 1
        t = Track(self.stack.enter_context(self.nc.semaphore("trk%d" % self.nsem)), inc)
        self.tracks.append(t)
        return t

    def add(self, eng, fn, reads=(), writes=(), dma=None, extra=()):
        op = Op()
        op.eng = eng
        op.fn = fn
        op.track = dma
        op.idx = len(self.ops)
        op.signal = False
        op.tick = None
        op.ordinal = None
        deps = set(extra)
        for b in reads:
            if b.last_w is not None:
                deps.add(b.last_w)
        for b in writes:
            if b.last_w is not None:
                deps.add(b.last_w)
            for r in b.readers:
                deps.add(r)
        op.deps = deps
        rkey = eng if dma is None else ("t", id(dma))
        for b in reads:
            b.readers = [r for r in b.readers if self.rkeys[r] != rkey]
            b.readers.append(op.idx)
        self.rkeys.append(rkey)
        for b in writes:
            b.last_w = op.idx
            b.readers = []
        if dma is not None:
            dma.count += 1
            op.ordinal = dma.count
            dma.last_op = op.idx
        self.ops.append(op)
        if fn is not None:
            self.last[eng] = op.idx
        return op

    def emit(self):
        ops = self.ops

        def skip(d, op):
            return d.eng == op.eng and (d.eng == "pe" or not SAME_ENGINE_SYNC)

        for op in ops:
            for j in op.deps:
                d = ops[j]
                if d.track is None and not skip(d, op):
                    d.signal = True
        cnt = {e: 0 for e in self.ENGS}
        for op in ops:
            if op.track is None and op.signal:
                cnt[op.eng] += 1
                op.tick = cnt[op.eng]
        seen = {e: {} for e in self.ENGS}
        for op in ops:
            w = {}
            for j in op.deps:
                d = ops[j]
                if d.track is not None:
                    key, val = d.track.sem, d.track.inc * d.ordinal
                else:
                    if skip(d, op):
                        continue
                    key, val = self.esem[d.eng], d.tick
                if w.get(key, 0) < val:
                    w[key] = val
            sd = seen[op.eng]
            op.waits = []
            for key, val in w.items():
                if sd.get(key, 0) < val:
                    sd[key] = val
                    op.waits.append((key, val))
        by_eng = {e: [op for op in ops if op.eng == e] for e in self.ENGS}
        esem = self.esem

        def run(engname, eng):
            for op in by_eng[engname]:
                for (sem, val) in op.waits:
                    eng.wait_ge(sem, val)
                if op.fn is None:
                    continue
                ins = op.fn(eng)
                if op.track is not None:
                    if op.track.inc == 16:
                        ins.then_inc(op.track.sem, 16)
                    else:
                        ins.then_inc(op.track.sem)
                elif op.signal:
                    ins.then_inc(esem[engname], 1)

        with self.nc.Block() as block:
            @block.tensor
            def _(e):
                run("pe", e)

            @block.scalar
            def _(e):
                run("act", e)

            @block.vector
            def _(e):
                run("dve", e)

            @block.gpsimd
            def _(e):
                run("pool", e)

            @block.sync
            def _(e):
                run("sp", e)


def _consts():
    c = np.zeros((128, 1024), np.float32)
    r = np.arange(128)
    c[:, 0:128] = np.eye(128)
    c[:, 128:256] = (r[:, None] <= r[None, :]).astype(np.float32)
    c[:, 256:384] = c[:, 128:256] * (-1.0 / 16.0)
    c[:, 384:512] = 1.0
    c[:, 512] = r
    for g_ in range(16):
        c[:, 528 + g_ * 16: 528 + (g_ + 1) * 16] = np.where(np.arange(16) < g_, 0.0, NEG)
    for h in range(8):
        for q in range(4):
            c[0:4, 784 + h * 4 + q] = (np.arange(4) <= q).astype(np.float32)
    em = np.zeros((64, 64 * 128), np.float32)
    for m in range(64):
        em[m, m * 128:(m + 1) * 128] = 1.0
    bd = np.zeros((4, 4 * 512), np.float32)
    for q in range(4):
        bd[q, q * 512:(q + 1) * 512] = 1.0
    return c, em, bd


def _owner(m):
    if m < 4:
        return m, 0
    if m < 8:
        return 7 - m, 1
    if m < 12:
        return m - 8, 2
    return 15 - m, 3


class _Stop(Exception):
    pass


def build(npool=NPOOL, stop=None):
    nc = bass.Bass("TRN2", target_bir_lowering=False)

    def din(name, shape, dt=F32):
        return nc.dram_tensor(name, list(shape), dt, kind="ExternalInput").ap()

    def dout(name, shape, dt=F32):
        return nc.dram_tensor(name, list(shape), dt, kind="ExternalOutput").ap()

    def dscr(name, shape, dt=F32):
        return nc.dram_tensor(name, list(shape), dt).ap()

    xp = din("xp", [1024, D])
    xs = din("xs", [4, D])
    ck = din("ck", [2 * npool * 128, 1024])
    cv = din("cv", [2 * npool * 128, 1024])
    sg = din("sg", [2, 4, 128, 256])
    ptin = din("pt", [1, 128], I32)
    meta = din("meta", [1, 64], I32)
    norm_g = din("norm_g", [2, D])
    w_in = din("w_in", [2, D, NIN])
    w_gate2 = din("w_gate2", [2, 16, 512])
    b_gate = din("b_gate", [2, 512])
    gla_g = din("gla_g", [2, 1024])
    w_a = din("w_a", [2, 1024, D])
    w_b = din("w_b", [2, 1024, D])
    b_merge = din("b_merge", [2, 4096])
    w_out = din("w_out", [2, D, D])
    fng = din("fng", [1, D])
    cst = din("cst", [128, 1024])
    emc = din("emc", [64, 64 * 128])
    bdc = din("bdc", [4, 2048])
    tab = din("tab", [128, 160])

    yp = dout("yp", [1024, D])
    ys = dout("ys", [4, D])
    kp = dout("kp", [2, 1024, 1024])
    vp = dout("vp", [2, 1024, 1024])
    gp = dout("gp", [2, 4, 128, 256])
    ks = dout("ks", [2, 4, 1024])
    vs = dout("vs", [2, 4, 1024])
    gs = dout("gs", [2, 4, 128, 256])

    xscr = dscr("xscr", [1028, D])
    gscr = dscr("gscr", [516, 4096], BF16)
    oscr = dscr("oscr", [4, 1024])
    EKN = 8 * 128 * 512
    EVN = 512 * 1024
    E2S = 128 * 4 * 256
    ESN = 2 * E2S
    EMN = 2 * 512 + 2 * 1024
    LR = [(l, r) for l in range(2) for r in range(2)]
    ekin = {lr: dscr("ekin%d%d" % lr, [EKN], BF16) for lr in LR}
    ekout = {lr: dscr("ekout%d%d" % lr, [4 * EKN], BF16) for lr in LR}
    evin = {lr: dscr("evin%d%d" % lr, [EVN], BF16) for lr in LR}
    evout = {lr: dscr("evout%d%d" % lr, [4 * EVN], BF16) for lr in LR}
    esin = {lr: dscr("esin%d%d" % lr, [ESN]) for lr in LR}
    esout = {lr: dscr("esout%d%d" % lr, [4 * ESN]) for lr in LR}
    emin = {lr: dscr("emin%d%d" % lr, [EMN]) for lr in LR}
    emout = {lr: dscr("emout%d%d" % lr, [4 * EMN]) for lr in LR}

    with ExitStack() as st:
        g = Graph(nc, st)

        TOTAL = (nc.sbuf_bytes_remaining // 64) * 64 - 64
        big = st.enter_context(nc.sbuf_tensor("big", [128, TOTAL // 2], BF16))

        class Reg:
            def __init__(self, base, size):
                self.base, self.size, self.cur = base, size, base

            def reset(self):
                self.cur = self.base

        PSZ, RSZ = 47104, 81920
        RP = Reg(0, PSZ)
        RR = Reg(PSZ, RSZ)
        RX = Reg(PSZ + RSZ, TOTAL - PSZ - RSZ)
        DTS = {F32: 4, BF16: 2, I32: 4, U32: 4}

        def sb(name, shape, dt, reg=None):
            reg = reg or RP
            n = 1
            for d_ in shape[1:]:
                n *= d_
            nb = n * DTS[dt]
            nb_al = (nb + 31) // 32 * 32
            off = reg.cur
            reg.cur += nb_al
            assert reg.cur <= reg.base + reg.size, (name, reg.cur, reg.base + reg.size)
            ap = big[0:shape[0], off // 2:(off + nb) // 2]
            if dt != BF16:
                ap = ap.bitcast(dt)
            if len(shape) == 3:
                ap = ap.rearrange("p (a b) -> p a b", a=shape[1])
            return ap


        bufs = {}

        def B(*key):
            if key not in bufs:
                bufs[key] = Buf(str(key))
            return bufs[key]

        wlc = [0]

        def load(eng, out, in_, dst, reads=()):
            if dst.lt is None:
                dst.lt = {}
            if eng not in dst.lt:
                dst.lt[eng] = g.track()
            if _DEV.get("nowl") and eng == "pool" and dst.name.startswith("('wbuf'"):
                wlc[0] += 1
                if wlc[0] > 2:
                    return
            g.add(eng, lambda e: e.dma_start(out=out, in_=in_, allow_slow_non_contiguous=True), reads=list(reads), writes=[dst], dma=dst.lt[eng])

        def store(eng, out, in_, src, dst):
            if src.st is None:
                src.st = {}
            if eng not in src.st:
                src.st[eng] = g.track()
            g.add(eng, lambda e: e.dma_start(out=out, in_=in_, allow_slow_non_contiguous=True), reads=[src], writes=[dst], dma=src.st[eng])

        def mm(out, lhsT, rhs, start, stop, reads, writes):
            g.add("pe", lambda e: e.matmul(out, lhsT, rhs, start=start, stop=stop), reads=reads, writes=writes)

        def tr(out, in_, ident_ap, reads, writes):
            g.add("pe", lambda e: e.transpose(out, in_, ident_ap), reads=reads, writes=writes)

        def act(out, in_, func, reads, writes, bias=None, scale=None, accum=None):
            kw = {}
            if bias is not None:
                kw["bias"] = bias
            if scale is not None:
                kw["scale"] = scale
            if accum is not None:
                kw["accum_out"] = accum
            g.add("act", lambda e: e.activation(out=out, in_=in_, func=func, **kw), reads=reads, writes=writes)

        def dve(fn, reads, writes, eng="dve"):
            g.add(eng, fn, reads=reads, writes=writes)

        def cp(out, in_, reads, writes, eng="dve"):
            g.add(eng, lambda e: e.tensor_copy(out=out, in_=in_), reads=reads, writes=writes)

        def barrier():
            extra = set(g.last[e_] for e_ in ("pe", "act", "dve") if g.last[e_] is not None)
            for t_ in g.tracks:
                if t_.inc == 16 and t_.last_op is not None:
                    extra.add(t_.last_op)
            for e_ in Graph.ENGS:
                g.add(e_, None, extra=extra)

        psum = [st.enter_context(nc.psum_tensor("ps%d" % i, [128, 512], F32)) for i in range(8)]
        psb = [B("ps", i) for i in range(8)]
        rot = [0]

        def ps_next():
            i = rot[0]
            rot[0] = (rot[0] + 1) % 5
            return psum[i], psb[i]

        cf = sb("cf", [128, 1024], F32)
        cb = sb("cb", [128, 528], BF16)
        emb = sb("emb", [16, 16 * 128], BF16)
        bdb = sb("bdb", [4, 2048], BF16)
        tabs = sb("tabs", [128, 160], F32)
        ident_b = cb[:, 0:128]
        tri_b = cb[:, 128:256]
        ones_b = cb[:, 384:512]
        ident_f = cf[:, 0:128]
        trineg_f = cf[:, 256:384]
        iota_f = cf[:, 512:513]
        BC = B("const")
        load("sp", cf, cst[:, :], BC)
        load("pool", cb, cst[:, 0:528], BC)
        load("pool", emb, emc[0:16, 0:2048], BC)
        load("pool", bdb, bdc[:, :], BC)
        load("sp", tabs, tab[:, :], BC)
        gcol = sb("gcol", [128, 2, KC], F32)
        load("sp", gcol, norm_g.rearrange("l (k p) -> p l k", p=128), BC)
        gnrep = sb("gnrep", [128, 1024], F32)
        wg2 = sb("wg2", [17, 2, 512], F32)
        load("sp", wg2[0:16, :, :], w_gate2.rearrange("l r n -> r l n"), BC)
        load("sp", wg2[16:17, :, :], b_gate.rearrange("(o l) n -> o l n", o=1), BC)
        pidx = sb("pidx", [128, 2, 128], I32)
        S_run = sb("S_run", [128, 4, 256], F32)
        S_st = [sb("S_st%d" % i, [128, 4, 256], F32) for i in range(2)]
        S_bf = sb("S_bf", [128, 4, 256], BF16)
        S_s = sb("S_s", [128, 4, 256], F32)
        meansf = sb("meansf", [128, 16, 8], F32)
        meansb = sb("meansb", [128, 16, 8], BF16)
        mown = sb("mown", [128, 2, 8], F32)
        edec = sb("edec", [128, 5, 4], F32)
        ssq = sb("ssq", [128, 1], F32)
        rstd = sb("rstd", [128, 1], F32)
        epsc = sb("epsc", [128, 1], F32)
        dve(lambda e: e.memset(epsc, EPS), [], [B("ssq")])
        rec = sb("rec", [128, 1], F32)
        atot = sb("atot", [128, 4], F32)
        ssq4 = sb("ssq4", [128, 4], F32)
        smv4 = [sb("smv%d" % i, [128, 16], F32) for i in range(4)]
        top84 = [sb("top8%d" % i, [128, 8], F32) for i in range(4)]
        selb4 = [sb("selb%d" % i, [128, 16], F32) for i in range(4)]
        selbb4 = [sb("selbb%d" % i, [128, 16], BF16) for i in range(4)]
        selT4 = [sb("selT%d" % i, [16, 128], BF16) for i in range(4)]
        rec2 = [sb("rec2_%d" % i, [128, 1], F32) for i in range(2)]

        NT = 516
        xnT = sb("xnT", [128, KC, NT], BF16, RR)
        qT = sb("qT", [128, 8, NT], BF16, RR)
        obT = sb("obT", [128, 8, NT], BF16, RR)
        qi = sb("qi", [128, 4, NT], BF16, RR)
        ki = sb("ki", [128, 4, NT], BF16, RR)
        kd = sb("kd", [128, 5, 512], BF16, RR)
        vbr = sb("vbr", [128, 5, 1024], BF16, RR)
        sza = sb("sza", [128, 5, 1024], BF16, RR)
        szb = sb("szb", [128, 5, 1024], BF16, RR)
        alrT = sb("alrT", [17, NT], F32, RR)
        ksT = sb("ksT", [128, 8, 4], BF16, RR)
        dve(lambda e: e.memset(alrT, 1.0), [], [B("alrT", ti_) for ti_ in range(5)])
        vsr = sb("vsr", [4, 1024], BF16, RR)

        RX.reset()
        wbuf = [sb("wbuf%d" % i, [128, KC * 512], BF16, RX) for i in range(2)]
        wB = [B("wbuf", i) for i in range(2)]
        wrot = [0]
        qbT = sb("qbT", [128, 4, NT], BF16, RX)
        kbT = sb("kbT", [128, 4, NT], BF16, RX)
        xt = [sb("xt0", [128, D], F32, RX)] * 2
        xnb = sb("xnb", [128, D], BF16, RX)
        junk = xnb
        ef32 = [sb("ef32_%d" % i, [128, 512], F32, RX) for i in range(2)]
        eb16 = [sb("eb16_%d" % i, [128, 512], BF16, RX) for i in range(2)]
        ktst = [sb("ktst%d" % i, [128, 4, 128], BF16, RX) for i in range(2)]
        bms = [sb("bms%d" % i, [1, 512], BF16, RX) for i in range(2)]
        gtile = sb("gtile", [128, 512], F32, RX)
        bTt = sb("bTt", [128, 4, 128], F32, RX)
        eb_ = sb("eb_", [128, 4, 128], F32, RX)
        enb = sb("enb", [128, 4, 128], F32, RX)
        ekd = sb("ekd", [128, 4, 128], F32, RX)
        kdT = sb("kdT", [128, 4, 128], BF16, RX)
        Sl = sb("Sl", [128, 4, 256], F32, RX)
        ecnt = [0]
        cnt2 = [0]
        ocnt = [0]
        pend2 = []
        pti = sb("pti", [128, 128], I32, RX)
        ptf = sb("ptf", [128, 2, 128], F32, RX)
        RX.reset()
        kpgb = [sb("kpg%d" % i, [128, 2, 1024], BF16, RX) for i in range(2)]
        kTs = [sb("kTs%d" % i, [128, 16, 128], BF16, RX) for i in range(2)]
        PallT = sb("PallT", [128, 128, 32], BF16, RX)
        means_s = sb("means_s", [128, 512], BF16, RX)
        sms = sb("sms", [4, 512], F32, RX)
        top8s = sb("top8s", [4, 64], F32, RX)
        sel01b = sb("sel01b", [4, 512], BF16, RX)
        rhsbd = sb("rhsbd", [4, 4, 512], BF16, RX)
        maskrep = sb("maskrep", [128, 4, 512], BF16, RX)
        pown = sb("pown", [4, 32], F32, RX)
        pownb = sb("pownb", [4, 32], BF16, RX)
        osb = sb("osb", [32, 1024], F32, RX)
        ost = sb("ost", [4, 1024], F32, RX)
        oasb = sb("oasb", [4, 1024], BF16, RX)
        RX.reset()
        slb = [sb("slb%d" % i, [128, 4, 256], F32, RX) for i in range(2)]
        atg = [sb("atg%d" % i, [128, 4], F32, RX) for i in range(2)]
        att = [sb("att%d" % i, [128, 128], BF16, RX) for i in range(2)]
        otmp = sb("otmp", [128, 256], F32, RX)
        obst = sb("obst", [128, 1024], BF16, RX)
        junk2 = sb("junk2", [128, 256], BF16, RX)
        kbuf = [sb("kbuf%d" % i, [128, 17 * 256], BF16, RX) for i in range(2)]
        vbuf = [sb("vbuf%d" % i, [128, 34, 129], BF16, RX) for i in range(2)]
        ptt = [sb("ptt%d" % i, [128, 512], BF16, RX) for i in range(3)]
        oat2 = [sb("oat%d" % i, [128, 128], BF16, RX) for i in range(2)]
        RX.reset()
        wbufD = [sb("wbufD%d" % i, [128, KC * 512], BF16, RX) for i in range(2)]
        gab = [sb("gab%d" % i, [128, 2, 512], BF16, RX) for i in range(2)]
        mst2 = [sb("mst%d" % i, [128, 512], BF16, RX) for i in range(2)]
        t1 = sb("t1", [128, 512], F32, RX)
        t2 = sb("t2", [128, 512], F32, RX)
        xres = [sb("xres%d" % i, [128, 512], F32, RX) for i in range(2)]
        RX.reset()
        xtF = sb("xtF", [128, D], F32, RX)
        fngrep = sb("fngrep", [128, D], F32, RX)
        junkF = sb("junkF", [128, D], BF16, RX)

        BPI = B("pidx")
        load("sp", pti, ptin.broadcast_to([128, 128]), BPI)
        cp(ptf[:, 0, :], pti, [BPI], [BPI])
        dve(lambda e: e.tensor_scalar(out=ptf[:, 0, :], in0=ptf[:, 0, :], scalar1=128.0, scalar2=iota_f,
                                      op0=ALU.mult, op1=ALU.add), [BC, BPI], [BPI])
        dve(lambda e: e.tensor_scalar(out=ptf[:, 1, :], in0=ptf[:, 0, :], scalar1=float(npool * 128), scalar2=None,
                                      op0=ALU.add), [BC, BPI], [BPI])
        cp(pidx, ptf, [BPI], [BPI])

        ONE = B("one")

        def rms_rstd(dst_rstd, src_ap, n, tsz, reads, tmpjunk, jb):
            act(tmpjunk, src_ap, AF.Square, reads, [jb, B("ssq")], accum=ssq[:tsz, :])
            act(dst_rstd, ssq[:tsz, :], AF.Sqrt, [B("ssq")], [B("rstd")], scale=1.0 / n, bias=epsc[:tsz, :])
            dve(lambda e: e.reciprocal(out=dst_rstd, in_=dst_rstd), [B("rstd")], [B("rstd")])

        try:
            if stop == "INIT":
                raise _Stop()
            for l in range(2):
                dve(lambda e: e.memset(S_run[:], 0.0), [], [B("S_run")])
                dve(lambda e: e.memset(meansf, 0.0), [], [B("meansf")])
                load("sp", gnrep, gla_g[l:l + 1, :].broadcast_to([128, 1024]), B("gnrep"))
                load("sp", S_s[:], sg[l].rearrange("h k v -> k h v"), B("S_s"))
                for rd in range(2):
                    tiles = [(rd * 512 + i * 128, i * 128, 128, False) for i in range(4)]
                    if rd == 0:
                        tiles.append((0, 512, 4, True))
                    nt = len(tiles)
                    xsrc = (lambda r0, n: xp[r0:r0 + n, :]) if l == 0 else (lambda r0, n: xscr[r0:r0 + n, :])
                    xssrc = xs[:, :] if l == 0 else xscr[1024:1028, :]

                    barrier()
                    for ti, (r0, c0, tsz, smp) in enumerate(tiles):
                        xb_ = xt[ti % 2]
                        XB = B("xt", 0)
                        load("sp", xb_[:tsz, :], xssrc if smp else xsrc(r0, tsz), XB,
                             reads=([B("xscr", "s") if smp else B("xscr", r0)] if l == 1 else []))
                        rms_rstd(rstd[:tsz, :], xb_[:tsz, :], D, tsz, [XB], junk[:tsz, :], B("xnb"))
                        act(xnb[:tsz, :], xb_[:tsz, :], AF.Copy, [XB, B("rstd")], [B("xnb")], scale=rstd[:tsz, :])
                        for half in range(2):
                            pt_, PB = ps_next()
                            pv = pt_[:].bitcast(BF16)
                            for kk in range(8):
                                k = half * 8 + kk
                                tr(pv[:, kk * 128: kk * 128 + tsz], xnb[:tsz, k * 128:(k + 1) * 128],
                                   ident_b[:tsz, :tsz], [B("xnb"), BC], [PB])
                            for kk in range(8):
                                k = half * 8 + kk
                                if half == 0:
                                    act(xnT[:, k, c0:c0 + tsz], pv[:, kk * 128: kk * 128 + tsz], AF.Copy,
                                        [PB, BC], [B("xnT", ti)], scale=gcol[:, l, k:k + 1])
                                else:
                                    dve(lambda e, k=k, kk=kk, pv=pv, c0=c0, tsz=tsz, l=l: e.tensor_scalar(
                                        out=xnT[:, k, c0:c0 + tsz], in0=pv[:, kk * 128: kk * 128 + tsz],
                                        scalar1=gcol[:, l, k:k + 1], scalar2=None, op0=ALU.mult),
                                        [PB, BC], [B("xnT", ti)])

                    if stop == "P1":
                        raise _Stop()
                    chunks = []
                    for i in range(2):
                        chunks.append((i * 512, 512, "q", i))
                    for i in range(2):
                        chunks.append((1024 + i * 512, 512, "k", i))
                    for i in range(2):
                        chunks.append((2048 + i * 512, 512, "v", i))
                    for i in range(2):
                        chunks.append((3072 + i * 512, 512, "za", i))
                    chunks.append((4096, 512, "qb", 0))
                    chunks.append((4608, 512, "kb", 0))
                    for i in range(2):
                        chunks.append((5120 + i * 512, 512, "vb", i))
                    for i in range(2):
                        chunks.append((6144 + i * 512, 512, "zb", i))
                    chunks.append((7168, 16, "alr", 0))
                    for i in range(8):
                        chunks.append((7184 + i * 512, 512, "g", i))
                    wv = w_in[l].rearrange("(k p) n -> p k n", p=128)
                    e1kT = ekin[(l, rd)].rearrange("(h d t) -> h d t", h=8, d=128)
                    e1V = evin[(l, rd)].rearrange("(t n) -> t n", n=1024)
                    EKB = B("ekin", l, rd)
                    EVB = B("evin", l, rd)
                    for (col0, w, kind, sub) in chunks:
                        if _DEV.get("kinds") and kind not in _DEV["kinds"]:
                            continue
                        ws = wrot[0] % 2
                        wrot[0] += 1
                        wt = wbuf[ws][:, 0:KC * w].rearrange("p (k n) -> p k n", k=KC)
                        load("pool", wt, wv[:, :, col0:col0 + w], wB[ws])
                        if kind == "g":
                            load("pool", bms[sub % 2], b_merge[l:l + 1, sub * 512:(sub + 1) * 512], B("bms", sub % 2))
                        for ti, (r0, c0, tsz, smp) in enumerate(tiles):
                            pt_, PB = ps_next()
                            po = pt_[:tsz, 0:w]
                            for k in range(KC):
                                mm(po, xnT[:, k, c0:c0 + tsz], wt[:, k, :], k == 0, (k == KC - 1) and kind != "g",
                                   [B("xnT", ti), wB[ws]], [PB])
                            if kind == "g":
                                mm(po, ones_b[0:1, :tsz], bms[sub % 2][0:1, :], False, True,
                                   [BC, B("bms", sub % 2)], [PB])
                            while pend2:
                                pend2.pop(0)()
                            es = ecnt[0] % 2
                            ecnt[0] += 1
                            EF, EB = B("ef32", es), B("eb16", es)
                            f32t, b16t = ef32[es], eb16[es]
                            if kind in ("q", "qb", "kb"):
                                act(b16t[:tsz, :], po, AF.Copy, [PB], [EB])
                                if kind == "q":
                                    dstap = qT[:, sub * 4:(sub + 1) * 4, c0:c0 + tsz]
                                    dB = [B("qT", ti, sub * 4 + hh) for hh in range(4)]
                                elif kind == "qb":
                                    dstap = qbT[:, :, c0:c0 + tsz]
                                    dB = [B("qbT", ti)]
                                else:
                                    dstap = kbT[:, :, c0:c0 + tsz]
                                    dB = [B("kbT", ti)]

                                def fin_tr(b16t=b16t, tsz=tsz, EB=EB, dstap=dstap, dB=dB):
                                    p2, PB2 = ps_next()
                                    pv = p2[:].bitcast(BF16)
                                    for hh in range(4):
                                        tr(pv[:, hh * 128: hh * 128 + tsz], b16t[:tsz, hh * 128:(hh + 1) * 128],
                                           ident_b[:tsz, :tsz], [EB, BC], [PB2])
                                    pv4 = pv[:, 0:512].rearrange("p (h t) -> p h t", h=4)[:, :, 0:tsz]
                                    cp(dstap, pv4, [PB2], dB)
                                pend2.append(fin_tr)
                            elif kind == "k":
                                cp(f32t[:tsz, :], po, [PB], [EF])
                                act(b16t[:tsz, :], f32t[:tsz, :], AF.Copy, [EF], [EB])
                                if smp:
                                    store("sp", ks[l][:, sub * 512:(sub + 1) * 512], f32t[:tsz, :], EF, B("ks", l))
                                else:
                                    store("sp", kp[l][r0:r0 + tsz, sub * 512:(sub + 1) * 512], f32t[:tsz, :], EF,
                                          B("kp", l))
                                p2, PB2 = ps_next()
                                pv = p2[:].bitcast(BF16)
                                for hh in range(4):
                                    tr(pv[:, hh * 128: hh * 128 + tsz], b16t[:tsz, hh * 128:(hh + 1) * 128],
                                       ident_b[:tsz, :tsz], [EB, BC], [PB2])
                                pv4 = pv[:, 0:512].rearrange("p (h t) -> p h t", h=4)[:, :, 0:tsz]
                                if smp:
                                    cp(ksT[:, sub * 4:(sub + 1) * 4, :], pv4, [PB2], [B("ksT")])
                                else:
                                    kt_ = ktst[es]
                                    KT = B("ktst", es)
                                    cp(kt_[:], pv4, [PB2], [KT])
                                    store("sp", e1kT[sub * 4:(sub + 1) * 4, :, c0:c0 + 128].rearrange("h d t -> d h t"),
                                          kt_[:], KT, EKB)
                                    bs = ti // 2
                                    if ti % 2 == 0:
                                        dve(lambda e, kt_=kt_, sub=sub, bs=bs: e.tensor_reduce(
                                            out=mown[:, bs, sub * 4:(sub + 1) * 4], in_=kt_[:], axis=AX.X, op=ALU.add),
                                            [KT], [B("mown")])
                                    else:
                                        dve(lambda e, kt_=kt_: e.tensor_reduce(
                                            out=ssq4[:], in_=kt_[:], axis=AX.X, op=ALU.add), [KT], [B("ssq4")])
                                        dve(lambda e, sub=sub, bs=bs: e.tensor_tensor(
                                            out=mown[:, bs, sub * 4:(sub + 1) * 4], in0=mown[:, bs, sub * 4:(sub + 1) * 4],
                                            in1=ssq4[:], op=ALU.add), [B("ssq4"), B("mown")], [B("mown")])
                            elif kind == "v":
                                cp(f32t[:tsz, :], po, [PB], [EF])
                                if smp:
                                    act(vsr[:, sub * 512:(sub + 1) * 512], f32t[:tsz, :], AF.Copy, [EF], [B("vsr")])
                                    if not _DEV.get("no_vs"):
                                        store("sp", vs[l][:, sub * 512:(sub + 1) * 512], f32t[:tsz, :], EF, B("vs", l))
                                else:
                                    act(b16t[:tsz, :], f32t[:tsz, :], AF.Copy, [EF], [EB])
                                    if not _DEV.get("no_vp"):
                                        store("sp", vp[l][r0:r0 + tsz, sub * 512:(sub + 1) * 512], f32t[:tsz, :], EF,
                                              B("vp", l))
                                    if not _DEV.get("no_e1"):
                                        store("sp", e1V[c0:c0 + tsz, sub * 512:(sub + 1) * 512], b16t[:tsz, :], EB, EVB)
                            elif kind == "za":
                                act(sza[:tsz, ti, sub * 512:(sub + 1) * 512], po, AF.Silu, [PB], [B("sza", ti)])
                            elif kind == "zb":
                                act(szb[:tsz, ti, sub * 512:(sub + 1) * 512], po, AF.Silu, [PB], [B("szb", ti)])
                            elif kind == "vb":
                                act(vbr[:tsz, ti, sub * 512:(sub + 1) * 512], po, AF.Copy, [PB], [B("vbr", ti)])
                            elif kind == "alr":
                                cp(f32t[:tsz, 0:16], po, [PB], [EF])
                                p2, PB2 = ps_next()
                                tr(p2[0:16, 0:tsz], f32t[:tsz, 0:16], ident_f[:tsz, :tsz], [EF, BC], [PB2])
                                cp(alrT[0:16, c0:c0 + tsz], p2[0:16, 0:tsz], [PB2], [B("alrT", ti)])
                            elif kind == "g":
                                act(b16t[:tsz, :], po, AF.Sigmoid, [PB], [EB])
                                store("sp", gscr[c0:c0 + tsz, sub * 512:(sub + 1) * 512], b16t[:tsz, :], EB, B("gscr"))

                    while pend2:
                        pend2.pop(0)()
                    if stop == "P2":
                        raise _Stop()
                    e2s = esin[(l, rd)]
                    e2m = emin[(l, rd)]
                    ESB = B("esin", l, rd)
                    EMB = B("emin", l, rd)
                    for ti, (r0, c0, tsz, smp) in enumerate(tiles):
                        if ti == 0 and rd == 0 and l == 0:
                            pass
                        pu, PU = ps_next()
                        mm(pu[:tsz, :], alrT[0:17, c0:c0 + tsz], wg2[0:17, l, :], True, True, [B("alrT", ti), BC], [PU])
                        gt = gtile
                        act(gt[:tsz, :], pu[:tsz, :], AF.Exp, [PU], [B("gt")], scale=-1.0)
                        act(gt[:tsz, :], gt[:tsz, :], AF.Ln, [B("gt")], [B("gt")], bias=1.0)
                        pb_, PBb = ps_next()
                        pb4 = pb_[:, :].rearrange("p (h t) -> p h t", h=4)
                        for h in range(4):
                            mm(pb4[:, h, 0:tsz], gt[:tsz, h * 128:(h + 1) * 128], trineg_f[:tsz, :tsz], True, True,
                               [B("gt"), BC], [PBb])
                        cp(bTt[:, :, 0:tsz], pb4[:, :, 0:tsz], [PBb], [B("bTt")])
                        act(eb_[:, :, 0:tsz], bTt[:, :, 0:tsz], AF.Exp, [B("bTt")], [B("eb")])
                        act(enb[:, :, 0:tsz], bTt[:, :, 0:tsz], AF.Exp, [B("bTt")], [B("enb")], scale=-1.0)
                        for h in range(4):
                            act(ekd[:, h, 0:tsz], bTt[:, h, 0:tsz], AF.Exp, [B("bTt")], [B("ekd")], scale=-1.0,
                                bias=bTt[:, h, tsz - 1:tsz])
                        act(edec[:, ti, :], bTt[:, :, tsz - 1], AF.Exp, [B("bTt")], [B("edec", ti)])
                        dve(lambda e, c0=c0, tsz=tsz: e.scalar_tensor_tensor(
                            out=qi[:, :, c0:c0 + tsz], in0=qbT[:, :, c0:c0 + tsz], scalar=SCALE, in1=eb_[:, :, 0:tsz],
                            op0=ALU.mult, op1=ALU.mult), [B("qbT", ti), B("eb")], [B("qi", ti)])
                        dve(lambda e, c0=c0, tsz=tsz: e.tensor_tensor(
                            out=ki[:, :, c0:c0 + tsz], in0=kbT[:, :, c0:c0 + tsz], in1=enb[:, :, 0:tsz], op=ALU.mult),
                            [B("kbT", ti), B("enb")], [B("ki", ti)])
                        dve(lambda e, c0=c0, tsz=tsz: e.tensor_tensor(
                            out=kdT[:, :, 0:tsz], in0=kbT[:, :, c0:c0 + tsz], in1=ekd[:, :, 0:tsz], op=ALU.mult),
                            [B("kbT", ti), B("ekd")], [B("kdT")])
                        p2, PB2 = ps_next()
                        pv = p2[:].bitcast(BF16)
                        for h in range(4):
                            tr(pv[:tsz, h * 128:(h + 1) * 128], kdT[:, h, 0:tsz], ident_b[:, :], [B("kdT"), BC], [PB2])
                        cp(kd[:tsz, ti, :], pv[:tsz, 0:512], [PB2], [B("kd", ti)])
                        if smp:
                            continue
                        first = (ti % 2 == 0)
                        for half in range(2):
                            ps_, PS_ = ps_next()
                            for hh in range(2):
                                h = half * 2 + hh
                                mm(ps_[:, hh * 256:(hh + 1) * 256], kd[:tsz, ti, h * 128:(h + 1) * 128],
                                   vbr[:tsz, ti, h * 256:(h + 1) * 256], True, True, [B("kd", ti), B("vbr", ti)], [PS_])
                            for hh in range(2):
                                h = half * 2 + hh
                                if first:
                                    cp(Sl[:, h, :], ps_[:, hh * 256:(hh + 1) * 256], [PS_], [B("Sl")])
                                else:
                                    dve(lambda e, h=h, hh=hh, ps_=ps_, ti=ti: e.scalar_tensor_tensor(
                                        out=Sl[:, h, :], in0=Sl[:, h, :], scalar=edec[:, ti, h:h + 1],
                                        in1=ps_[:, hh * 256:(hh + 1) * 256], op0=ALU.mult, op1=ALU.add),
                                        [PS_, B("Sl"), B("edec", ti)], [B("Sl")])
                        if not first:
                            bs = ti // 2
                            store("sp", e2s[bs * E2S:(bs + 1) * E2S].rearrange("(p h v) -> p h v", p=128, h=4),
                                  Sl[:], B("Sl"), ESB)
                            dve(lambda e, ti=ti: e.tensor_tensor(out=atot[:], in0=edec[:, ti - 1, :], in1=edec[:, ti, :],
                                                                 op=ALU.mult), [B("edec", ti - 1), B("edec", ti)], [B("atot")])
                            store("sp", e2m[bs * 512:(bs + 1) * 512].rearrange("(p h) -> p h", p=128),
                                  atot[:], B("atot"), EMB)
                    store("sp", e2m[1024:EMN].rearrange("(p s h) -> p s h", p=128, s=2), mown[:], B("mown"), EMB)

                    if stop == "P3":
                        raise _Stop()
                    for (src, dst, SB_, DB_) in ((emin[(l, rd)], emout[(l, rd)], EMB, B("emout", l, rd)),
                                                (esin[(l, rd)], esout[(l, rd)], ESB, B("esout", l, rd)),
                                                (ekin[(l, rd)], ekout[(l, rd)], EKB, B("ekout", l, rd)),
                                                (evin[(l, rd)], evout[(l, rd)], EVB, B("evout", l, rd))):
                        trk = g.track(inc=1)
                        g.add("pool", lambda e, src=src, dst=dst: e.collective_compute(
                            "AllGather", ALU.bypass, replica_groups=[[0, 1, 2, 3], [4, 5, 6, 7]], ins=[src.opt()], outs=[dst.opt()]),
                            reads=[SB_], writes=[DB_], dma=trk)

                    eko = [ekout[(l, r_)].rearrange("(r n) -> r n", r=4) for r_ in range(2)]
                    evo = [evout[(l, r_)].rearrange("(r n) -> r n", r=4) for r_ in range(2)]
                    eso = esout[(l, rd)].rearrange("(r n) -> r n", r=4)
                    emo = emout[(l, rd)].rearrange("(r n) -> r n", r=4)
                    EKO = [B("ekout", l, r_) for r_ in range(2)]
                    EVO = [B("evout", l, r_) for r_ in range(2)]
                    ESO = B("esout", l, rd)
                    EMO = B("emout", l, rd)

                    if stop == "EX":
                        raise _Stop()
                    barrier()
                    if rd == 0:
                        QS = [B("qT", 4, h) for h in range(8)]
                        MP, MPB = psum[7], psb[7]
                        for n in range(_DEV.get("nsb", 64)):
                            kpg = kpgb[n % 2]
                            KPG = B("kpg", n % 2)
                            for i in range(2):
                                pg = 2 * n + i
                                if KPG.lt is None:
                                    KPG.lt = {"pool": g.track()}
                                g.add("pool", lambda e, kpg=kpg, i=i, pg=pg, l=l: e.indirect_dma_start(
                                    out=kpg[:, i, :], out_offset=None, in_=ck[:, :],
                                    in_offset=bass.IndirectOffsetOnAxis(ap=pidx[:, l, pg:pg + 1], axis=0)),
                                    reads=[BPI], writes=[KPG], dma=KPG.lt["pool"])
                            for h in range(8):
                                for i in range(2):
                                    mm(MP[:, h * 64 + n: h * 64 + n + 1], kpg[:, i, h * 128:(h + 1) * 128], ones_b[:, 0:1],
                                       i == 0, i == 1, [KPG, BC], [MPB])
                            kts = kTs[n % 2]
                            KTS = B("kTs", n % 2)
                            for i in range(2):
                                p2, PB2 = ps_next()
                                pv = p2[:].bitcast(BF16)
                                for h in range(8):
                                    tr(pv[:, h * 128:(h + 1) * 128], kpg[:, i, h * 128:(h + 1) * 128], ident_b[:, :],
                                       [KPG, BC], [PB2])
                                cp(kts[:, i * 8:(i + 1) * 8, :], pv[:, 0:1024].rearrange("p (h t) -> p h t", h=8), [PB2], [KTS])
                            p3, PB3 = ps_next()
                            for i in range(2):
                                for h in range(8):
                                    mm(p3[:, i * 32 + h * 4: i * 32 + h * 4 + 4], kts[:, i * 8 + h, :], qT[:, h, 512:516],
                                       True, True, [KTS, QS[h]], [PB3])
                            act(PallT[:, 2 * n:2 * n + 2, :], p3[:, 0:64].rearrange("p (i c) -> p i c", i=2), AF.Exp,
                                [PB3], [B("PallT")], scale=SCALE)
                        cp(means_s[:], MP[:, :], [MPB], [B("means_s")])
                        p4, PB4 = ps_next()
                        for h in range(8):
                            mm(p4[0:4, h * 64:(h + 1) * 64], qT[:, h, 512:516], means_s[:, h * 64:(h + 1) * 64], True, True,
                               [QS[h], B("means_s")], [PB4])
                        cp(sms[:], p4[0:4, :], [PB4], [B("sms")])
                        for h in range(8):
                            dve(lambda e, h=h: e.max(out=top8s[0:4, h * 8:(h + 1) * 8], in_=sms[0:4, h * 64:(h + 1) * 64]),
                                [B("sms")], [B("top8s")])
                        for h in range(8):
                            dve(lambda e, h=h: e.tensor_scalar(out=sel01b[0:4, h * 64:(h + 1) * 64],
                                                               in0=sms[0:4, h * 64:(h + 1) * 64],
                                                               scalar1=top8s[0:4, h * 8 + 2:h * 8 + 3], scalar2=None,
                                                               op0=ALU.is_ge), [B("sms"), B("top8s")], [B("sel01b")])
                        for q_ in range(4):
                            dve(lambda e, q_=q_: e.tensor_tensor(out=rhsbd[0:4, q_, :], in0=bdb[0:4, q_ * 512:(q_ + 1) * 512],
                                                                 in1=sel01b[0:4, :], op=ALU.mult), [B("sel01b"), BC],
                                [B("rhsbd")])
                        for q_ in range(4):
                            p5, PB5 = ps_next()
                            mm(p5[:, :], ones_b[0:4, :], rhsbd[0:4, q_, :], True, True, [B("rhsbd"), BC], [PB5])
                            cp(maskrep[:, q_, :], p5[:, :], [PB5], [B("maskrep")])
                        P5v = PallT[:, :, :].rearrange("p (n i) (h q) -> p n i h q", i=2, h=8)
                        Mv = maskrep[:, :, :].rearrange("p q (h n) -> p n h q", h=8)
                        for i in range(2):
                            for h in range(8):
                                dve(lambda e, i=i, h=h: e.tensor_tensor(out=P5v[:, :, i, h, :], in0=P5v[:, :, i, h, :],
                                                                         in1=Mv[:, :, h, :], op=ALU.mult),
                                    [B("PallT"), B("maskrep")], [B("PallT")])
                        OA, OAB = psum[5], psb[5]
                        OB_, OBB = psum[6], psb[6]
                        DN, DNB = psum[7], psb[7]
                        for n in range(_DEV.get("nsb", 64)):
                            vpg = kpgb[n % 2]
                            KPG = B("kpg", n % 2)
                            for i in range(2):
                                pg = 2 * n + i
                                g.add("pool", lambda e, vpg=vpg, i=i, pg=pg, l=l: e.indirect_dma_start(
                                    out=vpg[:, i, :], out_offset=None, in_=cv[:, :],
                                    in_offset=bass.IndirectOffsetOnAxis(ap=pidx[:, l, pg:pg + 1], axis=0)),
                                    reads=[BPI], writes=[KPG], dma=KPG.lt["pool"])
                            for i in range(2):
                                pg = 2 * n + i
                                mm(OA[0:32, :], PallT[:, pg, :], vpg[:, i, 0:512], pg == 0, False, [B("PallT"), KPG], [OAB])
                                mm(OB_[0:32, :], PallT[:, pg, :], vpg[:, i, 512:1024], pg == 0, False, [B("PallT"), KPG], [OBB])
                                mm(DN[0:32, 0:1], PallT[:, pg, :], ones_b[:, 0:1], pg == 0, False, [B("PallT"), BC], [DNB])
                        p6, PB6 = ps_next()
                        for h in range(8):
                            mm(p6[0:4, h * 4:(h + 1) * 4], ksT[:, h, :], qT[:, h, 512:516], True, True, [B("ksT"), QS[h]], [PB6])
                        act(pown[:], p6[0:4, 0:32], AF.Exp, [PB6], [B("pown")], scale=SCALE)
                        dve(lambda e: e.tensor_tensor(out=pownb[:], in0=pown[:], in1=cf[0:4, 784:816], op=ALU.mult),
                            [B("pown"), BC], [B("pownb")])
                        mm(OA[0:32, :], pownb[0:4, :], vsr[0:4, 0:512], False, True, [B("pownb"), B("vsr")], [OAB])
                        mm(OB_[0:32, :], pownb[0:4, :], vsr[0:4, 512:1024], False, True, [B("pownb"), B("vsr")], [OBB])
                        mm(DN[0:32, 0:1], pownb[0:4, :], ones_b[0:4, 0:1], False, True, [B("pownb"), BC], [DNB])
                        dve(lambda e: e.reciprocal(out=rec[0:32, :], in_=DN[0:32, 0:1]), [DNB], [B("rec")])
                        dve(lambda e: e.tensor_scalar(out=osb[:, 0:512], in0=OA[0:32, :], scalar1=rec[0:32, :], scalar2=None,
                                                      op0=ALU.mult), [OAB, B("rec")], [B("osb")])
                        dve(lambda e: e.tensor_scalar(out=osb[:, 512:1024], in0=OB_[0:32, :], scalar1=rec[0:32, :],
                                                      scalar2=None, op0=ALU.mult), [OBB, B("rec")], [B("osb")])
                        for h in range(8):
                            store("sp", oscr[:, h * 128:(h + 1) * 128], osb[4 * h:4 * h + 4, h * 128:(h + 1) * 128],
                                  B("osb"), B("oscr"))
                        load("sp", ost[:], oscr[:, :], B("ost"), reads=[B("oscr")])
                        dve(lambda e: e.tensor_tensor(out=oasb[:], in0=ost[:], in1=sza[0:4, 4, :], op=ALU.mult),
                            [B("ost"), B("sza", 4)], [B("oasb")])
                        p7, PB7 = ps_next()
                        pv = p7[:].bitcast(BF16)
                        for h in range(8):
                            tr(pv[:, h * 128:h * 128 + 4], oasb[0:4, h * 128:(h + 1) * 128], ident_b[0:4, 0:4],
                               [B("oasb"), BC], [PB7])
                        cp(qT[:, :, 512:516], pv[:, 0:1024].rearrange("p (h t) -> p h t", h=8)[:, :, 0:4], [PB7], QS)

                    if stop == "P4d":
                        raise _Stop()
                    if rd == 0:
                        barrier()
                    for gq in range(8):
                        m = rd * 8 + gq
                        jm, slot = _owner(m)
                        bs = slot % 2
                        for sl in range(2):
                            col = 128 + (rd * 2 + sl) * 8 + gq
                            if gq == 0:
                                dve(lambda e, sl=sl, col=col: e.tensor_scalar(out=S_st[sl][:], in0=S_run[:],
                                                                              scalar1=tabs[:, col:col + 1], scalar2=None,
                                                                              op0=ALU.mult), [B("S_run"), BC], [B("S_st", sl)])
                            else:
                                dve(lambda e, sl=sl, col=col: e.scalar_tensor_tensor(
                                    out=S_st[sl][:], in0=S_run[:], scalar=tabs[:, col:col + 1], in1=S_st[sl][:],
                                    op0=ALU.mult, op1=ALU.add), [B("S_run"), BC, B("S_st", sl)], [B("S_st", sl)])
                        sl_ = slb[gq % 2]
                        SLB = B("slb", gq % 2)
                        load("sp", sl_[:], eso[jm, bs * E2S:(bs + 1) * E2S].rearrange("(p h v) -> p h v", p=128, h=4), SLB,
                             reads=[ESO])
                        ag_ = atg[gq % 2]
                        AGB = B("atg", gq % 2)
                        load("sp", ag_[:], emo[jm, bs * 512:(bs + 1) * 512].rearrange("(p h) -> p h", p=128),
                             AGB, reads=[EMO])
                        for h in range(4):
                            dve(lambda e, h=h, sl_=sl_, ag_=ag_: e.scalar_tensor_tensor(
                                out=S_run[:, h, :], in0=S_run[:, h, :], scalar=ag_[:, h:h + 1], in1=sl_[:, h, :],
                                op0=ALU.mult, op1=ALU.add), [B("S_run"), SLB, AGB], [B("S_run")])
                        load("sp", meansf[:, m, :],
                             emo[jm, 1024:EMN].rearrange("(p s h) -> p s h", p=128, s=2)[:, bs, :], B("meansf"),
                             reads=[EMO])
                    if rd == 1:
                        store("sp", gp[l].rearrange("h k v -> k h v"), S_run[:], B("S_run"), B("gp"))
                    cp(meansb[:], meansf[:], [B("meansf")], [B("meansb")])

                    if stop == "P4a":
                        raise _Stop()
                    def gla_tile(ti, c0, tsz, Sf, SB_, update):
                        for h in range(4):
                            pa, PA = ps_next()
                            mm(pa[:tsz, 0:tsz], ki[:, h, c0:c0 + tsz], qi[:, h, c0:c0 + tsz], True, True,
                               [B("ki", ti), B("qi", ti)], [PA])
                            at_ = att[h % 2]
                            ATB = B("att", h % 2)
                            dve(lambda e, pa=pa, at_=at_: e.tensor_tensor(out=at_[:tsz, :tsz], in0=pa[:tsz, 0:tsz],
                                                                           in1=tri_b[:tsz, :tsz], op=ALU.mult),
                                [PA, BC], [ATB])
                            po_, PO = ps_next()
                            mm(po_[:tsz, 0:256], at_[:tsz, :tsz], vbr[:tsz, ti, h * 256:(h + 1) * 256], True, False,
                               [ATB, B("vbr", ti)], [PO])
                            mm(po_[:tsz, 0:256], qi[:, h, c0:c0 + tsz], S_bf[:, h, :], False, True,
                               [B("qi", ti), B("S_bf")], [PO])
                            rms_rstd(rstd[:tsz, :], po_[:tsz, 0:256], 256, tsz, [PO], junk2[:tsz, 0:256], B("junk2"))
                            dve(lambda e, po_=po_, h=h: e.scalar_tensor_tensor(
                                out=otmp[:tsz, :], in0=po_[:tsz, 0:256], scalar=rstd[:tsz, :],
                                in1=gnrep[:tsz, h * 256:(h + 1) * 256], op0=ALU.mult, op1=ALU.mult),
                                [PO, B("rstd"), B("gnrep")], [B("otmp")])
                            dve(lambda e, h=h: e.tensor_tensor(out=obst[:tsz, h * 256:(h + 1) * 256], in0=otmp[:tsz, :],
                                                               in1=szb[:tsz, ti, h * 256:(h + 1) * 256], op=ALU.mult),
                                [B("otmp"), B("szb", ti)], [B("obst")])
                            if update:
                                pn, PN = ps_next()
                                mm(pn[:, 0:256], kd[:tsz, ti, h * 128:(h + 1) * 128], vbr[:tsz, ti, h * 256:(h + 1) * 256],
                                   True, True, [B("kd", ti), B("vbr", ti)], [PN])
                                dve(lambda e, pn=pn, h=h: e.scalar_tensor_tensor(
                                    out=Sf[:, h, :], in0=Sf[:, h, :], scalar=edec[:, ti, h:h + 1], in1=pn[:, 0:256],
                                    op0=ALU.mult, op1=ALU.add), [PN, SB_, B("edec", ti)], [SB_])
                        p2, PB2 = ps_next()
                        pv = p2[:].bitcast(BF16)
                        for jj in range(8):
                            tr(pv[:, jj * 128:jj * 128 + tsz], obst[:tsz, jj * 128:(jj + 1) * 128], ident_b[:tsz, :tsz],
                               [B("obst"), BC], [PB2])
                        cp(obT[:, :, c0:c0 + tsz], pv[:, 0:1024].rearrange("p (h t) -> p h t", h=8)[:, :, 0:tsz], [PB2],
                           [B("obT", ti)])

                    for sl in range(2):
                        for tt in range(2):
                            ti = 2 * sl + tt
                            cp(S_bf[:], S_st[sl][:], [B("S_st", sl)], [B("S_bf")])
                            gla_tile(ti, ti * 128, 128, S_st[sl], B("S_st", sl), tt == 0)
                    if rd == 0:
                        cp(S_bf[:], S_s[:], [B("S_s")], [B("S_bf")])
                        gla_tile(4, 512, 4, S_s, B("S_s"), True)
                        store("sp", gs[l].rearrange("h k v -> k h v"), S_s[:], B("S_s"), B("gs"))

                    if stop == "P4b":
                        raise _Stop()
                    nblk = 7 if rd == 0 else 15
                    cands = ([0, 1, 2], list(range(7))) if rd == 0 else (list(range(11)), list(range(15)))
                    e1kT = ekin[(l, rd)].rearrange("(h d t) -> h d t", h=8, d=128)
                    e1V = evin[(l, rd)].rearrange("(t n) -> t n", n=1024)
                    for vb_i in range(2):
                        dve(lambda e, vb_i=vb_i: e.memset(vbuf[vb_i][:, :, 128:129], 1.0), [], [B("vbuf", vb_i)])
                    for h in range(8):
                        kb_ = kbuf[h % 2]
                        vb_ = vbuf[h % 2]
                        KB, VB = B("kbuf", h % 2), B("vbuf", h % 2)
                        for m in range(nblk):
                            jm, slot = _owner(m)
                            rnd, bs = slot // 2, slot % 2
                            srck = eko[rnd][jm, :].rearrange("(h d t) -> h d t", h=8, d=128)
                            srcv = evo[rnd][jm, :].rearrange("(t n) -> t n", n=1024)
                            load("sp", kb_[:, m * 256:(m + 1) * 256], srck[h, :, bs * 256:(bs + 1) * 256], KB, reads=[EKO[rnd]])
                            load("sp", vb_[:, 2 * m:2 * m + 2, 0:128],
                                 srcv[bs * 256:(bs + 1) * 256, h * 128:(h + 1) * 128].rearrange("(t p) d -> p t d", p=128),
                                 VB, reads=[EVO[rnd]])
                        load("sp", kb_[:, nblk * 256:nblk * 256 + 512], e1kT[h, :, :], KB, reads=[EKB])
                        load("sp", vb_[:, 2 * nblk:2 * nblk + 4, 0:128],
                             e1V[:, h * 128:(h + 1) * 128].rearrange("(t p) d -> p t d", p=128), VB, reads=[EVB])
                        for sl in range(2):
                            s4 = rd * 2 + sl
                            for qt in range(2):
                                ti = 2 * sl + qt
                                c0 = ti * 128
                                QB = B("qT", ti, h)
                                qap = qT[:, h, c0:c0 + 128]
                                smv, top8, selb, selbb, selT = smv4[ti], top84[ti], selb4[ti], selbb4[ti], selT4[ti]
                                p1, P1B = ps_next()
                                mm(p1[:, 0:16], qap, meansb[:, :, h], True, True, [QB, B("meansb")], [P1B])
                                dve(lambda e, p1=p1, s4=s4, smv=smv: e.tensor_tensor(
                                    out=smv[:], in0=p1[:, 0:16], in1=tabs[:, s4 * 16:(s4 + 1) * 16], op=ALU.add),
                                    [P1B, BC], [B("smv", ti)])
                                dve(lambda e, smv=smv, top8=top8: e.max(out=top8[:], in_=smv[:]), [B("smv", ti)], [B("top8", ti)])
                                dve(lambda e, smv=smv, top8=top8, selb=selb: e.tensor_scalar(
                                    out=selb[:], in0=smv[:], scalar1=top8[:, 2:3], scalar2=None, op0=ALU.is_ge),
                                    [B("smv", ti), B("top8", ti)], [B("selb", ti)])
                                dve(lambda e, s4=s4, selb=selb: e.tensor_tensor(
                                    out=selb[:], in0=selb[:], in1=tabs[:, 64 + s4 * 16:64 + (s4 + 1) * 16], op=ALU.mult),
                                    [B("selb", ti), BC], [B("selb", ti)])
                                dve(lambda e, selb=selb, selbb=selbb: e.tensor_scalar(
                                    out=selbb[:], in0=selb[:], scalar1=1.0, scalar2=1.0e30, op0=ALU.subtract, op1=ALU.mult),
                                    [B("selb", ti)], [B("selbb", ti)])
                                p2, PB2 = ps_next()
                                pv = p2[:].bitcast(BF16)
                                tr(pv[0:16, 0:128], selbb[:, :], ident_b[:, :], [B("selbb", ti), BC], [PB2])
                                cp(selT[:], pv[0:16, 0:128], [PB2], [B("selT", ti)])
                        for sl in range(2):
                            for qt in range(2):
                                ti = 2 * sl + qt
                                c0 = ti * 128
                                QB = B("qT", ti, h)
                                qap = qT[:, h, c0:c0 + 128]
                                selT = selT4[ti]
                                ob = 5 + (ocnt[0] % 3)
                                oi = ocnt[0] % 2
                                ocnt[0] += 1
                                OP, OPB = psum[ob], psb[ob]
                                rec_, oat_ = rec2[oi], oat2[oi]
                                RB, OTB = B("rec2", oi), B("oat2", oi)
                                kts_ = []
                                for m in cands[sl]:
                                    for kt in range(2):
                                        kts_.append((kb_[:, m * 256 + kt * 128:m * 256 + (kt + 1) * 128], m, 2 * m + kt, False))
                                for kt in range(qt + 1):
                                    o_ = (nblk + sl) * 256 + kt * 128
                                    kts_.append((kb_[:, o_:o_ + 128], None, 2 * (nblk + sl) + kt, kt == qt))
                                nk = len(kts_)
                                def emit_pv(g0_, grp_, pt2_, PTB_):
                                    for jx, (kap, em_, vi, tri_) in enumerate(grp_):
                                        first = (g0_ + jx == 0)
                                        last = (g0_ + jx == nk - 1)
                                        mm(OP[:, 0:129], pt2_[:, jx * 128:(jx + 1) * 128], vb_[:, vi, :], first, last,
                                           [PTB_, VB], [OPB])

                                pend = None
                                for g0 in range(0, nk, 4):
                                    grp = kts_[g0:g0 + 4]
                                    p3, PB3 = ps_next()
                                    for jx, (kap, em_, vi, tri_) in enumerate(grp):
                                        mm(p3[:, jx * 128:(jx + 1) * 128], kap, qap, True, em_ is None, [KB, QB], [PB3])
                                        if em_ is not None:
                                            mm(p3[:, jx * 128:(jx + 1) * 128], emb[0:16, em_ * 128:(em_ + 1) * 128], selT[:, :],
                                               False, True, [B("selT", ti), BC], [PB3])
                                    pt2 = ptt[cnt2[0] % 3]
                                    PTB = B("ptt", cnt2[0] % 3)
                                    cnt2[0] += 1
                                    act(pt2[:, 0:len(grp) * 128], p3[:, 0:len(grp) * 128], AF.Exp, [PB3], [PTB], scale=SCALE)
                                    for jx, (kap, em_, vi, tri_) in enumerate(grp):
                                        if tri_:
                                            dve(lambda e, pt2=pt2, jx=jx: e.tensor_tensor(
                                                out=pt2[:, jx * 128:(jx + 1) * 128], in0=pt2[:, jx * 128:(jx + 1) * 128],
                                                in1=tri_b[:, :], op=ALU.mult), [PTB, BC], [PTB])
                                    if pend is not None:
                                        emit_pv(*pend)
                                    pend = (g0, grp, pt2, PTB)
                                emit_pv(*pend)
                                dve(lambda e, OP=OP, rec_=rec_: e.reciprocal(out=rec_[:], in_=OP[:, 128:129]), [OPB], [RB])
                                dve(lambda e, ti=ti, h=h, OP=OP, rec_=rec_, oat_=oat_: e.scalar_tensor_tensor(
                                    out=oat_[:], in0=OP[:, 0:128], scalar=rec_[:, 0:1], in1=sza[:, ti, h * 128:(h + 1) * 128],
                                    op0=ALU.mult, op1=ALU.mult), [OPB, RB, B("sza", ti)], [OTB])
                                p4, PB4 = ps_next()
                                pv4 = p4[:].bitcast(BF16)
                                tr(pv4[:, 0:128], oat_[:, :], ident_b[:, :], [OTB, BC], [PB4])
                                cp(qT[:, h, c0:c0 + 128], pv4[:, 0:128], [PB4], [QB])

                    if stop == "P4c":
                        raise _Stop()
                    barrier()
                    wav = w_a[l].rearrange("(k p) n -> p k n", p=128)
                    wbv = w_b[l].rearrange("(k p) n -> p k n", p=128)
                    wov = w_out[l].rearrange("(k p) n -> p k n", p=128)
                    for cc in range(4):
                        wsa = wrot[0] % 2
                        wrot[0] += 1
                        wta = wbufD[wsa][:, 0:8 * 512].rearrange("p (k n) -> p k n", k=8)
                        load("pool", wta, wav[:, :, cc * 512:(cc + 1) * 512], wB[wsa])
                        wsb = wrot[0] % 2
                        wrot[0] += 1
                        wtb = wbufD[wsb][:, 0:8 * 512].rearrange("p (k n) -> p k n", k=8)
                        load("pool", wtb, wbv[:, :, cc * 512:(cc + 1) * 512], wB[wsb])
                        for ti, (r0, c0, tsz, smp) in enumerate(tiles):
                            pa, PA = ps_next()
                            for h in range(8):
                                mm(pa[:tsz, :], qT[:, h, c0:c0 + tsz], wta[:, h, :], h == 0, h == 7, [B("qT", ti, h), wB[wsa]], [PA])
                            pb2, PBB = ps_next()
                            for h in range(8):
                                mm(pb2[:tsz, :], obT[:, h, c0:c0 + tsz], wtb[:, h, :], h == 0, h == 7, [B("obT", ti), wB[wsb]], [PBB])
                            while pend2:
                                pend2.pop(0)()
                            gs_ = cnt2[0] % 2
                            cnt2[0] += 1
                            gb_ = gab[gs_]
                            GB = B("gab", gs_)
                            load("sp", gb_[:tsz, :, :],
                                 gscr[c0:c0 + tsz, :].rearrange("t (a c n) -> t a c n", a=2, c=4)[:, :, cc, :], GB,
                                 reads=[B("gscr")])
                            dve(lambda e, pa=pa, gb_=gb_, tsz=tsz: e.tensor_tensor(out=t1[:tsz, :], in0=pa[:tsz, :],
                                                                                  in1=gb_[:tsz, 0, :], op=ALU.mult),
                                [PA, GB], [B("t1")])
                            dve(lambda e, pb2=pb2, gb_=gb_, tsz=tsz: e.tensor_tensor(out=t2[:tsz, :], in0=pb2[:tsz, :],
                                                                                   in1=gb_[:tsz, 1, :], op=ALU.mult),
                                [PBB, GB], [B("t2")])
                            ms_ = mst2[cnt2[0] % 2]
                            MSB = B("mst2", cnt2[0] % 2)
                            dve(lambda e, tsz=tsz, ms_=ms_: e.tensor_tensor(out=ms_[:tsz, :], in0=t1[:tsz, :], in1=t2[:tsz, :],
                                                                           op=ALU.add), [B("t1"), B("t2")], [MSB])
                            def fin_m(ms_=ms_, MSB=MSB, tsz=tsz, cc=cc, c0=c0, ti=ti):
                                p2, PB2 = ps_next()
                                pv = p2[:].bitcast(BF16)
                                for i in range(4):
                                    tr(pv[:, i * 128:i * 128 + tsz], ms_[:tsz, i * 128:(i + 1) * 128], ident_b[:tsz, :tsz],
                                       [MSB, BC], [PB2])
                                cp(xnT[:, cc * 4:(cc + 1) * 4, c0:c0 + tsz],
                                   pv[:, 0:512].rearrange("p (h t) -> p h t", h=4)[:, :, 0:tsz], [PB2], [B("xnT", ti)])
                            pend2.append(fin_m)
                    while pend2:
                        pend2.pop(0)()
                    for cc in range(4):
                        ws = wrot[0] % 2
                        wrot[0] += 1
                        wto = wbufD[ws][:, 0:KC * 512].rearrange("p (k n) -> p k n", k=KC)
                        load("pool", wto, wov[:, :, cc * 512:(cc + 1) * 512], wB[ws])
                        for ti, (r0, c0, tsz, smp) in enumerate(tiles):
                            po_, PO = ps_next()
                            for k in range(KC):
                                mm(po_[:tsz, :], xnT[:, k, c0:c0 + tsz], wto[:, k, :], k == 0, k == KC - 1,
                                   [B("xnT", ti), wB[ws]], [PO])
                            xs_ = cnt2[0] % 2
                            cnt2[0] += 1
                            xr = xres[xs_]
                            XR = B("xres", xs_)
                            XS = B("xscr", "s") if smp else B("xscr", r0)
                            if smp:
                                src = (xs if l == 0 else xscr[1024:1028, :])[:, cc * 512:(cc + 1) * 512]
                                dst = xscr[1024:1028, cc * 512:(cc + 1) * 512]
                            else:
                                src = (xp if l == 0 else xscr)[r0:r0 + tsz, cc * 512:(cc + 1) * 512]
                                dst = xscr[r0:r0 + tsz, cc * 512:(cc + 1) * 512]
                            load("sp", xr[:tsz, :], src, XR, reads=[XS])
                            dve(lambda e, po_=po_, xr=xr, tsz=tsz: e.tensor_tensor(out=xr[:tsz, :], in0=po_[:tsz, :],
                                                                                 in1=xr[:tsz, :], op=ALU.add), [PO, XR], [XR])
                            store("sp", dst, xr[:tsz, :], XR, XS)

            barrier()
            load("sp", fngrep, fng.broadcast_to([128, D]), B("fngrep"))
            fin = [(r0, 128, False) for r0 in range(0, 1024, 128)] + [(1024, 4, True)]
            for fi, (r0, tsz, smp) in enumerate(fin):
                xb_ = xtF
                XB = B("xtF")
                XS = B("xscr", "s") if smp else B("xscr", r0)
                load("sp", xb_[:tsz, :], xscr[r0:r0 + tsz, :], XB, reads=[XS])
                rms_rstd(rstd[:tsz, :], xb_[:tsz, :], D, tsz, [XB], junkF[:tsz, :], B("junkF"))
                dve(lambda e, xb_=xb_, tsz=tsz: e.scalar_tensor_tensor(out=xb_[:tsz, :], in0=xb_[:tsz, :], scalar=rstd[:tsz, :],
                                                                      in1=fngrep[:tsz, :], op0=ALU.mult, op1=ALU.mult),
                    [XB, B("rstd"), B("fngrep")], [XB])
                if smp:
                    store("sp", ys[:, :], xb_[:tsz, :], XB, B("ys"))
                else:
                    store("sp", yp[r0:r0 + tsz, :], xb_[:tsz, :], XB, B("yp"))
        except _Stop:
            pass
        fextra = set(g.last[e_] for e_ in ("pe", "act", "dve") if g.last[e_] is not None)
        for t_ in g.tracks:
            if t_.last_op is not None:
                fextra.add(t_.last_op)
        for e_ in Graph.ENGS:
            g.add(e_, None, extra=fextra)
        g.add("sp", None, reads=[B("yp"), B("ys"), B("kp", 0), B("kp", 1), B("vp", 0), B("vp", 1), B("gp"),
                                 B("ks", 0), B("ks", 1), B("vs", 0), B("vs", 1), B("gs")])
        g.emit()
    return nc


_NC_CACHE = {}
_DEV = {"npool": NPOOL, "stop": None}


def _blocks(j):
    return [j, 7 - j, 8 + j, 15 - j]


def _tab(j):
    t = np.zeros((128, 160), np.float32)
    blks = _blocks(j)
    for s_ in range(4):
        g_ = blks[s_]
        t[:, s_ * 16:(s_ + 1) * 16] = np.where(np.arange(16) < g_, 0.0, NEG)
        t[:, 64 + s_ * 16:64 + (s_ + 1) * 16] = (np.arange(16) < g_).astype(np.float32)
    for rd in range(2):
        for sl in range(2):
            g_ = blks[rd * 2 + sl]
            for gq in range(8):
                t[:, 128 + (rd * 2 + sl) * 8 + gq] = 1.0 if (rd * 8 + gq) == g_ else 0.0
    return t


def kernel(x_prompt, x_sample, cache_k, cache_v, state_gla, page_table, norm_g, w_in, w_gate2, b_gate,
           gla_norm_g, w_branch_a, w_branch_b, b_merge, w_out, final_norm_g):
    f = lambda a: np.ascontiguousarray(np.asarray(a, dtype=np.float32))
    x_prompt, x_sample = f(x_prompt), f(x_sample)
    ckf = f(cache_k).reshape(2 * _DEV["npool"] * 128, 1024)
    cvf = f(cache_v).reshape(2 * _DEV["npool"] * 128, 1024)
    state_gla = f(state_gla)
    page_table = np.ascontiguousarray(np.asarray(page_table, dtype=np.int32))
    cst, emc, bdc = _consts()
    shared = {
        "ck": ckf, "cv": cvf, "norm_g": f(norm_g), "w_in": f(w_in), "w_gate2": f(w_gate2), "b_gate": f(b_gate),
        "gla_g": f(gla_norm_g).reshape(2, 1024), "w_a": f(w_branch_a), "w_b": f(w_branch_b), "b_merge": f(b_merge),
        "w_out": f(w_out), "fng": f(final_norm_g).reshape(1, D), "cst": cst, "emc": emc, "bdc": bdc,
    }
    in_maps = []
    for c in range(8):
        b, j = c // 4, c % 4
        rows = np.concatenate([np.arange(m * 256, (m + 1) * 256) for m in _blocks(j)])
        d = dict(shared)
        d["xp"] = np.ascontiguousarray(x_prompt[b][rows])
        d["xs"] = np.ascontiguousarray(x_sample[c])
        d["sg"] = np.ascontiguousarray(state_gla[:, c])
        d["pt"] = np.ascontiguousarray(page_table[c].reshape(1, 128))
        d["meta"] = np.full((1, 64), c, np.int32)
        d["tab"] = _tab(j)
        in_maps.append(d)
    if "nc" not in _NC_CACHE:
        _NC_CACHE["nc"] = build(npool=_DEV["npool"], stop=_DEV["stop"])
    if _DEV.get("trace"):
        res = run_bass_kernel_spmd(_NC_CACHE["nc"], in_maps, core_ids=list(range(8)), trace=True)
        print("EXEC_TIME_NS", res.exec_time_ns)
    else:
        res = run_bass_kernel_spmd(_NC_CACHE["nc"], in_maps, core_ids=list(range(8)))
    R = res.results
    y_prompt = np.zeros((2, 4096, D), np.float32)
    k_prompt = np.zeros((2, 2, 4096, 8, 128), np.float32)
    v_prompt = np.zeros((2, 2, 4096, 8, 128), np.float32)
    gla_prompt = np.zeros((2, 2, 4, 128, 256), np.float32)
    y_sample = np.zeros((8, 4, D), np.float32)
    k_sample = np.zeros((2, 8, 4, 8, 128), np.float32)
    v_sample = np.zeros((2, 8, 4, 8, 128), np.float32)
    gla_sample = np.zeros((2, 8, 4, 128, 256), np.float32)
    for c in range(8):
        b, j = c // 4, c % 4
        rows = np.concatenate([np.arange(m * 256, (m + 1) * 256) for m in _blocks(j)])
        r = R[c]
        y_prompt[b, rows] = r["yp"]
        k_prompt[:, b, rows] = np.asarray(r["kp"]).reshape(2, 1024, 8, 128)
        v_prompt[:, b, rows] = np.asarray(r["vp"]).reshape(2, 1024, 8, 128)
        if j == 0:
            gla_prompt[:, b] = r["gp"]
        y_sample[c] = r["ys"]
        k_sample[:, c] = np.asarray(r["ks"]).reshape(2, 4, 8, 128)
        v_sample[:, c] = np.asarray(r["vs"]).reshape(2, 4, 8, 128)
        gla_sample[:, c] = r["gs"]
    return (y_prompt, y_sample, k_prompt, v_prompt, gla_prompt, k_sample, v_sample, gla_sample)
```

```python
from contextlib import ExitStack
import numpy as np
import ml_dtypes
import concourse.bass as bass
import concourse.mybir as mybir
from concourse.bass_utils import run_bass_kernel_spmd

F32 = mybir.dt.float32
BF16 = mybir.dt.bfloat16
I32 = mybir.dt.int32
U32 = mybir.dt.uint32
AF = mybir.ActivationFunctionType
ALU = mybir.AluOpType
AX = mybir.AxisListType

SAME_ENGINE_SYNC = True

D = 2048
KC = 16
NIN = 11280
NPOOL = 1280
NEG = -1.0e30
EPS = 1e-6
SCALE = 128 ** -0.5


class Buf:
    __slots__ = ("name", "last_w", "readers", "lt", "st")

    def __init__(self, name):
        self.name = name
        self.last_w = None
        self.readers = []
        self.lt = None
        self.st = None


class Track:
    __slots__ = ("sem", "count", "inc", "last_op")

    def __init__(self, sem, inc=16):
        self.sem = sem
        self.count = 0
        self.inc = inc
        self.last_op = None


class Op:
    __slots__ = ("eng", "fn", "deps", "track", "ordinal", "tick", "signal", "idx", "waits")


class Graph:
    ENGS = ("pe", "act", "dve", "pool", "sp")

    def __init__(self, nc, stack):
        self.nc = nc
        self.stack = stack
        self.ops = []
        self.esem = {e: stack.enter_context(nc.semaphore("es_" + e)) for e in self.ENGS}
        self.nsem = 5
        self.last = {e: None for e in self.ENGS}
        self.tracks = []
        self.rkeys = []

    def track(self, inc=16):
        self.nsem += 1
        t = Track(self.stack.enter_context(self.nc.semaphore("trk%d" % self.nsem)), inc)
        self.tracks.append(t)
        return t

    def add(self, eng, fn, reads=(), writes=(), dma=None, extra=()):
        op = Op()
        op.eng = eng
        op.fn = fn
        op.track = dma
        op.idx = len(self.ops)
        op.signal = False
        op.tick = None
        op.ordinal = None
        deps = set(extra)
        for b in reads:
            if b.last_w is not None:
                deps.add(b.last_w)
        for b in writes:
            if b.last_w is not None:
                deps.add(b.last_w)
            for r in b.readers:
                deps.add(r)
        op.deps = deps
        rkey = eng if dma is None else ("t", id(dma))
        for b in reads:
            b.readers = [r for r in b.readers if self.rkeys[r] != rkey]
            b.readers.append(op.idx)
        self.rkeys.append(rkey)
        for b in writes:
            b.last_w = op.idx
            b.readers = []
        if dma is not None:
            dma.count += 1
            op.ordinal = dma.count
            dma.last_op = op.idx
        self.ops.append(op)
        if fn is not None:
            self.last[eng] = op.idx
        return op

    def emit(self):
        ops = self.ops

        def skip(d, op):
            return d.eng == op.eng and (d.eng == "pe" or not SAME_ENGINE_SYNC)

        for op in ops:
            for j in op.deps:
                d = ops[j]
                if d.track is None and not skip(d, op):
                    d.signal = True
        cnt = {e: 0 for e in self.ENGS}
        for op in ops:
            if op.track is None and op.signal:
                cnt[op.eng] += 1
                op.tick = cnt[op.eng]
        seen = {e: {} for e in self.ENGS}
        for op in ops:
            w = {}
            for j in op.deps:
                d = ops[j]
                if d.track is not None:
                    key, val = d.track.sem, d.track.inc * d.ordinal
                else:
                    if skip(d, op):
                        continue
                    key, val = self.esem[d.eng], d.tick
                if w.get(key, 0) < val:
                    w[key] = val
            sd = seen[op.eng]
            op.waits = []
            for key, val in w.items():
                if sd.get(key, 0) < val:
                    sd[key] = val
                    op.waits.append((key, val))
        by_eng = {e: [op for op in ops if op.eng == e] for e in self.ENGS}
        esem = self.esem

        def run(engname, eng):
            for op in by_eng[engname]:
                for (sem, val) in op.waits:
                    eng.wait_ge(sem, val)
                if op.fn is None:
                    continue
                ins = op.fn(eng)
                if op.track is not None:
                    if op.track.inc == 16:
                        ins.then_inc(op.track.sem, 16)
                    else:
                        ins.then_inc(op.track.sem)
                elif op.signal:
                    ins.then_inc(esem[engname], 1)

        with self.nc.Block() as block:
            @block.tensor
            def _(e):
                run("pe", e)

            @block.scalar
            def _(e):
                run("act", e)

            @block.vector
            def _(e):
                run("dve", e)

            @block.gpsimd
            def _(e):
                run("pool", e)

            @block.sync
            def _(e):
                run("sp", e)


def _consts():
    c = np.zeros((128, 1024), np.float32)
    r = np.arange(128)
    c[:, 0:128] = np.eye(128)
    c[:, 128:256] = (r[:, None] <= r[None, :]).astype(np.float32)
    c[:, 256:384] = c[:, 128:256] * (-1.0 / 16.0)
    c[:, 384:512] = 1.0
    c[:, 512] = r
    for g_ in range(16):
        c[:, 528 + g_ * 16: 528 + (g_ + 1) * 16] = np.where(np.arange(16) < g_, 0.0, NEG)
    for h in range(8):
        for q in range(4):
            c[0:4, 784 + h * 4 + q] = (np.arange(4) <= q).astype(np.float32)
    em = np.zeros((64, 64 * 128), np.float32)
    for m in range(64):
        em[m, m * 128:(m + 1) * 128] = 1.0
    bd = np.zeros((4, 4 * 512), np.float32)
    for q in range(4):
        bd[q, q * 512:(q + 1) * 512] = 1.0
    return c, em, bd


def _owner(m):
    if m < 4:
        return m, 0
    if m < 8:
        return 7 - m, 1
    if m < 12:
        return m - 8, 2
    return 15 - m, 3


class _Stop(Exception):
    pass


def build(npool=NPOOL, stop=None):
    nc = bass.Bass("TRN2", target_bir_lowering=False)

    def din(name, shape, dt=F32):
        return nc.dram_tensor(name, list(shape), dt, kind="ExternalInput").ap()

    def dout(name, shape, dt=F32):
        return nc.dram_tensor(name, list(shape), dt, kind="ExternalOutput").ap()

    def dscr(name, shape, dt=F32):
        return nc.dram_tensor(name, list(shape), dt).ap()

    xp = din("xp", [1024, D])
    xs = din("xs", [4, D])
    ck = din("ck", [2 * npool * 128, 1024])
    cv = din("cv", [2 * npool * 128, 1024])
    sg = din("sg", [2, 4, 128, 256])
    ptin = din("pt", [1, 128], I32)
    meta = din("meta", [1, 64], I32)
    norm_g = din("norm_g", [2, D])
    w_in = din("w_in", [2, D, NIN])
    w_gate2 = din("w_gate2", [2, 16, 512])
    b_gate = din("b_gate", [2, 512])
    gla_g = din("gla_g", [2, 1024])
    w_a = din("w_a", [2, 1024, D])
    w_b = din("w_b", [2, 1024, D])
    b_merge = din("b_merge", [2, 4096])
    w_out = din("w_out", [2, D, D])
    fng = din("fng", [1, D])
    cst = din("cst", [128, 1024])
    emc = din("emc", [64, 64 * 128])
    bdc = din("bdc", [4, 2048])
    tab = din("tab", [128, 160])

    yp = dout("yp", [1024, D])
    ys = dout("ys", [4, D])
    kp = dout("kp", [2, 1024, 1024])
    vp = dout("vp", [2, 1024, 1024])
    gp = dout("gp", [2, 4, 128, 256])
    ks = dout("ks", [2, 4, 1024])
    vs = dout("vs", [2, 4, 1024])
    gs = dout("gs", [2, 4, 128, 256])

    xscr = dscr("xscr", [1028, D])
    gscr = dscr("gscr", [516, 4096], BF16)
    oscr = dscr("oscr", [4, 1024])
    EKN = 8 * 128 * 512
    EVN = 512 * 1024
    E2S = 128 * 4 * 256
    ESN = 2 * E2S
    EMN = 2 * 512 + 2 * 1024
    LR = [(l, r) for l in range(2) for r in range(2)]
    ekin = {lr: dscr("ekin%d%d" % lr, [EKN], BF16) for lr in LR}
    ekout = {lr: dscr("ekout%d%d" % lr, [4 * EKN], BF16) for lr in LR}
    evin = {lr: dscr("evin%d%d" % lr, [EVN], BF16) for lr in LR}
    evout = {lr: dscr("evout%d%d" % lr, [4 * EVN], BF16) for lr in LR}
    esin = {lr: dscr("esin%d%d" % lr, [ESN]) for lr in LR}
    esout = {lr: dscr("esout%d%d" % lr, [4 * ESN]) for lr in LR}
    emin = {lr: dscr("emin%d%d" % lr, [EMN]) for lr in LR}
    emout = {lr: dscr("emout%d%d" % lr, [4 * EMN]) for lr in LR}

    with ExitStack() as st:
        g = Graph(nc, st)

        TOTAL = (nc.sbuf_bytes_remaining // 64) * 64 - 64
        big = st.enter_context(nc.sbuf_tensor("big", [128, TOTAL // 2], BF16))

        class Reg:
            def __init__(self, base, size):
                self.base, self.size, self.cur = base, size, base

            def reset(self):
                self.cur = self.base

        PSZ, RSZ = 47104, 81920
        RP = Reg(0, PSZ)
        RR = Reg(PSZ, RSZ)
        RX = Reg(PSZ + RSZ, TOTAL - PSZ - RSZ)
        DTS = {F32: 4, BF16: 2, I32: 4, U32: 4}

        def sb(name, shape, dt, reg=None):
            reg = reg or RP
            n = 1
            for d_ in shape[1:]:
                n *= d_
            nb = n * DTS[dt]
            nb_al = (nb + 31) // 32 * 32
            off = reg.cur
            reg.cur += nb_al
            assert reg.cur <= reg.base + reg.size, (name, reg.cur, reg.base + reg.size)
            ap = big[0:shape[0], off // 2:(off + nb) // 2]
            if dt != BF16:
                ap = ap.bitcast(dt)
            if len(shape) == 3:
                ap = ap.rearrange("p (a b) -> p a b", a=shape[1])
            return ap


        bufs = {}

        def B(*key):
            if key not in bufs:
                bufs[key] = Buf(str(key))
            return bufs[key]

        wlc = [0]

        def load(eng, out, in_, dst, reads=()):
            if dst.lt is None:
                dst.lt = {}
            if eng not in dst.lt:
                dst.lt[eng] = g.track()
            if _DEV.get("nowl") and eng == "pool" and dst.name.startswith("('wbuf'"):
                wlc[0] += 1
                if wlc[0] > 2:
                    return
            g.add(eng, lambda e: e.dma_start(out=out, in_=in_, allow_slow_non_contiguous=True), reads=list(reads), writes=[dst], dma=dst.lt[eng])

        def store(eng, out, in_, src, dst):
            if src.st is None:
                src.st = {}
            if eng not in src.st:
                src.st[eng] = g.track()
            g.add(eng, lambda e: e.dma_start(out=out, in_=in_, allow_slow_non_contiguous=True), reads=[src], writes=[dst], dma=src.st[eng])

        def mm(out, lhsT, rhs, start, stop, reads, writes):
            g.add("pe", lambda e: e.matmul(out, lhsT, rhs, start=start, stop=stop), reads=reads, writes=writes)

        def tr(out, in_, ident_ap, reads, writes):
            g.add("pe", lambda e: e.transpose(out, in_, ident_ap), reads=reads, writes=writes)

        def act(out, in_, func, reads, writes, bias=None, scale=None, accum=None):
            kw = {}
            if bias is not None:
                kw["bias"] = bias
            if scale is not None:
                kw["scale"] = scale
            if accum is not None:
                kw["accum_out"] = accum
            g.add("act", lambda e: e.activation(out=out, in_=in_, func=func, **kw), reads=reads, writes=writes)

        def dve(fn, reads, writes, eng="dve"):
            g.add(eng, fn, reads=reads, writes=writes)

        def cp(out, in_, reads, writes, eng="dve"):
            g.add(eng, lambda e: e.tensor_copy(out=out, in_=in_), reads=reads, writes=writes)

        def barrier():
            extra = set(g.last[e_] for e_ in ("pe", "act", "dve") if g.last[e_] is not None)
            for t_ in g.tracks:
                if t_.inc == 16 and t_.last_op is not None:
                    extra.add(t_.last_op)
            for e_ in Graph.ENGS:
                g.add(e_, None, extra=extra)

        psum = [st.enter_context(nc.psum_tensor("ps%d" % i, [128, 512], F32)) for i in range(8)]
        psb = [B("ps", i) for i in range(8)]
        rot = [0]

        def ps_next():
            i = rot[0]
            rot[0] = (rot[0] + 1) % 5
            return psum[i], psb[i]

        cf = sb("cf", [128, 1024], F32)
        cb = sb("cb", [128, 528], BF16)
        emb = sb("emb", [16, 16 * 128], BF16)
        bdb = sb("bdb", [4, 2048], BF16)
        tabs = sb("tabs", [128, 160], F32)
        ident_b = cb[:, 0:128]
        tri_b = cb[:, 128:256]
        ones_b = cb[:, 384:512]
        ident_f = cf[:, 0:128]
        trineg_f = cf[:, 256:384]
        iota_f = cf[:, 512:513]
        BC = B("const")
        load("sp", cf, cst[:, :], BC)
        load("pool", cb, cst[:, 0:528], BC)
        load("pool", emb, emc[0:16, 0:2048], BC)
        load("pool", bdb, bdc[:, :], BC)
        load("sp", tabs, tab[:, :], BC)
        gcol = sb("gcol", [128, 2, KC], F32)
        load("sp", gcol, norm_g.rearrange("l (k p) -> p l k", p=128), BC)
        gnrep = sb("gnrep", [128, 1024], F32)
        wg2 = sb("wg2", [17, 2, 512], F32)
        load("sp", wg2[0:16, :, :], w_gate2.rearrange("l r n -> r l n"), BC)
        load("sp", wg2[16:17, :, :], b_gate.rearrange("(o l) n -> o l n", o=1), BC)
        pidx = sb("pidx", [128, 2, 128], I32)
        S_run = sb("S_run", [128, 4, 256], F32)
        S_st = [sb("S_st%d" % i, [128, 4, 256], F32) for i in range(2)]
        S_bf = sb("S_bf", [128, 4, 256], BF16)
        S_s = sb("S_s", [128, 4, 256], F32)
        meansf = sb("meansf", [128, 16, 8], F32)
        meansb = sb("meansb", [128, 16, 8], BF16)
        mown = sb("mown", [128, 2, 8], F32)
        edec = sb("edec", [128, 5, 4], F32)
        ssq = sb("ssq", [128, 1], F32)
        rstd = sb("rstd", [128, 1], F32)
        epsc = sb("epsc", [128, 1], F32)
        dve(lambda e: e.memset(epsc, EPS), [], [B("ssq")])
        rec = sb("rec", [128, 1], F32)
        atot = sb("atot", [128, 4], F32)
        ssq4 = sb("ssq4", [128, 4], F32)
        smv4 = [sb("smv%d" % i, [128, 16], F32) for i in range(4)]
        top84 = [sb("top8%d" % i, [128, 8], F32) for i in range(4)]
        selb4 = [sb("selb%d" % i, [128, 16], F32) for i in range(4)]
        selbb4 = [sb("selbb%d" % i, [128, 16], BF16) for i in range(4)]
        selT4 = [sb("selT%d" % i, [16, 128], BF16) for i in range(4)]
        rec2 = [sb("rec2_%d" % i, [128, 1], F32) for i in range(2)]

        NT = 516
        xnT = sb("xnT", [128, KC, NT], BF16, RR)
        qT = sb("qT", [128, 8, NT], BF16, RR)
        obT = sb("obT", [128, 8, NT], BF16, RR)
        qi = sb("qi", [128, 4, NT], BF16, RR)
        ki = sb("ki", [128, 4, NT], BF16, RR)
        kd = sb("kd", [128, 5, 512], BF16, RR)
        vbr = sb("vbr", [128, 5, 1024], BF16, RR)
        sza = sb("sza", [128, 5, 1024], BF16, RR)
        szb = sb("szb", [128, 5, 1024], BF16, RR)
        alrT = sb("alrT", [17, NT], F32, RR)
        ksT = sb("ksT", [128, 8, 4], BF16, RR)
        dve(lambda e: e.memset(alrT, 1.0), [], [B("alrT", ti_) for ti_ in range(5)])
        vsr = sb("vsr", [4, 1024], BF16, RR)

        RX.reset()
        wbuf = [sb("wbuf%d" % i, [128, KC * 512], BF16, RX) for i in range(2)]
        wB = [B("wbuf", i) for i in range(2)]
        wrot = [0]
        qbT = sb("qbT", [128, 4, NT], BF16, RX)
        kbT = sb("kbT", [128, 4, NT], BF16, RX)
        xt = [sb("xt0", [128, D], F32, RX)] * 2
        xnb = sb("xnb", [128, D], BF16, RX)
        junk = xnb
        ef32 = [sb("ef32_%d" % i, [128, 512], F32, RX) for i in range(2)]
        eb16 = [sb("eb16_%d" % i, [128, 512], BF16, RX) for i in range(2)]
        ktst = [sb("ktst%d" % i, [128, 4, 128], BF16, RX) for i in range(2)]
        bms = [sb("bms%d" % i, [1, 512], BF16, RX) for i in range(2)]
        gtile = sb("gtile", [128, 512], F32, RX)
        bTt = sb("bTt", [128, 4, 128], F32, RX)
        eb_ = sb("eb_", [128, 4, 128], F32, RX)
        enb = sb("enb", [128, 4, 128], F32, RX)
        ekd = sb("ekd", [128, 4, 128], F32, RX)
        kdT = sb("kdT", [128, 4, 128], BF16, RX)
        Sl = sb("Sl", [128, 4, 256], F32, RX)
        ecnt = [0]
        cnt2 = [0]
        ocnt = [0]
        pend2 = []
        pti = sb("pti", [128, 128], I32, RX)
        ptf = sb("ptf", [128, 2, 128], F32, RX)
        RX.reset()
        kpgb = [sb("kpg%d" % i, [128, 2, 1024], BF16, RX) for i in range(2)]
        kTs = [sb("kTs%d" % i, [128, 16, 128], BF16, RX) for i in range(2)]
        PallT = sb("PallT", [128, 128, 32], BF16, RX)
        means_s = sb("means_s", [128, 512], BF16, RX)
        sms = sb("sms", [4, 512], F32, RX)
        top8s = sb("top8s", [4, 64], F32, RX)
        sel01b = sb("sel01b", [4, 512], BF16, RX)
        rhsbd = sb("rhsbd", [4, 4, 512], BF16, RX)
        maskrep = sb("maskrep", [128, 4, 512], BF16, RX)
        pown = sb("pown", [4, 32], F32, RX)
        pownb = sb("pownb", [4, 32], BF16, RX)
        osb = sb("osb", [32, 1024], F32, RX)
        ost = sb("ost", [4, 1024], F32, RX)
        oasb = sb("oasb", [4, 1024], BF16, RX)
        RX.reset()
        slb = [sb("slb%d" % i, [128, 4, 256], F32, RX) for i in range(2)]
        atg = [sb("atg%d" % i, [128, 4], F32, RX) for i in range(2)]
        att = [sb("att%d" % i, [128, 128], BF16, RX) for i in range(2)]
        otmp = sb("otmp", [128, 256], F32, RX)
        obst = sb("obst", [128, 1024], BF16, RX)
        junk2 = sb("junk2", [128, 256], BF16, RX)
        kbuf = [sb("kbuf%d" % i, [128, 17 * 256], BF16, RX) for i in range(2)]
        vbuf = [sb("vbuf%d" % i, [128, 34, 129], BF16, RX) for i in range(2)]
        ptt = [sb("ptt%d" % i, [128, 512], BF16, RX) for i in range(4)]
        oat2 = [sb("oat%d" % i, [128, 128], BF16, RX) for i in range(2)]
        RX.reset()
        wbufD = [sb("wbufD%d" % i, [128, KC * 512], BF16, RX) for i in range(2)]
        gab = [sb("gab%d" % i, [128, 2, 512], BF16, RX) for i in range(2)]
        mst2 = [sb("mst%d" % i, [128, 512], BF16, RX) for i in range(2)]
        t1 = sb("t1", [128, 512], F32, RX)
        t2 = sb("t2", [128, 512], F32, RX)
        xres = [sb("xres%d" % i, [128, 512], F32, RX) for i in range(2)]
        RX.reset()
        xtF = sb("xtF", [128, D], F32, RX)
        fngrep = sb("fngrep", [128, D], F32, RX)
        junkF = sb("junkF", [128, D], BF16, RX)

        BPI = B("pidx")
        load("sp", pti, ptin.broadcast_to([128, 128]), BPI)
        cp(ptf[:, 0, :], pti, [BPI], [BPI])
        dve(lambda e: e.tensor_scalar(out=ptf[:, 0, :], in0=ptf[:, 0, :], scalar1=128.0, scalar2=iota_f,
                                      op0=ALU.mult, op1=ALU.add), [BC, BPI], [BPI])
        dve(lambda e: e.tensor_scalar(out=ptf[:, 1, :], in0=ptf[:, 0, :], scalar1=float(npool * 128), scalar2=None,
                                      op0=ALU.add), [BC, BPI], [BPI])
        cp(pidx, ptf, [BPI], [BPI])

        ONE = B("one")

        def rms_rstd(dst_rstd, src_ap, n, tsz, reads, tmpjunk, jb):
            act(tmpjunk, src_ap, AF.Square, reads, [jb, B("ssq")], accum=ssq[:tsz, :])
            act(dst_rstd, ssq[:tsz, :], AF.Sqrt, [B("ssq")], [B("rstd")], scale=1.0 / n, bias=epsc[:tsz, :])
            dve(lambda e: e.reciprocal(out=dst_rstd, in_=dst_rstd), [B("rstd")], [B("rstd")])

        try:
            if stop == "INIT":
                raise _Stop()
            for l in range(2):
                dve(lambda e: e.memset(S_run[:], 0.0), [], [B("S_run")])
                dve(lambda e: e.memset(meansf, 0.0), [], [B("meansf")])
                load("sp", gnrep, gla_g[l:l + 1, :].broadcast_to([128, 1024]), B("gnrep"))
                load("sp", S_s[:], sg[l].rearrange("h k v -> k h v"), B("S_s"))
                for rd in range(2):
                    tiles = [(rd * 512 + i * 128, i * 128, 128, False) for i in range(4)]
                    if rd == 0:
                        tiles.append((0, 512, 4, True))
                    nt = len(tiles)
                    xsrc = (lambda r0, n: xp[r0:r0 + n, :]) if l == 0 else (lambda r0, n: xscr[r0:r0 + n, :])
                    xssrc = xs[:, :] if l == 0 else xscr[1024:1028, :]

                    barrier()
                    for ti, (r0, c0, tsz, smp) in enumerate(tiles):
                        xb_ = xt[ti % 2]
                        XB = B("xt", 0)
                        load("sp", xb_[:tsz, :], xssrc if smp else xsrc(r0, tsz), XB,
                             reads=([B("xscr", "s") if smp else B("xscr", r0)] if l == 1 else []))
                        rms_rstd(rstd[:tsz, :], xb_[:tsz, :], D, tsz, [XB], junk[:tsz, :], B("xnb"))
                        act(xnb[:tsz, :], xb_[:tsz, :], AF.Copy, [XB, B("rstd")], [B("xnb")], scale=rstd[:tsz, :])
                        for half in range(2):
                            pt_, PB = ps_next()
                            pv = pt_[:].bitcast(BF16)
                            for kk in range(8):
                                k = half * 8 + kk
                                tr(pv[:, kk * 128: kk * 128 + tsz], xnb[:tsz, k * 128:(k + 1) * 128],
                                   ident_b[:tsz, :tsz], [B("xnb"), BC], [PB])
                            for kk in range(8):
                                k = half * 8 + kk
                                if half == 0:
                                    act(xnT[:, k, c0:c0 + tsz], pv[:, kk * 128: kk * 128 + tsz], AF.Copy,
                                        [PB, BC], [B("xnT", ti)], scale=gcol[:, l, k:k + 1])
                                else:
                                    dve(lambda e, k=k, kk=kk, pv=pv, c0=c0, tsz=tsz, l=l: e.tensor_scalar(
                                        out=xnT[:, k, c0:c0 + tsz], in0=pv[:, kk * 128: kk * 128 + tsz],
                                        scalar1=gcol[:, l, k:k + 1], scalar2=None, op0=ALU.mult),
                                        [PB, BC], [B("xnT", ti)])

                    if stop == "P1":
                        raise _Stop()
                    chunks = []
                    for i in range(2):
                        chunks.append((i * 512, 512, "q", i))
                    for i in range(2):
                        chunks.append((1024 + i * 512, 512, "k", i))
                    for i in range(2):
                        chunks.append((2048 + i * 512, 512, "v", i))
                    for i in range(2):
                        chunks.append((3072 + i * 512, 512, "za", i))
                    chunks.append((4096, 512, "qb", 0))
                    chunks.append((4608, 512, "kb", 0))
                    for i in range(2):
                        chunks.append((5120 + i * 512, 512, "vb", i))
                    for i in range(2):
                        chunks.append((6144 + i * 512, 512, "zb", i))
                    chunks.append((7168, 16, "alr", 0))
                    for i in range(8):
                        chunks.append((7184 + i * 512, 512, "g", i))
                    wv = w_in[l].rearrange("(k p) n -> p k n", p=128)
                    e1kT = ekin[(l, rd)].rearrange("(h d t) -> h d t", h=8, d=128)
                    e1V = evin[(l, rd)].rearrange("(t n) -> t n", n=1024)
                    EKB = B("ekin", l, rd)
                    EVB = B("evin", l, rd)
                    for (col0, w, kind, sub) in chunks:
                        if _DEV.get("kinds") and kind not in _DEV["kinds"]:
                            continue
                        ws = wrot[0] % 2
                        wrot[0] += 1
                        wt = wbuf[ws][:, 0:KC * w].rearrange("p (k n) -> p k n", k=KC)
                        load("pool", wt, wv[:, :, col0:col0 + w], wB[ws])
                        if kind == "g":
                            load("pool", bms[sub % 2], b_merge[l:l + 1, sub * 512:(sub + 1) * 512], B("bms", sub % 2))
                        for ti, (r0, c0, tsz, smp) in enumerate(tiles):
                            pt_, PB = ps_next()
                            po = pt_[:tsz, 0:w]
                            for k in range(KC):
                                mm(po, xnT[:, k, c0:c0 + tsz], wt[:, k, :], k == 0, (k == KC - 1) and kind != "g",
                                   [B("xnT", ti), wB[ws]], [PB])
                            if kind == "g":
                                mm(po, ones_b[0:1, :tsz], bms[sub % 2][0:1, :], False, True,
                                   [BC, B("bms", sub % 2)], [PB])
                            while pend2:
                                pend2.pop(0)()
                            es = ecnt[0] % 2
                            ecnt[0] += 1
                            EF, EB = B("ef32", es), B("eb16", es)
                            f32t, b16t = ef32[es], eb16[es]
                            if kind in ("q", "qb", "kb"):
                                act(b16t[:tsz, :], po, AF.Copy, [PB], [EB])
                                if kind == "q":
                                    dstap = qT[:, sub * 4:(sub + 1) * 4, c0:c0 + tsz]
                                    dB = [B("qT", ti, sub * 4 + hh) for hh in range(4)]
                                elif kind == "qb":
                                    dstap = qbT[:, :, c0:c0 + tsz]
                                    dB = [B("qbT", ti)]
                                else:
                                    dstap = kbT[:, :, c0:c0 + tsz]
                                    dB = [B("kbT", ti)]

                                def fin_tr(b16t=b16t, tsz=tsz, EB=EB, dstap=dstap, dB=dB):
                                    p2, PB2 = ps_next()
                                    pv = p2[:].bitcast(BF16)
                                    for hh in range(4):
                                        tr(pv[:, hh * 128: hh * 128 + tsz], b16t[:tsz, hh * 128:(hh + 1) * 128],
                                           ident_b[:tsz, :tsz], [EB, BC], [PB2])
                                    pv4 = pv[:, 0:512].rearrange("p (h t) -> p h t", h=4)[:, :, 0:tsz]
                                    cp(dstap, pv4, [PB2], dB)
                                pend2.append(fin_tr)
                            elif kind == "k":
                                cp(f32t[:tsz, :], po, [PB], [EF])
                                act(b16t[:tsz, :], f32t[:tsz, :], AF.Copy, [EF], [EB])
                                if smp:
                                    store("sp", ks[l][:, sub * 512:(sub + 1) * 512], f32t[:tsz, :], EF, B("ks", l))
                                else:
                                    store("sp", kp[l][r0:r0 + tsz, sub * 512:(sub + 1) * 512], f32t[:tsz, :], EF,
                                          B("kp", l))
                                p2, PB2 = ps_next()
                                pv = p2[:].bitcast(BF16)
                                for hh in range(4):
                                    tr(pv[:, hh * 128: hh * 128 + tsz], b16t[:tsz, hh * 128:(hh + 1) * 128],
                                       ident_b[:tsz, :tsz], [EB, BC], [PB2])
                                pv4 = pv[:, 0:512].rearrange("p (h t) -> p h t", h=4)[:, :, 0:tsz]
                                if smp:
                                    cp(ksT[:, sub * 4:(sub + 1) * 4, :], pv4, [PB2], [B("ksT")])
                                else:
                                    kt_ = ktst[es]
                                    KT = B("ktst", es)
                                    cp(kt_[:], pv4, [PB2], [KT])
                                    store("sp", e1kT[sub * 4:(sub + 1) * 4, :, c0:c0 + 128].rearrange("h d t -> d h t"),
                                          kt_[:], KT, EKB)
                                    bs = ti // 2
                                    if ti % 2 == 0:
                                        dve(lambda e, kt_=kt_, sub=sub, bs=bs: e.tensor_reduce(
                                            out=mown[:, bs, sub * 4:(sub + 1) * 4], in_=kt_[:], axis=AX.X, op=ALU.add),
                                            [KT], [B("mown")])
                                    else:
                                        dve(lambda e, kt_=kt_: e.tensor_reduce(
                                            out=ssq4[:], in_=kt_[:], axis=AX.X, op=ALU.add), [KT], [B("ssq4")])
                                        dve(lambda e, sub=sub, bs=bs: e.tensor_tensor(
                                            out=mown[:, bs, sub * 4:(sub + 1) * 4], in0=mown[:, bs, sub * 4:(sub + 1) * 4],
                                            in1=ssq4[:], op=ALU.add), [B("ssq4"), B("mown")], [B("mown")])
                            elif kind == "v":
                                cp(f32t[:tsz, :], po, [PB], [EF])
                                if smp:
                                    act(vsr[:, sub * 512:(sub + 1) * 512], f32t[:tsz, :], AF.Copy, [EF], [B("vsr")])
                                    if not _DEV.get("no_vs"):
                                        store("sp", vs[l][:, sub * 512:(sub + 1) * 512], f32t[:tsz, :], EF, B("vs", l))
                                else:
                                    act(b16t[:tsz, :], f32t[:tsz, :], AF.Copy, [EF], [EB])
                                    if not _DEV.get("no_vp"):
                                        store("sp", vp[l][r0:r0 + tsz, sub * 512:(sub + 1) * 512], f32t[:tsz, :], EF,
                                              B("vp", l))
                                    if not _DEV.get("no_e1"):
                                        store("sp", e1V[c0:c0 + tsz, sub * 512:(sub + 1) * 512], b16t[:tsz, :], EB, EVB)
                            elif kind == "za":
                                act(sza[:tsz, ti, sub * 512:(sub + 1) * 512], po, AF.Silu, [PB], [B("sza", ti)])
                            elif kind == "zb":
                                act(szb[:tsz, ti, sub * 512:(sub + 1) * 512], po, AF.Silu, [PB], [B("szb", ti)])
                            elif kind == "vb":
                                act(vbr[:tsz, ti, sub * 512:(sub + 1) * 512], po, AF.Copy, [PB], [B("vbr", ti)])
                            elif kind == "alr":
                                cp(f32t[:tsz, 0:16], po, [PB], [EF])
                                p2, PB2 = ps_next()
                                tr(p2[0:16, 0:tsz], f32t[:tsz, 0:16], ident_f[:tsz, :tsz], [EF, BC], [PB2])
                                cp(alrT[0:16, c0:c0 + tsz], p2[0:16, 0:tsz], [PB2], [B("alrT", ti)])
                            elif kind == "g":
                                act(b16t[:tsz, :], po, AF.Sigmoid, [PB], [EB])
                                store("sp", gscr[c0:c0 + tsz, sub * 512:(sub + 1) * 512], b16t[:tsz, :], EB, B("gscr"))

                    while pend2:
                        pend2.pop(0)()
                    if stop == "P2":
                        raise _Stop()
                    e2s = esin[(l, rd)]
                    e2m = emin[(l, rd)]
                    ESB = B("esin", l, rd)
                    EMB = B("emin", l, rd)
                    for ti, (r0, c0, tsz, smp) in enumerate(tiles):
                        if ti == 0 and rd == 0 and l == 0:
                            pass
                        pu, PU = ps_next()
                        mm(pu[:tsz, :], alrT[0:17, c0:c0 + tsz], wg2[0:17, l, :], True, True, [B("alrT", ti), BC], [PU])
                        gt = gtile
                        act(gt[:tsz, :], pu[:tsz, :], AF.Exp, [PU], [B("gt")], scale=-1.0)
                        act(gt[:tsz, :], gt[:tsz, :], AF.Ln, [B("gt")], [B("gt")], bias=1.0)
                        pb_, PBb = ps_next()
                        pb4 = pb_[:, :].rearrange("p (h t) -> p h t", h=4)
                        for h in range(4):
                            mm(pb4[:, h, 0:tsz], gt[:tsz, h * 128:(h + 1) * 128], trineg_f[:tsz, :tsz], True, True,
                               [B("gt"), BC], [PBb])
                        cp(bTt[:, :, 0:tsz], pb4[:, :, 0:tsz], [PBb], [B("bTt")])
                        act(eb_[:, :, 0:tsz], bTt[:, :, 0:tsz], AF.Exp, [B("bTt")], [B("eb")])
                        act(enb[:, :, 0:tsz], bTt[:, :, 0:tsz], AF.Exp, [B("bTt")], [B("enb")], scale=-1.0)
                        for h in range(4):
                            act(ekd[:, h, 0:tsz], bTt[:, h, 0:tsz], AF.Exp, [B("bTt")], [B("ekd")], scale=-1.0,
                                bias=bTt[:, h, tsz - 1:tsz])
                        act(edec[:, ti, :], bTt[:, :, tsz - 1], AF.Exp, [B("bTt")], [B("edec", ti)])
                        dve(lambda e, c0=c0, tsz=tsz: e.scalar_tensor_tensor(
                            out=qi[:, :, c0:c0 + tsz], in0=qbT[:, :, c0:c0 + tsz], scalar=SCALE, in1=eb_[:, :, 0:tsz],
                            op0=ALU.mult, op1=ALU.mult), [B("qbT", ti), B("eb")], [B("qi", ti)])
                        dve(lambda e, c0=c0, tsz=tsz: e.tensor_tensor(
                            out=ki[:, :, c0:c0 + tsz], in0=kbT[:, :, c0:c0 + tsz], in1=enb[:, :, 0:tsz], op=ALU.mult),
                            [B("kbT", ti), B("enb")], [B("ki", ti)])
                        dve(lambda e, c0=c0, tsz=tsz: e.tensor_tensor(
                            out=kdT[:, :, 0:tsz], in0=kbT[:, :, c0:c0 + tsz], in1=ekd[:, :, 0:tsz], op=ALU.mult),
                            [B("kbT", ti), B("ekd")], [B("kdT")])
                        p2, PB2 = ps_next()
                        pv = p2[:].bitcast(BF16)
                        for h in range(4):
                            tr(pv[:tsz, h * 128:(h + 1) * 128], kdT[:, h, 0:tsz], ident_b[:, :], [B("kdT"), BC], [PB2])
                        cp(kd[:tsz, ti, :], pv[:tsz, 0:512], [PB2], [B("kd", ti)])
                        if smp:
                            continue
                        first = (ti % 2 == 0)
                        for half in range(2):
                            ps_, PS_ = ps_next()
                            for hh in range(2):
                                h = half * 2 + hh
                                mm(ps_[:, hh * 256:(hh + 1) * 256], kd[:tsz, ti, h * 128:(h + 1) * 128],
                                   vbr[:tsz, ti, h * 256:(h + 1) * 256], True, True, [B("kd", ti), B("vbr", ti)], [PS_])
                            for hh in range(2):
                                h = half * 2 + hh
                                if first:
                                    cp(Sl[:, h, :], ps_[:, hh * 256:(hh + 1) * 256], [PS_], [B("Sl")])
                                else:
                                    dve(lambda e, h=h, hh=hh, ps_=ps_, ti=ti: e.scalar_tensor_tensor(
                                        out=Sl[:, h, :], in0=Sl[:, h, :], scalar=edec[:, ti, h:h + 1],
                                        in1=ps_[:, hh * 256:(hh + 1) * 256], op0=ALU.mult, op1=ALU.add),
                                        [PS_, B("Sl"), B("edec", ti)], [B("Sl")])
                        if not first:
                            bs = ti // 2
                            store("sp", e2s[bs * E2S:(bs + 1) * E2S].rearrange("(p h v) -> p h v", p=128, h=4),
                                  Sl[:], B("Sl"), ESB)
                            dve(lambda e, ti=ti: e.tensor_tensor(out=atot[:], in0=edec[:, ti - 1, :], in1=edec[:, ti, :],
                                                                 op=ALU.mult), [B("edec", ti - 1), B("edec", ti)], [B("atot")])
                            store("sp", e2m[bs * 512:(bs + 1) * 512].rearrange("(p h) -> p h", p=128),
                                  atot[:], B("atot"), EMB)
                    store("sp", e2m[1024:EMN].rearrange("(p s h) -> p s h", p=128, s=2), mown[:], B("mown"), EMB)

                    if stop == "P3":
                        raise _Stop()
                    for (src, dst, SB_, DB_) in ((emin[(l, rd)], emout[(l, rd)], EMB, B("emout", l, rd)),
                                                (esin[(l, rd)], esout[(l, rd)], ESB, B("esout", l, rd)),
                                                (ekin[(l, rd)], ekout[(l, rd)], EKB, B("ekout", l, rd)),
                                                (evin[(l, rd)], evout[(l, rd)], EVB, B("evout", l, rd))):
                        trk = g.track(inc=1)
                        g.add("pool", lambda e, src=src, dst=dst: e.collective_compute(
                            "AllGather", ALU.bypass, replica_groups=[[0, 1, 2, 3], [4, 5, 6, 7]], ins=[src.opt()], outs=[dst.opt()]),
                            reads=[SB_], writes=[DB_], dma=trk)

                    eko = [ekout[(l, r_)].rearrange("(r n) -> r n", r=4) for r_ in range(2)]
                    evo = [evout[(l, r_)].rearrange("(r n) -> r n", r=4) for r_ in range(2)]
                    eso = esout[(l, rd)].rearrange("(r n) -> r n", r=4)
                    emo = emout[(l, rd)].rearrange("(r n) -> r n", r=4)
                    EKO = [B("ekout", l, r_) for r_ in range(2)]
                    EVO = [B("evout", l, r_) for r_ in range(2)]
                    ESO = B("esout", l, rd)
                    EMO = B("emout", l, rd)

                    if stop == "EX":
                        raise _Stop()
                    barrier()
                    if rd == 0:
                        QS = [B("qT", 4, h) for h in range(8)]
                        MP, MPB = psum[7], psb[7]
                        for n in range(_DEV.get("nsb", 64)):
                            kpg = kpgb[n % 2]
                            KPG = B("kpg", n % 2)
                            for i in range(2):
                                pg = 2 * n + i
                                if KPG.lt is None:
                                    KPG.lt = {"pool": g.track()}
                                g.add("pool", lambda e, kpg=kpg, i=i, pg=pg, l=l: e.indirect_dma_start(
                                    out=kpg[:, i, :], out_offset=None, in_=ck[:, :],
                                    in_offset=bass.IndirectOffsetOnAxis(ap=pidx[:, l, pg:pg + 1], axis=0)),
                                    reads=[BPI], writes=[KPG], dma=KPG.lt["pool"])
                            for h in range(8):
                                for i in range(2):
                                    mm(MP[:, h * 64 + n: h * 64 + n + 1], kpg[:, i, h * 128:(h + 1) * 128], ones_b[:, 0:1],
                                       i == 0, i == 1, [KPG, BC], [MPB])
                            kts = kTs[n % 2]
                            KTS = B("kTs", n % 2)
                            for i in range(2):
                                p2, PB2 = ps_next()
                                pv = p2[:].bitcast(BF16)
                                for h in range(8):
                                    tr(pv[:, h * 128:(h + 1) * 128], kpg[:, i, h * 128:(h + 1) * 128], ident_b[:, :],
                                       [KPG, BC], [PB2])
                                cp(kts[:, i * 8:(i + 1) * 8, :], pv[:, 0:1024].rearrange("p (h t) -> p h t", h=8), [PB2], [KTS])
                            p3, PB3 = ps_next()
                            for i in range(2):
                                for h in range(8):
                                    mm(p3[:, i * 32 + h * 4: i * 32 + h * 4 + 4], kts[:, i * 8 + h, :], qT[:, h, 512:516],
                                       True, True, [KTS, QS[h]], [PB3])
                            act(PallT[:, 2 * n:2 * n + 2, :], p3[:, 0:64].rearrange("p (i c) -> p i c", i=2), AF.Exp,
                                [PB3], [B("PallT")], scale=SCALE)
                        cp(means_s[:], MP[:, :], [MPB], [B("means_s")])
                        p4, PB4 = ps_next()
                        for h in range(8):
                            mm(p4[0:4, h * 64:(h + 1) * 64], qT[:, h, 512:516], means_s[:, h * 64:(h + 1) * 64], True, True,
                               [QS[h], B("means_s")], [PB4])
                        cp(sms[:], p4[0:4, :], [PB4], [B("sms")])
                        for h in range(8):
                            dve(lambda e, h=h: e.max(out=top8s[0:4, h * 8:(h + 1) * 8], in_=sms[0:4, h * 64:(h + 1) * 64]),
                                [B("sms")], [B("top8s")])
                        for h in range(8):
                            dve(lambda e, h=h: e.tensor_scalar(out=sel01b[0:4, h * 64:(h + 1) * 64],
                                                               in0=sms[0:4, h * 64:(h + 1) * 64],
                                                               scalar1=top8s[0:4, h * 8 + 2:h * 8 + 3], scalar2=None,
                                                               op0=ALU.is_ge), [B("sms"), B("top8s")], [B("sel01b")])
                        for q_ in range(4):
                            dve(lambda e, q_=q_: e.tensor_tensor(out=rhsbd[0:4, q_, :], in0=bdb[0:4, q_ * 512:(q_ + 1) * 512],
                                                                 in1=sel01b[0:4, :], op=ALU.mult), [B("sel01b"), BC],
                                [B("rhsbd")])
                        for q_ in range(4):
                            p5, PB5 = ps_next()
                            mm(p5[:, :], ones_b[0:4, :], rhsbd[0:4, q_, :], True, True, [B("rhsbd"), BC], [PB5])
                            cp(maskrep[:, q_, :], p5[:, :], [PB5], [B("maskrep")])
                        P5v = PallT[:, :, :].rearrange("p (n i) (h q) -> p n i h q", i=2, h=8)
                        Mv = maskrep[:, :, :].rearrange("p q (h n) -> p n h q", h=8)
                        for i in range(2):
                            for h in range(8):
                                dve(lambda e, i=i, h=h: e.tensor_tensor(out=P5v[:, :, i, h, :], in0=P5v[:, :, i, h, :],
                                                                         in1=Mv[:, :, h, :], op=ALU.mult),
                                    [B("PallT"), B("maskrep")], [B("PallT")])
                        OA, OAB = psum[5], psb[5]
                        OB_, OBB = psum[6], psb[6]
                        DN, DNB = psum[7], psb[7]
                        for n in range(_DEV.get("nsb", 64)):
                            vpg = kpgb[n % 2]
                            KPG = B("kpg", n % 2)
                            for i in range(2):
                                pg = 2 * n + i
                                g.add("pool", lambda e, vpg=vpg, i=i, pg=pg, l=l: e.indirect_dma_start(
                                    out=vpg[:, i, :], out_offset=None, in_=cv[:, :],
                                    in_offset=bass.IndirectOffsetOnAxis(ap=pidx[:, l, pg:pg + 1], axis=0)),
                                    reads=[BPI], writes=[KPG], dma=KPG.lt["pool"])
                            for i in range(2):
                                pg = 2 * n + i
                                mm(OA[0:32, :], PallT[:, pg, :], vpg[:, i, 0:512], pg == 0, False, [B("PallT"), KPG], [OAB])
                                mm(OB_[0:32, :], PallT[:, pg, :], vpg[:, i, 512:1024], pg == 0, False, [B("PallT"), KPG], [OBB])
                                mm(DN[0:32, 0:1], PallT[:, pg, :], ones_b[:, 0:1], pg == 0, False, [B("PallT"), BC], [DNB])
                        p6, PB6 = ps_next()
                        for h in range(8):
                            mm(p6[0:4, h * 4:(h + 1) * 4], ksT[:, h, :], qT[:, h, 512:516], True, True, [B("ksT"), QS[h]], [PB6])
                        act(pown[:], p6[0:4, 0:32], AF.Exp, [PB6], [B("pown")], scale=SCALE)
                        dve(lambda e: e.tensor_tensor(out=pownb[:], in0=pown[:], in1=cf[0:4, 784:816], op=ALU.mult),
                            [B("pown"), BC], [B("pownb")])
                        mm(OA[0:32, :], pownb[0:4, :], vsr[0:4, 0:512], False, True, [B("pownb"), B("vsr")], [OAB])
                        mm(OB_[0:32, :], pownb[0:4, :], vsr[0:4, 512:1024], False, True, [B("pownb"), B("vsr")], [OBB])
                        mm(DN[0:32, 0:1], pownb[0:4, :], ones_b[0:4, 0:1], False, True, [B("pownb"), BC], [DNB])
                        dve(lambda e: e.reciprocal(out=rec[0:32, :], in_=DN[0:32, 0:1]), [DNB], [B("rec")])
                        dve(lambda e: e.tensor_scalar(out=osb[:, 0:512], in0=OA[0:32, :], scalar1=rec[0:32, :], scalar2=None,
                                                      op0=ALU.mult), [OAB, B("rec")], [B("osb")])
                        dve(lambda e: e.tensor_scalar(out=osb[:, 512:1024], in0=OB_[0:32, :], scalar1=rec[0:32, :],
                                                      scalar2=None, op0=ALU.mult), [OBB, B("rec")], [B("osb")])
                        for h in range(8):
                            store("sp", oscr[:, h * 128:(h + 1) * 128], osb[4 * h:4 * h + 4, h * 128:(h + 1) * 128],
                                  B("osb"), B("oscr"))
                        load("sp", ost[:], oscr[:, :], B("ost"), reads=[B("oscr")])
                        dve(lambda e: e.tensor_tensor(out=oasb[:], in0=ost[:], in1=sza[0:4, 4, :], op=ALU.mult),
                            [B("ost"), B("sza", 4)], [B("oasb")])
                        p7, PB7 = ps_next()
                        pv = p7[:].bitcast(BF16)
                        for h in range(8):
                            tr(pv[:, h * 128:h * 128 + 4], oasb[0:4, h * 128:(h + 1) * 128], ident_b[0:4, 0:4],
                               [B("oasb"), BC], [PB7])
                        cp(qT[:, :, 512:516], pv[:, 0:1024].rearrange("p (h t) -> p h t", h=8)[:, :, 0:4], [PB7], QS)

                    if stop == "P4d":
                        raise _Stop()
                    if rd == 0:
                        barrier()
                    for gq in range(8):
                        m = rd * 8 + gq
                        jm, slot = _owner(m)
                        bs = slot % 2
                        for sl in range(2):
                            col = 128 + (rd * 2 + sl) * 8 + gq
                            if gq == 0:
                                dve(lambda e, sl=sl, col=col: e.tensor_scalar(out=S_st[sl][:], in0=S_run[:],
                                                                              scalar1=tabs[:, col:col + 1], scalar2=None,
                                                                              op0=ALU.mult), [B("S_run"), BC], [B("S_st", sl)])
                            else:
                                dve(lambda e, sl=sl, col=col: e.scalar_tensor_tensor(
                                    out=S_st[sl][:], in0=S_run[:], scalar=tabs[:, col:col + 1], in1=S_st[sl][:],
                                    op0=ALU.mult, op1=ALU.add), [B("S_run"), BC, B("S_st", sl)], [B("S_st", sl)])
                        sl_ = slb[gq % 2]
                        SLB = B("slb", gq % 2)
                        load("sp", sl_[:], eso[jm, bs * E2S:(bs + 1) * E2S].rearrange("(p h v) -> p h v", p=128, h=4), SLB,
                             reads=[ESO])
                        ag_ = atg[gq % 2]
                        AGB = B("atg", gq % 2)
                        load("sp", ag_[:], emo[jm, bs * 512:(bs + 1) * 512].rearrange("(p h) -> p h", p=128),
                             AGB, reads=[EMO])
                        for h in range(4):
                            dve(lambda e, h=h, sl_=sl_, ag_=ag_: e.scalar_tensor_tensor(
                                out=S_run[:, h, :], in0=S_run[:, h, :], scalar=ag_[:, h:h + 1], in1=sl_[:, h, :],
                                op0=ALU.mult, op1=ALU.add), [B("S_run"), SLB, AGB], [B("S_run")])
                        load("sp", meansf[:, m, :],
                             emo[jm, 1024:EMN].rearrange("(p s h) -> p s h", p=128, s=2)[:, bs, :], B("meansf"),
                             reads=[EMO])
                    if rd == 1:
                        store("sp", gp[l].rearrange("h k v -> k h v"), S_run[:], B("S_run"), B("gp"))
                    cp(meansb[:], meansf[:], [B("meansf")], [B("meansb")])

                    if stop == "P4a":
                        raise _Stop()
                    def gla_tile(ti, c0, tsz, Sf, SB_, update):
                        for h in range(4):
                            pa, PA = ps_next()
                            mm(pa[:tsz, 0:tsz], ki[:, h, c0:c0 + tsz], qi[:, h, c0:c0 + tsz], True, True,
                               [B("ki", ti), B("qi", ti)], [PA])
                            at_ = att[h % 2]
                            ATB = B("att", h % 2)
                            dve(lambda e, pa=pa, at_=at_: e.tensor_tensor(out=at_[:tsz, :tsz], in0=pa[:tsz, 0:tsz],
                                                                           in1=tri_b[:tsz, :tsz], op=ALU.mult),
                                [PA, BC], [ATB])
                            po_, PO = ps_next()
                            mm(po_[:tsz, 0:256], at_[:tsz, :tsz], vbr[:tsz, ti, h * 256:(h + 1) * 256], True, False,
                               [ATB, B("vbr", ti)], [PO])
                            mm(po_[:tsz, 0:256], qi[:, h, c0:c0 + tsz], S_bf[:, h, :], False, True,
                               [B("qi", ti), B("S_bf")], [PO])
                            rms_rstd(rstd[:tsz, :], po_[:tsz, 0:256], 256, tsz, [PO], junk2[:tsz, 0:256], B("junk2"))
                            dve(lambda e, po_=po_, h=h: e.scalar_tensor_tensor(
                                out=otmp[:tsz, :], in0=po_[:tsz, 0:256], scalar=rstd[:tsz, :],
                                in1=gnrep[:tsz, h * 256:(h + 1) * 256], op0=ALU.mult, op1=ALU.mult),
                                [PO, B("rstd"), B("gnrep")], [B("otmp")])
                            dve(lambda e, h=h: e.tensor_tensor(out=obst[:tsz, h * 256:(h + 1) * 256], in0=otmp[:tsz, :],
                                                               in1=szb[:tsz, ti, h * 256:(h + 1) * 256], op=ALU.mult),
                                [B("otmp"), B("szb", ti)], [B("obst")])
                            if update:
                                pn, PN = ps_next()
                                mm(pn[:, 0:256], kd[:tsz, ti, h * 128:(h + 1) * 128], vbr[:tsz, ti, h * 256:(h + 1) * 256],
                                   True, True, [B("kd", ti), B("vbr", ti)], [PN])
                                dve(lambda e, pn=pn, h=h: e.scalar_tensor_tensor(
                                    out=Sf[:, h, :], in0=Sf[:, h, :], scalar=edec[:, ti, h:h + 1], in1=pn[:, 0:256],
                                    op0=ALU.mult, op1=ALU.add), [PN, SB_, B("edec", ti)], [SB_])
                        p2, PB2 = ps_next()
                        pv = p2[:].bitcast(BF16)
                        for jj in range(8):
                            tr(pv[:, jj * 128:jj * 128 + tsz], obst[:tsz, jj * 128:(jj + 1) * 128], ident_b[:tsz, :tsz],
                               [B("obst"), BC], [PB2])
                        cp(obT[:, :, c0:c0 + tsz], pv[:, 0:1024].rearrange("p (h t) -> p h t", h=8)[:, :, 0:tsz], [PB2],
                           [B("obT", ti)])

                    for sl in range(2):
                        for tt in range(2):
                            ti = 2 * sl + tt
                            cp(S_bf[:], S_st[sl][:], [B("S_st", sl)], [B("S_bf")])
                            gla_tile(ti, ti * 128, 128, S_st[sl], B("S_st", sl), tt == 0)
                    if rd == 0:
                        cp(S_bf[:], S_s[:], [B("S_s")], [B("S_bf")])
                        gla_tile(4, 512, 4, S_s, B("S_s"), True)
                        store("sp", gs[l].rearrange("h k v -> k h v"), S_s[:], B("S_s"), B("gs"))

                    if stop == "P4b":
                        raise _Stop()
                    nblk = 7 if rd == 0 else 15
                    cands = ([0, 1, 2], list(range(7))) if rd == 0 else (list(range(11)), list(range(15)))
                    e1kT = ekin[(l, rd)].rearrange("(h d t) -> h d t", h=8, d=128)
                    e1V = evin[(l, rd)].rearrange("(t n) -> t n", n=1024)
                    for vb_i in range(2):
                        dve(lambda e, vb_i=vb_i: e.memset(vbuf[vb_i][:, :, 128:129], 1.0), [], [B("vbuf", vb_i)])
                    for h in range(8):
                        kb_ = kbuf[h % 2]
                        vb_ = vbuf[h % 2]
                        KB, VB = B("kbuf", h % 2), B("vbuf", h % 2)
                        for m in range(nblk):
                            jm, slot = _owner(m)
                            rnd, bs = slot // 2, slot % 2
                            srck = eko[rnd][jm, :].rearrange("(h d t) -> h d t", h=8, d=128)
                            srcv = evo[rnd][jm, :].rearrange("(t n) -> t n", n=1024)
                            load("sp", kb_[:, m * 256:(m + 1) * 256], srck[h, :, bs * 256:(bs + 1) * 256], KB, reads=[EKO[rnd]])
                            load("sp", vb_[:, 2 * m:2 * m + 2, 0:128],
                                 srcv[bs * 256:(bs + 1) * 256, h * 128:(h + 1) * 128].rearrange("(t p) d -> p t d", p=128),
                                 VB, reads=[EVO[rnd]])
                        load("sp", kb_[:, nblk * 256:nblk * 256 + 512], e1kT[h, :, :], KB, reads=[EKB])
                        load("sp", vb_[:, 2 * nblk:2 * nblk + 4, 0:128],
                             e1V[:, h * 128:(h + 1) * 128].rearrange("(t p) d -> p t d", p=128), VB, reads=[EVB])
                        for sl in range(2):
                            s4 = rd * 2 + sl
                            for qt in range(2):
                                ti = 2 * sl + qt
                                c0 = ti * 128
                                QB = B("qT", ti, h)
                                qap = qT[:, h, c0:c0 + 128]
                                smv, top8, selb, selbb, selT = smv4[ti], top84[ti], selb4[ti], selbb4[ti], selT4[ti]
                                p1, P1B = ps_next()
                                mm(p1[:, 0:16], qap, meansb[:, :, h], True, True, [QB, B("meansb")], [P1B])
                                dve(lambda e, p1=p1, s4=s4, smv=smv: e.tensor_tensor(
                                    out=smv[:], in0=p1[:, 0:16], in1=tabs[:, s4 * 16:(s4 + 1) * 16], op=ALU.add),
                                    [P1B, BC], [B("smv", ti)])
                                dve(lambda e, smv=smv, top8=top8: e.max(out=top8[:], in_=smv[:]), [B("smv", ti)], [B("top8", ti)])
                                dve(lambda e, smv=smv, top8=top8, selb=selb: e.tensor_scalar(
                                    out=selb[:], in0=smv[:], scalar1=top8[:, 2:3], scalar2=None, op0=ALU.is_ge),
                                    [B("smv", ti), B("top8", ti)], [B("selb", ti)])
                                dve(lambda e, s4=s4, selb=selb: e.tensor_tensor(
                                    out=selb[:], in0=selb[:], in1=tabs[:, 64 + s4 * 16:64 + (s4 + 1) * 16], op=ALU.mult),
                                    [B("selb", ti), BC], [B("selb", ti)])
                                dve(lambda e, selb=selb, selbb=selbb: e.tensor_scalar(
                                    out=selbb[:], in0=selb[:], scalar1=1.0, scalar2=1.0e30, op0=ALU.subtract, op1=ALU.mult),
                                    [B("selb", ti)], [B("selbb", ti)])
                                p2, PB2 = ps_next()
                                pv = p2[:].bitcast(BF16)
                                tr(pv[0:16, 0:128], selbb[:, :], ident_b[:, :], [B("selbb", ti), BC], [PB2])
                                cp(selT[:], pv[0:16, 0:128], [PB2], [B("selT", ti)])
                        for sl in range(2):
                            for qt in range(2):
                                ti = 2 * sl + qt
                                c0 = ti * 128
                                QB = B("qT", ti, h)
                                qap = qT[:, h, c0:c0 + 128]
                                selT = selT4[ti]
                                ob = 5 + (ocnt[0] % 3)
                                oi = ocnt[0] % 2
                                ocnt[0] += 1
                                OP, OPB = psum[ob], psb[ob]
                                rec_, oat_ = rec2[oi], oat2[oi]
                                RB, OTB = B("rec2", oi), B("oat2", oi)
                                kts_ = []
                                for m in cands[sl]:
                                    for kt in range(2):
                                        kts_.append((kb_[:, m * 256 + kt * 128:m * 256 + (kt + 1) * 128], m, 2 * m + kt, False))
                                for kt in range(qt + 1):
                                    o_ = (nblk + sl) * 256 + kt * 128
                                    kts_.append((kb_[:, o_:o_ + 128], None, 2 * (nblk + sl) + kt, kt == qt))
                                nk = len(kts_)
                                def emit_pv(g0_, grp_, pt2_, PTB_):
                                    for jx, (kap, em_, vi, tri_) in enumerate(grp_):
                                        first = (g0_ + jx == 0)
                                        last = (g0_ + jx == nk - 1)
                                        mm(OP[:, 0:129], pt2_[:, jx * 128:(jx + 1) * 128], vb_[:, vi, :], first, last,
                                           [PTB_, VB], [OPB])

                                pendq = []
                                for g0 in range(0, nk, 4):
                                    grp = kts_[g0:g0 + 4]
                                    p3, PB3 = ps_next()
                                    for jx, (kap, em_, vi, tri_) in enumerate(grp):
                                        mm(p3[:, jx * 128:(jx + 1) * 128], kap, qap, True, em_ is None, [KB, QB], [PB3])
                                        if em_ is not None:
                                            mm(p3[:, jx * 128:(jx + 1) * 128], emb[0:16, em_ * 128:(em_ + 1) * 128], selT[:, :],
                                               False, True, [B("selT", ti), BC], [PB3])
                                    pt2 = ptt[cnt2[0] % 4]
                                    PTB = B("ptt", cnt2[0] % 4)
                                    cnt2[0] += 1
                                    act(pt2[:, 0:len(grp) * 128], p3[:, 0:len(grp) * 128], AF.Exp, [PB3], [PTB], scale=SCALE)
                                    for jx, (kap, em_, vi, tri_) in enumerate(grp):
                                        if tri_:
                                            dve(lambda e, pt2=pt2, jx=jx: e.tensor_tensor(
                                                out=pt2[:, jx * 128:(jx + 1) * 128], in0=pt2[:, jx * 128:(jx + 1) * 128],
                                                in1=tri_b[:, :], op=ALU.mult), [PTB, BC], [PTB])
                                    pendq.append((g0, grp, pt2, PTB))
                                    if len(pendq) > 2:
                                        emit_pv(*pendq.pop(0))
                                while pendq:
                                    emit_pv(*pendq.pop(0))
                                dve(lambda e, OP=OP, rec_=rec_: e.reciprocal(out=rec_[:], in_=OP[:, 128:129]), [OPB], [RB])
                                dve(lambda e, ti=ti, h=h, OP=OP, rec_=rec_, oat_=oat_: e.scalar_tensor_tensor(
                                    out=oat_[:], in0=OP[:, 0:128], scalar=rec_[:, 0:1], in1=sza[:, ti, h * 128:(h + 1) * 128],
                                    op0=ALU.mult, op1=ALU.mult), [OPB, RB, B("sza", ti)], [OTB])
                                p4, PB4 = ps_next()
                                pv4 = p4[:].bitcast(BF16)
                                tr(pv4[:, 0:128], oat_[:, :], ident_b[:, :], [OTB, BC], [PB4])
                                cp(qT[:, h, c0:c0 + 128], pv4[:, 0:128], [PB4], [QB])

                    if stop == "P4c":
                        raise _Stop()
                    barrier()
                    wav = w_a[l].rearrange("(k p) n -> p k n", p=128)
                    wbv = w_b[l].rearrange("(k p) n -> p k n", p=128)
                    wov = w_out[l].rearrange("(k p) n -> p k n", p=128)
                    for cc in range(4):
                        wsa = wrot[0] % 2
                        wrot[0] += 1
                        wta = wbufD[wsa][:, 0:8 * 512].rearrange("p (k n) -> p k n", k=8)
                        load("pool", wta, wav[:, :, cc * 512:(cc + 1) * 512], wB[wsa])
                        wsb = wrot[0] % 2
                        wrot[0] += 1
                        wtb = wbufD[wsb][:, 0:8 * 512].rearrange("p (k n) -> p k n", k=8)
                        load("pool", wtb, wbv[:, :, cc * 512:(cc + 1) * 512], wB[wsb])
                        for ti, (r0, c0, tsz, smp) in enumerate(tiles):
                            pa, PA = ps_next()
                            for h in range(8):
                                mm(pa[:tsz, :], qT[:, h, c0:c0 + tsz], wta[:, h, :], h == 0, h == 7, [B("qT", ti, h), wB[wsa]], [PA])
                            pb2, PBB = ps_next()
                            for h in range(8):
                                mm(pb2[:tsz, :], obT[:, h, c0:c0 + tsz], wtb[:, h, :], h == 0, h == 7, [B("obT", ti), wB[wsb]], [PBB])
                            while pend2:
                                pend2.pop(0)()
                            gs_ = cnt2[0] % 2
                            cnt2[0] += 1
                            gb_ = gab[gs_]
                            GB = B("gab", gs_)
                            load("sp", gb_[:tsz, :, :],
                                 gscr[c0:c0 + tsz, :].rearrange("t (a c n) -> t a c n", a=2, c=4)[:, :, cc, :], GB,
                                 reads=[B("gscr")])
                            dve(lambda e, pa=pa, gb_=gb_, tsz=tsz: e.tensor_tensor(out=t1[:tsz, :], in0=pa[:tsz, :],
                                                                                  in1=gb_[:tsz, 0, :], op=ALU.mult),
                                [PA, GB], [B("t1")])
                            dve(lambda e, pb2=pb2, gb_=gb_, tsz=tsz: e.tensor_tensor(out=t2[:tsz, :], in0=pb2[:tsz, :],
                                                                                   in1=gb_[:tsz, 1, :], op=ALU.mult),
                                [PBB, GB], [B("t2")])
                            ms_ = mst2[cnt2[0] % 2]
                            MSB = B("mst2", cnt2[0] % 2)
                            dve(lambda e, tsz=tsz, ms_=ms_: e.tensor_tensor(out=ms_[:tsz, :], in0=t1[:tsz, :], in1=t2[:tsz, :],
                                                                           op=ALU.add), [B("t1"), B("t2")], [MSB])
                            def fin_m(ms_=ms_, MSB=MSB, tsz=tsz, cc=cc, c0=c0, ti=ti):
                                p2, PB2 = ps_next()
                                pv = p2[:].bitcast(BF16)
                                for i in range(4):
                                    tr(pv[:, i * 128:i * 128 + tsz], ms_[:tsz, i * 128:(i + 1) * 128], ident_b[:tsz, :tsz],
                                       [MSB, BC], [PB2])
                                cp(xnT[:, cc * 4:(cc + 1) * 4, c0:c0 + tsz],
                                   pv[:, 0:512].rearrange("p (h t) -> p h t", h=4)[:, :, 0:tsz], [PB2], [B("xnT", ti)])
                            pend2.append(fin_m)
                    while pend2:
                        pend2.pop(0)()
                    for cc in range(4):
                        ws = wrot[0] % 2
                        wrot[0] += 1
                        wto = wbufD[ws][:, 0:KC * 512].rearrange("p (k n) -> p k n", k=KC)
                        load("pool", wto, wov[:, :, cc * 512:(cc + 1) * 512], wB[ws])
                        for ti, (r0, c0, tsz, smp) in enumerate(tiles):
                            po_, PO = ps_next()
                            for k in range(KC):
                                mm(po_[:tsz, :], xnT[:, k, c0:c0 + tsz], wto[:, k, :], k == 0, k == KC - 1,
                                   [B("xnT", ti), wB[ws]], [PO])
                            xs_ = cnt2[0] % 2
                            cnt2[0] += 1
                            xr = xres[xs_]
                            XR = B("xres", xs_)
                            XS = B("xscr", "s") if smp else B("xscr", r0)
                            if smp:
                                src = (xs if l == 0 else xscr[1024:1028, :])[:, cc * 512:(cc + 1) * 512]
                                dst = xscr[1024:1028, cc * 512:(cc + 1) * 512]
                            else:
                                src = (xp if l == 0 else xscr)[r0:r0 + tsz, cc * 512:(cc + 1) * 512]
                                dst = xscr[r0:r0 + tsz, cc * 512:(cc + 1) * 512]
                            load("sp", xr[:tsz, :], src, XR, reads=[XS])
                            dve(lambda e, po_=po_, xr=xr, tsz=tsz: e.tensor_tensor(out=xr[:tsz, :], in0=po_[:tsz, :],
                                                                                 in1=xr[:tsz, :], op=ALU.add), [PO, XR], [XR])
                            store("sp", dst, xr[:tsz, :], XR, XS)

            barrier()
            load("sp", fngrep, fng.broadcast_to([128, D]), B("fngrep"))
            fin = [(r0, 128, False) for r0 in range(0, 1024, 128)] + [(1024, 4, True)]
            for fi, (r0, tsz, smp) in enumerate(fin):
                xb_ = xtF
                XB = B("xtF")
                XS = B("xscr", "s") if smp else B("xscr", r0)
                load("sp", xb_[:tsz, :], xscr[r0:r0 + tsz, :], XB, reads=[XS])
                rms_rstd(rstd[:tsz, :], xb_[:tsz, :], D, tsz, [XB], junkF[:tsz, :], B("junkF"))
                dve(lambda e, xb_=xb_, tsz=tsz: e.scalar_tensor_tensor(out=xb_[:tsz, :], in0=xb_[:tsz, :], scalar=rstd[:tsz, :],
                                                                      in1=fngrep[:tsz, :], op0=ALU.mult, op1=ALU.mult),
                    [XB, B("rstd"), B("fngrep")], [XB])
                if smp:
                    store("sp", ys[:, :], xb_[:tsz, :], XB, B("ys"))
                else:
                    store("sp", yp[r0:r0 + tsz, :], xb_[:tsz, :], XB, B("yp"))
        except _Stop:
            pass
        fextra = set(g.last[e_] for e_ in ("pe", "act", "dve") if g.last[e_] is not None)
        for t_ in g.tracks:
            if t_.last_op is not None:
                fextra.add(t_.last_op)
        for e_ in Graph.ENGS:
            g.add(e_, None, extra=fextra)
        g.add("sp", None, reads=[B("yp"), B("ys"), B("kp", 0), B("kp", 1), B("vp", 0), B("vp", 1), B("gp"),
                                 B("ks", 0), B("ks", 1), B("vs", 0), B("vs", 1), B("gs")])
        g.emit()
    return nc


_NC_CACHE = {}
_DEV = {"npool": NPOOL, "stop": None}


def _blocks(j):
    return [j, 7 - j, 8 + j, 15 - j]


def _tab(j):
    t = np.zeros((128, 160), np.float32)
    blks = _blocks(j)
    for s_ in range(4):
        g_ = blks[s_]
        t[:, s_ * 16:(s_ + 1) * 16] = np.where(np.arange(16) < g_, 0.0, NEG)
        t[:, 64 + s_ * 16:64 + (s_ + 1) * 16] = (np.arange(16) < g_).astype(np.float32)
    for rd in range(2):
        for sl in range(2):
            g_ = blks[rd * 2 + sl]
            for gq in range(8):
                t[:, 128 + (rd * 2 + sl) * 8 + gq] = 1.0 if (rd * 8 + gq) == g_ else 0.0
    return t


def kernel(x_prompt, x_sample, cache_k, cache_v, state_gla, page_table, norm_g, w_in, w_gate2, b_gate,
           gla_norm_g, w_branch_a, w_branch_b, b_merge, w_out, final_norm_g):
    f = lambda a: np.ascontiguousarray(np.asarray(a, dtype=np.float32))
    x_prompt, x_sample = f(x_prompt), f(x_sample)
    ckf = f(cache_k).reshape(2 * _DEV["npool"] * 128, 1024)
    cvf = f(cache_v).reshape(2 * _DEV["npool"] * 128, 1024)
    state_gla = f(state_gla)
    page_table = np.ascontiguousarray(np.asarray(page_table, dtype=np.int32))
    cst, emc, bdc = _consts()
    shared = {
        "ck": ckf, "cv": cvf, "norm_g": f(norm_g), "w_in": f(w_in), "w_gate2": f(w_gate2), "b_gate": f(b_gate),
        "gla_g": f(gla_norm_g).reshape(2, 1024), "w_a": f(w_branch_a), "w_b": f(w_branch_b), "b_merge": f(b_merge),
        "w_out": f(w_out), "fng": f(final_norm_g).reshape(1, D), "cst": cst, "emc": emc, "bdc": bdc,
    }
    in_maps = []
    for c in range(8):
        b, j = c // 4, c % 4
        rows = np.concatenate([np.arange(m * 256, (m + 1) * 256) for m in _blocks(j)])
        d = dict(shared)
        d["xp"] = np.ascontiguousarray(x_prompt[b][rows])
        d["xs"] = np.ascontiguousarray(x_sample[c])
        d["sg"] = np.ascontiguousarray(state_gla[:, c])
        d["pt"] = np.ascontiguousarray(page_table[c].reshape(1, 128))
        d["meta"] = np.full((1, 64), c, np.int32)
        d["tab"] = _tab(j)
        in_maps.append(d)
    if "nc" not in _NC_CACHE:
        _NC_CACHE["nc"] = build(npool=_DEV["npool"], stop=_DEV["stop"])
    if _DEV.get("trace"):
        res = run_bass_kernel_spmd(_NC_CACHE["nc"], in_maps, core_ids=list(range(8)), trace=True)
        print("EXEC_TIME_NS", res.exec_time_ns)
    else:
        res = run_bass_kernel_spmd(_NC_CACHE["nc"], in_maps, core_ids=list(range(8)))
    R = res.results
    y_prompt = np.zeros((2, 4096, D), np.float32)
    k_prompt = np.zeros((2, 2, 4096, 8, 128), np.float32)
    v_prompt = np.zeros((2, 2, 4096, 8, 128), np.float32)
    gla_prompt = np.zeros((2, 2, 4, 128, 256), np.float32)
    y_sample = np.zeros((8, 4, D), np.float32)
    k_sample = np.zeros((2, 8, 4, 8, 128), np.float32)
    v_sample = np.zeros((2, 8, 4, 8, 128), np.float32)
    gla_sample = np.zeros((2, 8, 4, 128, 256), np.float32)
    for c in range(8):
        b, j = c // 4, c % 4
        rows = np.concatenate([np.arange(m * 256, (m + 1) * 256) for m in _blocks(j)])
        r = R[c]
        y_prompt[b, rows] = r["yp"]
        k_prompt[:, b, rows] = np.asarray(r["kp"]).reshape(2, 1024, 8, 128)
        v_prompt[:, b, rows] = np.asarray(r["vp"]).reshape(2, 1024, 8, 128)
        if j == 0:
            gla_prompt[:, b] = r["gp"]
        y_sample[c] = r["ys"]
        k_sample[:, c] = np.asarray(r["ks"]).reshape(2, 4, 8, 128)
        v_sample[:, c] = np.asarray(r["vs"]).reshape(2, 4, 8, 128)
        gla_sample[:, c] = r["gs"]
    return (y_prompt, y_sample, k_prompt, v_prompt, gla_prompt, k_sample, v_sample, gla_sample)
```
